# Optimizing a Trainium2 kernel written in Bass

```python
import math
import jax, jax.numpy as jnp
from jax import lax
import numpy as np

D_MODEL = 2048
BATCH = 8
SEQ = 2048
DEPTH = 4

HEAD_DIM = 128
MIXER_HEADS = D_MODEL // 256
BRANCH_WIDTH = MIXER_HEADS * HEAD_DIM
N_BRANCHES = 3
FOX_HEADS = MIXER_HEADS
FOX_HEAD_DIM = HEAD_DIM
FOX_W = FOX_HEADS * FOX_HEAD_DIM
Q_BLOCK = 128
MLA_HEADS = MIXER_HEADS
MLA_Q_RANK = D_MODEL // 4
MLA_KV_RANK = D_MODEL // 4
MLA_NOPE_DIM = 128
MLA_ROPE_DIM = 64
MLA_V_DIM = 128
ROPE_THETA = 10000.0
GDN_HEADS = MIXER_HEADS
GDN_HEAD_DIM = HEAD_DIM
GDN_W = GDN_HEADS * GDN_HEAD_DIM
GDN_CONV = 4
GDN_CHUNK = 64
D_FF = 4 * D_MODEL
EPS = 1e-6

IN_SPLIT_SIZES = (FOX_W, FOX_W, FOX_W, FOX_HEADS,
                  MLA_Q_RANK, MLA_KV_RANK, MLA_ROPE_DIM,
                  3 * GDN_W, GDN_W, GDN_HEADS, GDN_HEADS,
                  N_BRANCHES * D_MODEL)
D_IN = sum(IN_SPLIT_SIZES)

kernel_name = 'hybrid_fox_mla_gdn_gated_parallel_block'


def rms_norm(x, gain):
    xf = x.astype(jnp.float32)
    y = xf * lax.rsqrt(jnp.mean(xf * xf, axis=-1, keepdims=True) + EPS)
    return (y * gain.astype(jnp.float32)).astype(x.dtype)


def l2_norm(x):
    return x * lax.rsqrt(jnp.sum(x * x, axis=-1, keepdims=True) + EPS)


def rope_tables(seq):
    inv_freq = ROPE_THETA ** (-jnp.arange(0, MLA_ROPE_DIM, 2, dtype=jnp.float32) / MLA_ROPE_DIM)
    ang = jnp.arange(seq, dtype=jnp.float32)[:, None] * inv_freq[None, :]
    return jnp.cos(ang), jnp.sin(ang)


def apply_rope(x, cos, sin):
    xf = x.astype(jnp.float32)
    x1, x2 = jnp.split(xf, 2, axis=-1)
    return jnp.concatenate([x1 * cos - x2 * sin, x2 * cos + x1 * sin], axis=-1).astype(x.dtype)


def causal_block_attention(q, k, v, cum_log_f=None):
    B, H, S, Dk = q.shape
    Dv = v.shape[-1]
    nb = S // Q_BLOCK
    scale = Dk ** -0.5
    xs = [jnp.arange(nb), q.reshape(B, H, nb, Q_BLOCK, Dk).transpose(2, 0, 1, 3, 4)]
    if cum_log_f is not None:
        xs.append(cum_log_f.reshape(B, H, nb, Q_BLOCK).transpose(2, 0, 1, 3))
    kpos = jnp.arange(S)

    def one_block(args):
        i, q_blk = args[0], args[1]
        s = jnp.einsum('bhqd,bhkd->bhqk', q_blk, k).astype(jnp.float32) * scale
        if cum_log_f is not None:
            s = s + args[2][..., :, None] - cum_log_f[:, :, None, :]
        qpos = i * Q_BLOCK + jnp.arange(Q_BLOCK)
        s = jnp.where(kpos[None, :] <= qpos[:, None], s, -jnp.inf)
        p = jax.nn.softmax(s, axis=-1)
        return jnp.einsum('bhqk,bhkd->bhqd', p.astype(v.dtype), v)

    out = lax.map(one_block, tuple(xs))
    return out.transpose(1, 2, 0, 3, 4).reshape(B, H, S, Dv)


def fox_mixer(q, k, v, f_logit, f_bias):
    B, S, _ = q.shape
    heads = lambda t: t.reshape(B, S, FOX_HEADS, FOX_HEAD_DIM).transpose(0, 2, 1, 3)
    log_f = jax.nn.log_sigmoid(f_logit.astype(jnp.float32) + f_bias.astype(jnp.float32))
    cum = jnp.cumsum(log_f, axis=1).transpose(0, 2, 1)
    o = causal_block_attention(heads(q), heads(k), heads(v), cum)
    return o.transpose(0, 2, 1, 3).reshape(B, S, FOX_W)


def mla_mixer(c_q, c_kv, k_pe, q_norm, kv_norm, w_uq, w_ukv, cos, sin):
    B, S, _ = c_q.shape
    q = jnp.einsum('bsr,rhd->bshd', rms_norm(c_q, q_norm), w_uq)
    q_nope, q_pe = q[..., :MLA_NOPE_DIM], q[..., MLA_NOPE_DIM:]
    q_pe = apply_rope(q_pe, cos[:, None, :], sin[:, None, :])
    kv = jnp.einsum('bsr,rhd->bshd', rms_norm(c_kv, kv_norm), w_ukv)
    k_nope, v = kv[..., :MLA_NOPE_DIM], kv[..., MLA_NOPE_DIM:]
    k_pe = apply_rope(k_pe, cos, sin)
    q = jnp.concatenate([q_nope, q_pe], axis=-1)
    k = jnp.concatenate([k_nope, jnp.broadcast_to(k_pe[:, :, None, :], (B, S, MLA_HEADS, MLA_ROPE_DIM))], axis=-1)
    t = lambda a: a.transpose(0, 2, 1, 3)
    o = causal_block_attention(t(q), t(k), t(v))
    return o.transpose(0, 2, 1, 3).reshape(B, S, MLA_HEADS * MLA_V_DIM)


def causal_depthwise_conv(x, w):
    K, C = w.shape
    return lax.conv_general_dilated(x, w[:, None, :].astype(x.dtype), window_strides=(1,),
                                    padding=[(K - 1, 0)], dimension_numbers=('NWC', 'WIO', 'NWC'),
                                    feature_group_count=C)


def gated_delta_rule_chunked(q, k, v, g, beta):
    B, H, S, Dk = q.shape
    Dv = v.shape[-1]
    C = GDN_CHUNK
    N = S // C
    chunk = lambda t: t.reshape(B, H, N, C, *t.shape[3:])
    q = chunk(q * Dk ** -0.5)
    k = chunk(k)
    v = chunk(v)
    beta = chunk(beta)
    g = jnp.cumsum(chunk(g), axis=-1)
    incl = jnp.tril(jnp.ones((C, C), dtype=bool))
    strict = jnp.tril(jnp.ones((C, C), dtype=bool), -1)
    decay = jnp.exp(jnp.where(incl, g[..., :, None] - g[..., None, :], -jnp.inf))
    k_beta = k * beta[..., None]
    lower = jnp.where(strict, jnp.einsum('bhnid,bhnjd->bhnij', k_beta, k) * decay, 0.0)
    eye = jnp.eye(C, dtype=q.dtype)
    t_inv = lax.linalg.triangular_solve(lower + eye, jnp.broadcast_to(eye, lower.shape),
                                        left_side=True, lower=True)
    u = jnp.einsum('bhnij,bhnjd->bhnid', t_inv, v * beta[..., None])
    w = jnp.einsum('bhnij,bhnjd->bhnid', t_inv, k_beta * jnp.exp(g)[..., None])
    intra = jnp.where(incl, jnp.einsum('bhnid,bhnjd->bhnij', q, k) * decay, 0.0)
    q_dec = q * jnp.exp(g)[..., None]
    k_dec = k * jnp.exp(g[..., -1:] - g)[..., None]
    chunk_decay = jnp.exp(g[..., -1])

    def step(state, xs):
        u_c, w_c, a_c, qd_c, kd_c, cd_c = xs
        v_new = u_c - jnp.einsum('bhik,bhkv->bhiv', w_c, state)
        out = jnp.einsum('bhik,bhkv->bhiv', qd_c, state) + jnp.einsum('bhij,bhjv->bhiv', a_c, v_new)
        state = state * cd_c[..., None, None] + jnp.einsum('bhik,bhiv->bhkv', kd_c, v_new)
        return state, out

    to_front = lambda t: jnp.moveaxis(t, 2, 0)
    state0 = jnp.zeros((B, H, Dk, Dv), q.dtype)
    _, out = lax.scan(step, state0, (to_front(u), to_front(w), to_front(intra),
                                     to_front(q_dec), to_front(k_dec), to_front(chunk_decay)))
    return jnp.moveaxis(out, 0, 2).reshape(B, H, S, Dv)


def gdn_mixer(qkv, z, b_logit, a_logit, conv_w, a_log, dt_bias, out_norm):
    B, S, _ = qkv.shape
    qkv_c = jax.nn.silu(causal_depthwise_conv(qkv, conv_w))
    q, k, v = jnp.split(qkv_c, 3, axis=-1)
    heads = lambda t: t.reshape(B, S, GDN_HEADS, GDN_HEAD_DIM).transpose(0, 2, 1, 3).astype(jnp.float32)
    q = l2_norm(heads(q))
    k = l2_norm(heads(k))
    v = heads(v)
    beta = jax.nn.sigmoid(b_logit.astype(jnp.float32)).transpose(0, 2, 1)
    g = (-jnp.exp(a_log.astype(jnp.float32))
         * jax.nn.softplus(a_logit.astype(jnp.float32) + dt_bias.astype(jnp.float32))).transpose(0, 2, 1)
    o = gated_delta_rule_chunked(q, k, v, g, beta).transpose(0, 2, 1, 3)
    zh = z.reshape(B, S, GDN_HEADS, GDN_HEAD_DIM).astype(jnp.float32)
    o = rms_norm(o, out_norm) * jax.nn.silu(zh)
    return o.reshape(B, S, GDN_W).astype(qkv.dtype)


def setup_inputs(seed: int = 0) -> dict:
    key = jax.random.key(seed)
    ks = jax.random.split(key, 20)
    nrm = lambda k, shape, fan_in: jax.random.normal(k, shape, jnp.float32) * (fan_in ** -0.5)
    gain = lambda k, shape: 1.0 + 0.02 * jax.random.normal(k, shape, jnp.float32)
    dt = jnp.exp(jax.random.uniform(ks[10], (DEPTH, GDN_HEADS), jnp.float32,
                                    minval=math.log(1e-3), maxval=math.log(1e-1)))
    return {
        'x': jax.random.normal(ks[0], (BATCH, SEQ, D_MODEL), jnp.float32),
        'attn_norm': gain(ks[1], (DEPTH, D_MODEL)),
        'w_in': nrm(ks[2], (DEPTH, D_MODEL, D_IN), D_MODEL),
        'fox_fgate_bias': 2.0 + 0.5 * jax.random.normal(ks[3], (DEPTH, FOX_HEADS), jnp.float32),
        'mla_q_norm': gain(ks[4], (DEPTH, MLA_Q_RANK)),
        'mla_kv_norm': gain(ks[5], (DEPTH, MLA_KV_RANK)),
        'w_mla_uq': nrm(ks[6], (DEPTH, MLA_Q_RANK, MLA_HEADS, MLA_NOPE_DIM + MLA_ROPE_DIM), MLA_Q_RANK),
        'w_mla_ukv': nrm(ks[7], (DEPTH, MLA_KV_RANK, MLA_HEADS, MLA_NOPE_DIM + MLA_V_DIM), MLA_KV_RANK),
        'gdn_conv': nrm(ks[8], (DEPTH, GDN_CONV, 3 * GDN_W), GDN_CONV),
        'gdn_a_log': jnp.log(jax.random.uniform(ks[9], (DEPTH, GDN_HEADS), jnp.float32, minval=1.0, maxval=16.0)),
        'gdn_dt_bias': dt + jnp.log(-jnp.expm1(-dt)),
        'gdn_out_norm': gain(ks[11], (DEPTH, GDN_HEAD_DIM)),
        'w_branch': nrm(ks[12], (DEPTH, N_BRANCHES, BRANCH_WIDTH, D_MODEL), BRANCH_WIDTH),
        'w_out': nrm(ks[13], (DEPTH, D_MODEL, D_MODEL), D_MODEL),
        'mlp_norm': gain(ks[14], (DEPTH, D_MODEL)),
        'w_up': nrm(ks[15], (DEPTH, D_MODEL, D_FF), D_MODEL),
        'w_down': nrm(ks[16], (DEPTH, D_FF, D_MODEL), D_FF),
        'final_norm': gain(ks[17], (D_MODEL,)),
    }


def reference(x, attn_norm, w_in, fox_fgate_bias, mla_q_norm, mla_kv_norm, w_mla_uq, w_mla_ukv,
              gdn_conv, gdn_a_log, gdn_dt_bias, gdn_out_norm, w_branch, w_out, mlp_norm,
              w_up, w_down, final_norm):
    B, S, _ = x.shape
    cos, sin = rope_tables(S)
    split_idx = np.cumsum(IN_SPLIT_SIZES)[:-1].tolist()
    for l in range(DEPTH):
        u = rms_norm(x, attn_norm[l])
        proj = jnp.einsum('bsd,de->bse', u, w_in[l])
        (fq, fk, fv, ff, cq, ckv, kpe, gqkv, gz, gb, ga, gate) = jnp.split(proj, split_idx, axis=-1)
        o_fox = fox_mixer(fq, fk, fv, ff, fox_fgate_bias[l])
        o_mla = mla_mixer(cq, ckv, kpe, mla_q_norm[l], mla_kv_norm[l], w_mla_uq[l], w_mla_ukv[l], cos, sin)
        o_gdn = gdn_mixer(gqkv, gz, gb, ga, gdn_conv[l], gdn_a_log[l], gdn_dt_bias[l], gdn_out_norm[l])
        gates = jax.nn.sigmoid(gate.astype(jnp.float32)).astype(x.dtype).reshape(B, S, N_BRANCHES, D_MODEL)
        merged = (gates[:, :, 0] * jnp.einsum('bsc,cd->bsd', o_fox, w_branch[l, 0])
                  + gates[:, :, 1] * jnp.einsum('bsc,cd->bsd', o_mla, w_branch[l, 1])
                  + gates[:, :, 2] * jnp.einsum('bsc,cd->bsd', o_gdn, w_branch[l, 2]))
        x = x + jnp.einsum('bsd,de->bse', merged, w_out[l])
        h = rms_norm(x, mlp_norm[l])
        hidden = jnp.square(jax.nn.relu(jnp.einsum('bsd,df->bsf', h, w_up[l])))
        x = x + jnp.einsum('bsf,fd->bsd', hidden, w_down[l])
    return rms_norm(x, final_norm)
```

```python
import math
import os
from contextlib import ExitStack

import numpy as np
import ml_dtypes
import concourse.bass as bass
import concourse.mybir as mybir
from concourse.bass_utils import run_bass_kernel_spmd

F32 = mybir.dt.float32
BF16 = mybir.dt.bfloat16
AF = mybir.ActivationFunctionType
ALU = mybir.AluOpType

ENGS = ['pe', 'act', 'dve', 'pool', 'sp']
GSTOP = int(os.environ.get('GSTOP', '0'))
GSUB = int(os.environ.get('GSUB', '9'))
GHEADS = int(os.environ.get('GHEADS', '8'))
EPOCH = 10 ** 9
SAME_ENGINE_SYNC = True

S = 2048
D = 2048
NL = 4
DIN = 14424
DFF = 8192
EPS = 1e-6
C_FQ, C_FK, C_FV, C_FF = 0, 1024, 2048, 3072
C_CQ, C_CKV, C_KPE = 3080, 3592, 4104
C_GQKV, C_GZ, C_GB, C_GA, C_GATE = 4168, 7240, 8264, 8272, 8280


class Prog:
    def __init__(self, nc, es):
        self.nc = nc
        self.es = es
        self.ops = {e: [] for e in ENGS}
        self.cnt = {e: 0 for e in ENGS}
        self.esems = {e: [] for e in ENGS}
        self.known = {e: {} for e in ENGS}
        self.last_w = {}
        self.readers = {}
        self.dsem = {}
        self.sem_owner = {}
        self.nsem = 0
        self.nops = 0
        self.retired = []
        self.pool = []
        self.phase_keys = {}

    def _newsem(self, name, owner=None):
        s = self.es.enter_context(self.nc.semaphore(f"{name}_{self.nsem}"))
        self.nsem += 1
        self.sem_owner[id(s)] = owner
        return s

    def _tick(self, e):
        k = self.cnt[e]
        self.cnt[e] += 1
        ep = k // EPOCH
        while len(self.esems[e]) <= ep:
            self.esems[e].append(self._newsem(f"s_{e}", e))
        return (self.esems[e][ep], k % EPOCH + 1)

    def _filter(self, e, need):
        waits = []
        for sid, (s, v) in need.items():
            if self.sem_owner.get(sid) == e and (e == 'pe' or not SAME_ENGINE_SYNC):
                continue
            if self.known[e].get(sid, 0) >= v:
                continue
            self.known[e][sid] = v
            waits.append((s, v))
        return waits

    def _deps(self, e, reads, writes):
        need = {}

        def add(ev):
            s, v = ev
            if v > need.get(id(s), (None, 0))[1]:
                need[id(s)] = (s, v)
        for r in reads:
            if r in self.last_w:
                add(self.last_w[r])
        for w in writes:
            if w in self.last_w:
                add(self.last_w[w])
            for ev in self.readers.get(w, {}).values():
                add(ev)
        return self._filter(e, need)

    def _commit(self, ev, reads, writes):
        s, v = ev
        for r in reads:
            d = self.readers.setdefault(r, {})
            if v > d.get(id(s), (None, 0))[1]:
                d[id(s)] = ev
        for w in writes:
            self.last_w[w] = ev
            self.readers[w] = {}

    def op(self, e, fn, reads=(), writes=()):
        writes = list(writes) + [r for r in reads if isinstance(r, tuple) and r[0] == 'ps' and r not in writes]
        waits = self._deps(e, reads, writes)
        ev = self._tick(e)
        self.ops[e].append((fn, waits, ev, 1))
        self._commit(ev, reads, writes)
        self.nops += 1

    def dma(self, q, fn, reads=(), writes=(), key=None):
        waits = self._deps(q, reads, writes)
        if key not in self.phase_keys:
            self.phase_keys[key] = len(self.phase_keys)
        idx = self.phase_keys[key]
        while len(self.pool) <= idx:
            self.pool.append([self._newsem("d"), 0])
        ds = self.pool[idx]
        ds[1] += 16
        ev = (ds[0], ds[1])
        self.ops[q].append((fn, waits, ev, 16))
        self._commit(ev, reads, writes)
        self.nops += 1

    def barrier(self):
        evs = {}
        for e in ENGS:
            if self.cnt[e] > 0:
                k = self.cnt[e] - 1
                s = self.esems[e][k // EPOCH]
                evs[id(s)] = (s, k % EPOCH + 1)
        for (s, c) in self.pool:
            if c > 0:
                evs[id(s)] = (s, c)
        self.phase_keys = {}
        for e in ENGS:
            waits = self._filter(e, dict(evs))
            if waits:
                self.ops[e].append((None, waits, None, 0))
        self.last_w = {}
        self.readers = {}

    def simcheck(self):
        if not hasattr(self, 'simsem'):
            self.simsem = {}
        ptr = {e: 0 for e in ENGS}
        prog = True
        while prog:
            prog = False
            for e in ENGS:
                while ptr[e] < len(self.ops[e]):
                    fn, waits, ev, amt = self.ops[e][ptr[e]]
                    if all(self.simsem.get(id(s_), 0) >= v for s_, v in waits):
                        if ev is not None:
                            self.simsem[id(ev[0])] = self.simsem.get(id(ev[0]), 0) + amt
                            assert self.simsem[id(ev[0])] == ev[1], (e, ptr[e], self.simsem[id(ev[0])], ev[1])
                        ptr[e] += 1
                        prog = True
                    else:
                        break
        for e in ENGS:
            assert ptr[e] == len(self.ops[e]), f"DEADLOCK on {e} at {ptr[e]}/{len(self.ops[e])}: {self.ops[e][ptr[e]][1]}"

    def emit(self):
        nc = self.nc
        self.simcheck()
        with nc.Block() as block:
            def mk(e):
                def body(eng):
                    for (fn, waits, ev, amt) in self.ops[e]:
                        for (s, v) in waits:
                            eng.wait_ge(s, v)
                        if fn is None:
                            continue
                        ins = fn(eng)
                        ins.then_inc(ev[0], amt)
                return body
            block.tensor(mk('pe'))
            block.scalar(mk('act'))
            block.vector(mk('dve'))
            block.gpsimd(mk('pool'))
            block.sync(mk('sp'))
        self.ops = {e: [] for e in ENGS}


class Rot:
    def __init__(self, tiles, name):
        self.tiles = tiles
        self.name = name
        self.i = 0

    def next(self):
        j = self.i % len(self.tiles)
        self.i += 1
        return self.tiles[j], (self.name, j)


def build(nl=NL, dbg=(), stop=None, only=None, ext_in=()):
    nc = bass.Bass("TRN2", target_bir_lowering=False)
    din = lambda name, shape, dt=F32: nc.dram_tensor(name, list(shape), dt, kind="ExternalInput").ap()
    x_in = din("x", [S, D])
    attn_norm = din("attn_norm", [NL, 128, 16])
    w_in = din("w_in", [NL, D, DIN])
    fbias = din("fox_fgate_bias", [NL, 8, 1])
    qnorm = din("mla_q_norm", [NL, 128, 4])
    kvnorm = din("mla_kv_norm", [NL, 128, 4])
    w_uq = din("w_mla_uq", [NL, 512, 1536])
    w_ukv = din("w_mla_ukv", [NL, 512, 2048])
    gconv = din("gdn_conv", [NL, 3072, 4])
    galog = din("gdn_a_log", [NL, 8, 1])
    gdtb = din("gdn_dt_bias", [NL, 8, 1])
    gonorm = din("gdn_out_norm", [NL, 128, 1])
    w_branch = din("w_branch", [NL, 3, 1024, D])
    w_out = din("w_out", [NL, D, D])
    mlp_norm = din("mlp_norm", [NL, 128, 16])
    w_up = din("w_up", [NL, D, DFF])
    w_down = din("w_down", [NL, DFF, D])
    final_norm = din("final_norm", [128, 16])
    c_ident = din("c_ident", [128, 128])
    c_mask = din("c_mask", [4, 128, 512])
    c_rope = din("c_rope", [2, 64, S])
    c_rot = din("c_rot", [64, 64])
    c_blk = din("c_blk", [8, S])
    c_lvl = din("c_lvl", [7, 2, 128, 128])
    c_tri = din("c_tri", [2, 128, 128])
    out_d = nc.dram_tensor("out", [S, D], F32, kind="ExternalOutput").ap()

    def scratch(name, shape, dt):
        kind = "ExternalOutput" if name in dbg else ("ExternalInput" if name in ext_in else "Internal")
        return nc.dram_tensor(name, list(shape), dt, kind=kind).ap()
    xT = scratch("xT", [D, S], F32)
    fqT = scratch("fqT", [1024, S], BF16)
    fkT = scratch("fkT", [1024, S], BF16)
    fv = scratch("fv", [S, 1024], BF16)
    smallT = scratch("smallT", [96, S], F32)
    kpeT = scratch("kpeT", [64, S], F32)
    cqT = scratch("cqT", [512, S], F32)
    ckvT = scratch("ckvT", [512, S], F32)
    gqkvT = scratch("gqkvT", [3072, S], F32)
    gzsT = scratch("gzsT", [1024, S], BF16)
    gateT = scratch("gateT", [6144, S], BF16)
    negc = scratch("negc", [8, S], F32)
    crow = scratch("crow", [3, 8, S], BF16)
    oT = scratch("oT", [3, 1024, S], BF16)
    hidT = scratch("hidT", [4, 128, 64, 512], BF16)
    gsc = scratch("gsc", [6, 8, S], F32)
    gcd = scratch("gcd", [8, 16], F32)

    es = ExitStack()
    with es:
        P = Prog(nc, es)
        es.enter_context(nc.allow_non_contiguous_dma(reason="small strided parameter loads"))
        es.enter_context(nc.allow_low_precision(reason="bf16 matmul operands, fp32 accumulation"))
        ps = [es.enter_context(nc.psum_tensor(f"ps{i}", [128, 512], F32)) for i in range(8)]
        gsb = lambda name, shape, dt: es.enter_context(nc.sbuf_tensor(name, list(shape), dt))
        identf = gsb("identf", [128, 128], F32)
        identb = gsb("identb", [128, 128], BF16)
        onesf = gsb("onesf", [128, 128], F32)
        onesb = gsb("onesb", [128, 128], BF16)
        epst = gsb("epst", [128, 1], F32)
        maskb = gsb("maskb", [128, 4, 512], BF16)

        state = {'phase': 0, 'done': False}

        class Phase:
            def __init__(self, name):
                self.name = name

            def __enter__(self):
                self.pes = ExitStack()
                self.pes.__enter__()
                self.cache = {}
                return self

            def sb(self, name, shape, dt):
                if name not in self.cache:
                    self.cache[name] = self.pes.enter_context(
                        nc.sbuf_tensor(f"{self.name}_{name}_{state['phase']}", list(shape), dt))
                return self.cache[name]

            def rot(self, name, shape, dt, n):
                if ('rot', name) not in self.cache:
                    self.cache[('rot', name)] = Rot([self.sb(f"{name}{i}", shape, dt) for i in range(n)], name)
                return self.cache[('rot', name)]

            def __exit__(self, *a):
                P.barrier()
                P.emit()
                self.pes.__exit__(None, None, None)
                state['phase'] += 1
                return False

        with Phase("init") as ph:
            mstage = ph.sb("mstage", [128, 4, 512], F32)
            P.dma('sp', lambda e: e.dma_start(out=identf[:], in_=c_ident), writes=['identf'], key=('ld', 0))
            P.dma('sp', lambda e: e.dma_start(out=mstage[:], in_=c_mask.rearrange("j p t -> p j t")),
                  writes=['mstage'], key=('ld', 1))
            P.op('dve', lambda e: e.tensor_copy(out=identb[:], in_=identf[:]), reads=['identf'], writes=['identb'])
            P.op('dve', lambda e: e.memset(onesf[:], 1.0), writes=['onesf'])
            P.op('dve', lambda e: e.memset(onesb[:], 1.0), writes=['onesb'])
            P.op('dve', lambda e: e.memset(epst[:], EPS), writes=['epst'])
            P.op('dve', lambda e: e.tensor_copy(out=maskb[:], in_=mstage[:]), reads=['mstage'], writes=['maskb'])

        psrot = {'i': 0}

        def next_ps(lo=0, n=4):
            j = lo + psrot['i'] % n
            psrot['i'] += 1
            return j

        def phase_transpose_in():
            with Phase("tin") as ph:
                xt = ph.rot("xt", [128, D], F32, 2)
                st = ph.rot("st", [128, 4, 128], F32, 4)
                for tb in range(16):
                    xtile, xk = xt.next()
                    P.dma('sp', lambda e, tb=tb, xtile=xtile: e.dma_start(out=xtile[:], in_=x_in[tb * 128:(tb + 1) * 128, :]),
                          writes=[xk], key=xk)
                    for kg in range(4):
                        pb = next_ps()
                        for j in range(4):
                            kc = kg * 4 + j
                            P.op('pe', lambda e, kc=kc, j=j, pb=pb, xtile=xtile: e.transpose(
                                out=ps[pb][:, j * 128:(j + 1) * 128], in_=xtile[:, kc * 128:(kc + 1) * 128],
                                identity=identf[:]), reads=[xk], writes=[('ps', pb)])
                        stile, sk = st.next()
                        eng = 'act' if kg % 2 == 0 else 'dve'
                        if eng == 'act':
                            P.op('act', lambda e, pb=pb, stile=stile: e.copy(
                                out=stile[:], in_=ps[pb][:].rearrange("p (a b) -> p a b", a=4)),
                                reads=[('ps', pb)], writes=[sk])
                        else:
                            P.op('dve', lambda e, pb=pb, stile=stile: e.tensor_copy(
                                out=stile[:], in_=ps[pb][:].rearrange("p (a b) -> p a b", a=4)),
                                reads=[('ps', pb)], writes=[sk])
                        P.dma('sp', lambda e, kg=kg, tb=tb, stile=stile: e.dma_start(
                            out=xT[kg * 512:(kg + 1) * 512, tb * 128:(tb + 1) * 128].rearrange("(a p) t -> p a t", p=128),
                            in_=stile[:]), reads=[sk], key=sk)

        def norm_fm(ph, srcT, nk, gain_ap, dst, dkey, dn, tag="n", final_out=None):
            ld = ph.rot(tag + "ld", [128, 512], F32, 4)
            sq = ph.rot(tag + "sq", [128, 512], F32, 2)
            rstd = ph.sb(tag + "rstd", [128, S], F32)
            gt = ph.sb(tag + "gain", [128, nk], F32)
            gk = tag + 'gain'
            P.dma('sp', lambda e: e.dma_start(out=gt[:], in_=gain_ap), writes=[gk], key=('ld', 9))
            for tt in range(4):
                pb = 4 + tt % 2
                for kc in range(nk):
                    t, k = ld.next()
                    P.dma('sp', lambda e, t=t, kc=kc, tt=tt: e.dma_start(
                        out=t[:], in_=srcT[kc * 128:(kc + 1) * 128, tt * 512:(tt + 1) * 512]), writes=[k], key=k)
                    q, qk = sq.next()
                    P.op('act', lambda e, t=t, q=q: e.activation(out=q[:], in_=t[:], func=AF.Square),
                         reads=[k], writes=[qk])
                    P.op('pe', lambda e, q=q, pb=pb, kc=kc: e.matmul(ps[pb][:], lhsT=onesf[:], rhs=q[:],
                                                                      start=(kc == 0), stop=(kc == nk - 1)),
                         reads=[qk], writes=[('ps', pb)])
                P.op('act', lambda e, pb=pb, tt=tt: e.activation(out=rstd[:, tt * 512:(tt + 1) * 512], in_=ps[pb][:],
                                                                 func=AF.Sqrt, bias=epst[:, 0:1], scale=1.0 / dn),
                     reads=[('ps', pb)], writes=[(tag + 'rstd', tt)])
                P.op('dve', lambda e, tt=tt: e.reciprocal(out=rstd[:, tt * 512:(tt + 1) * 512],
                                                          in_=rstd[:, tt * 512:(tt + 1) * 512]),
                     reads=[(tag + 'rstd', tt)], writes=[(tag + 'rstd', tt)])
            if final_out is not None:
                yt = ph.rot(tag + "y", [128, 512], F32, 2)
                ost = ph.rot(tag + "ost", [128, 4, 128], F32, 3)
            for tt in range(4):
                for kc in range(nk):
                    t, k = ld.next()
                    P.dma('sp', lambda e, t=t, kc=kc, tt=tt: e.dma_start(
                        out=t[:], in_=srcT[kc * 128:(kc + 1) * 128, tt * 512:(tt + 1) * 512]), writes=[k], key=k)
                    if final_out is None:
                        P.op('dve', lambda e, t=t, kc=kc, tt=tt: e.scalar_tensor_tensor(
                            out=dst[:, kc, tt * 512:(tt + 1) * 512], in0=t[:], scalar=gt[:, kc:kc + 1],
                            in1=rstd[:, tt * 512:(tt + 1) * 512], op0=ALU.mult, op1=ALU.mult),
                            reads=[k, gk, (tag + 'rstd', tt)], writes=[(dkey, tt)])
                    else:
                        y, yk = yt.next()
                        P.op('dve', lambda e, t=t, kc=kc, tt=tt, y=y: e.scalar_tensor_tensor(
                            out=y[:], in0=t[:], scalar=gt[:, kc:kc + 1],
                            in1=rstd[:, tt * 512:(tt + 1) * 512], op0=ALU.mult, op1=ALU.mult),
                            reads=[k, gk, (tag + 'rstd', tt)], writes=[yk])
                        pb = next_ps()
                        for j in range(4):
                            P.op('pe', lambda e, y=y, j=j, pb=pb: e.transpose(
                                out=ps[pb][:, j * 128:(j + 1) * 128], in_=y[:, j * 128:(j + 1) * 128], identity=identf[:]),
                                reads=[yk], writes=[('ps', pb)])
                        o, ok = ost.next()
                        P.op('act', lambda e, o=o, pb=pb: e.copy(out=o[:], in_=ps[pb][:].rearrange("p (a b) -> p a b", a=4)),
                             reads=[('ps', pb)], writes=[ok])
                        P.dma('sp', lambda e, o=o, tt=tt, kc=kc: e.dma_start(
                            out=final_out[tt * 512:(tt + 1) * 512, kc * 128:(kc + 1) * 128].rearrange("(a p) d -> p a d", p=128),
                            in_=o[:]), reads=[ok], key=ok)

        def load_w(wt, wk, w2d, nk, c0, ncols, dcol=0):
            step = 8
            for k0 in range(0, nk, step):
                k1 = min(nk, k0 + step)
                P.dma('pool', lambda e, k0=k0, k1=k1: e.dma_start(
                    out=wt[:, k0:k1, dcol:dcol + ncols],
                    in_=w2d[k0 * 128:k1 * 128, c0:c0 + ncols].rearrange("(kc p) c -> p kc c", p=128)),
                    writes=[wk], key=wk)

        def gemm_fm(wt, wk, nk, ncols, act, akey, epi):
            for ct in range((ncols + 127) // 128):
                m = min(128, ncols - ct * 128)
                for tt in range(4):
                    pb = next_ps()
                    for kc in range(nk):
                        P.op('pe', lambda e, kc=kc, ct=ct, tt=tt, pb=pb, m=m: e.matmul(
                            ps[pb][0:m, :], lhsT=wt[:, kc, ct * 128:ct * 128 + m],
                            rhs=act[:, kc, tt * 512:(tt + 1) * 512], start=(kc == 0), stop=(kc == nk - 1)),
                            reads=[wk, (akey, tt)], writes=[('ps', pb)])
                    epi(pb, m, ct, tt)

        def gemm_tm(wt, wk, nk, ncols, act, akey, epi):
            for tb in range(16):
                pb = next_ps()
                for kc in range(nk):
                    P.op('pe', lambda e, kc=kc, tb=tb, pb=pb: e.matmul(
                        ps[pb][:, 0:ncols], lhsT=act[:, kc, tb * 128:(tb + 1) * 128], rhs=wt[:, kc, 0:ncols],
                        start=(kc == 0), stop=(kc == nk - 1)),
                        reads=[wk, (akey, tb // 4)], writes=[('ps', pb)])
                epi(pb, tb)

        evrot = {'i': 0}

        def evac(pb, m, dst_tile, dkey, func=None, ncols=512):
            if func is not None:
                P.op('act', lambda e: e.activation(out=dst_tile[0:m, 0:ncols], in_=ps[pb][0:m, 0:ncols], func=func),
                     reads=[('ps', pb)], writes=[dkey])
                return
            evrot['i'] += 1
            if evrot['i'] % 2 == 0:
                P.op('act', lambda e: e.copy(out=dst_tile[0:m, 0:ncols], in_=ps[pb][0:m, 0:ncols]),
                     reads=[('ps', pb)], writes=[dkey])
            else:
                P.op('dve', lambda e: e.tensor_copy(out=dst_tile[0:m, 0:ncols], in_=ps[pb][0:m, 0:ncols]),
                     reads=[('ps', pb)], writes=[dkey])

        def phase_inproj(l, actT):
            with Phase("inproj") as ph:
                norm_fm(ph, xT, 16, attn_norm[l], actT, 'actT', D)
                wts = ph.rot("w", [128, 16, 512], BF16, 3)
                stb = ph.rot("stb", [128, 512], BF16, 4)
                stf = ph.rot("stf", [128, 512], F32, 4)
                W = w_in[l]
                jobs = []

                def fm_store(dstT, r0, dt, func=None):
                    def epi_factory(ncols):
                        def epi(pb, m, ct, tt):
                            t, k = (stb if dt == BF16 else stf).next()
                            evac(pb, m, t, k, func)
                            P.dma('sp', lambda e: e.dma_start(
                                out=dstT[r0 + ct * 128:r0 + ct * 128 + m, tt * 512:(tt + 1) * 512], in_=t[0:m, :]),
                                reads=[k], key=k)
                        return epi
                    return epi_factory

                def add_fm(c0, ncols, dstT, r0, dt, func=None):
                    for j in range(0, ncols, 512):
                        n = min(512, ncols - j)
                        jobs.append(('fm', c0 + j, n, fm_store(dstT, r0 + j, dt, func)(n)))
                add_fm(C_FQ, 1024, fqT, 0, BF16)
                add_fm(C_FK, 1024, fkT, 0, BF16)
                for j in range(2):
                    def epi_tm(pb, tb, j=j):
                        t, k = stb.next()
                        evac(pb, 128, t, k)
                        P.dma('sp', lambda e: e.dma_start(
                            out=fv[tb * 128:(tb + 1) * 128, j * 512:(j + 1) * 512], in_=t[:]), reads=[k], key=k)
                    jobs.append(('tm', C_FV + j * 512, 512, epi_tm))
                jobs.append(('small', None, 96, fm_store(smallT, 0, F32)(96)))
                add_fm(C_KPE, 64, kpeT, 0, F32)
                add_fm(C_CQ, 512, cqT, 0, F32)
                add_fm(C_CKV, 512, ckvT, 0, F32)
                add_fm(C_GQKV, 3072, gqkvT, 0, F32)
                add_fm(C_GZ, 1024, gzsT, 0, BF16, AF.Silu)
                add_fm(C_GATE, 6144, gateT, 0, BF16, AF.Sigmoid)

                slots = {}

                def issue_load(ji):
                    kind, c0, n, epi = jobs[ji]
                    wt, wk = wts.next()
                    slots[ji] = (wt, wk)
                    if kind == 'small':
                        load_w(wt, wk, W, 16, C_FF, 8, 0)
                        load_w(wt, wk, W, 16, C_GB, 8, 32)
                        load_w(wt, wk, W, 16, C_GA, 8, 64)
                    else:
                        load_w(wt, wk, W, 16, c0, n)
                issue_load(0)
                issue_load(1)
                for ji in range(len(jobs)):
                    kind, c0, n, epi = jobs[ji]
                    wt, wk = slots[ji]
                    if kind == 'tm':
                        gemm_tm(wt, wk, 16, n, actT, 'actT', epi)
                    else:
                        gemm_fm(wt, wk, 16, n, actT, 'actT', epi)
                    if ji + 2 < len(jobs):
                        issue_load(ji + 2)

        def attention(ph, pools, h, branch, kT, kk, qT, qk, ek, ekk, eq, eqk, KX, v, vk, bias_t, bk, scale, sbanks, obanks):
            pT_pool, rl_pool, ob_pool = pools
            srot = {'i': 0}
            for tt in range(4):
                nsb = 4 * (tt + 1)
                pO, pL = obanks
                for sb in range(nsb):
                    pb = sbanks[srot['i'] % len(sbanks)]
                    srot['i'] += 1
                    diag = sb >= 4 * tt
                    P.op('pe', lambda e, sb=sb, tt=tt, pb=pb: e.matmul(
                        ps[pb][:], lhsT=kT[:, sb * 128:(sb + 1) * 128], rhs=qT[:, tt * 512:(tt + 1) * 512],
                        start=True, stop=False), reads=[kk, qk], writes=[('ps', pb)])
                    if ek is None:
                        lhs_fn = lambda sb: onesb[0:KX, 0:128]
                    else:
                        lhs_fn = lambda sb: ek[0:KX, sb * 128:(sb + 1) * 128]
                    P.op('pe', lambda e, sb=sb, tt=tt, pb=pb, diag=diag, lhs_fn=lhs_fn: e.matmul(
                        ps[pb][:], lhsT=lhs_fn(sb), rhs=eq[0:KX, tt * 512:(tt + 1) * 512],
                        start=False, stop=(not diag)), reads=[ekk, eqk], writes=[('ps', pb)])
                    if diag:
                        P.op('pe', lambda e, sb=sb, tt=tt, pb=pb: e.matmul(
                            ps[pb][:], lhsT=identb[:], rhs=maskb[:, sb - 4 * tt, :], start=False, stop=True),
                            reads=['maskb'], writes=[('ps', pb)])
                    pt, ptk = pT_pool.next()
                    if bias_t is not None:
                        P.op('act', lambda e, pb=pb, pt=pt, sb=sb: e.activation(
                            out=pt[:], in_=ps[pb][:], func=AF.Exp, bias=bias_t[:, sb:sb + 1], scale=scale),
                            reads=[('ps', pb), bk], writes=[ptk])
                    else:
                        P.op('act', lambda e, pb=pb, pt=pt: e.activation(
                            out=pt[:], in_=ps[pb][:], func=AF.Exp, scale=scale),
                            reads=[('ps', pb)], writes=[ptk])
                    P.op('pe', lambda e, sb=sb, pt=pt, pO=pO, nsb=nsb: e.matmul(
                        ps[pO][:], lhsT=v[:, sb, :], rhs=pt[:], start=(sb == 0), stop=(sb == nsb - 1)),
                        reads=[vk, ptk], writes=[('ps', pO)])
                    P.op('pe', lambda e, sb=sb, pt=pt, pL=pL, nsb=nsb: e.matmul(
                        ps[pL][:], lhsT=onesb[:], rhs=pt[:], start=(sb == 0), stop=(sb == nsb - 1)),
                        reads=[ptk], writes=[('ps', pL)])
                    yield
                rl, rlk = rl_pool.next()
                P.op('dve', lambda e, rl=rl, pL=pL: e.reciprocal(out=rl[:], in_=ps[pL][:]),
                     reads=[('ps', pL)], writes=[rlk])
                ob, obk = ob_pool.next()
                P.op('dve', lambda e, rl=rl, ob=ob, pO=pO: e.tensor_tensor(out=ob[:], in0=ps[pO][:], in1=rl[:], op=ALU.mult),
                     reads=[('ps', pO), rlk], writes=[obk])
                P.dma('pool', lambda e, ob=ob, tt=tt: e.dma_start(
                    out=oT[branch, h * 128:(h + 1) * 128, tt * 512:(tt + 1) * 512], in_=ob[:]), reads=[obk], key=obk)
                yield

        def attn_pools(ph, tag=""):
            return (ph.rot(tag + "pT", [128, 512], BF16, 3), ph.rot(tag + "rl", [128, 512], F32, 2), ph.rot(tag + "ob", [128, 512], BF16, 2))

        def phase_fox_prep(l):
            with Phase("fprep") as ph:
                ff = ph.sb("ff", [8, S], F32)
                nb = ph.sb("nb", [8, 1], F32)
                one8 = ph.sb("one8", [8, 1], F32)
                ones8 = ph.sb("ones8", [8, S], F32)
                e1 = ph.sb("e1", [8, S], F32)
                l1 = ph.sb("l1", [8, S], F32)
                cum = ph.sb("cum", [8, S], F32)
                cs = ph.sb("cs", [8, S], F32)
                r1 = ph.sb("r1", [8, S], F32)
                hf = ph.sb("hf", [8, S], F32)
                parts = [ph.sb(f"part{i}", [8, S], BF16) for i in range(3)]
                P.dma('sp', lambda e: e.dma_start(out=ff[:], in_=smallT[0:8, :]), writes=['ff'], key=('ld', 0))
                P.dma('sp', lambda e: e.dma_start(out=nb[:], in_=fbias[l]), writes=['nb'], key=('ld', 1))
                P.op('dve', lambda e: e.tensor_scalar_mul(out=nb[:], in0=nb[:], scalar1=-1.0), reads=['nb'], writes=['nb'])
                P.op('dve', lambda e: e.memset(one8[:], 1.0), writes=['one8'])
                P.op('dve', lambda e: e.memset(ones8[:], 1.0), writes=['ones8'])
                P.op('act', lambda e: e.activation(out=e1[:], in_=ff[:], func=AF.Exp, bias=nb[:, 0:1], scale=-1.0),
                     reads=['ff', 'nb'], writes=['e1'])
                P.op('act', lambda e: e.activation(out=l1[:], in_=e1[:], func=AF.Ln, bias=one8[:, 0:1]),
                     reads=['e1', 'one8'], writes=['l1'])
                P.op('dve', lambda e: e.tensor_tensor_scan(out=cum[:], data0=ones8[:], data1=l1[:], initial=0.0,
                                                           op0=ALU.mult, op1=ALU.add),
                     reads=['ones8', 'l1'], writes=['cum'])
                P.dma('sp', lambda e: e.dma_start(out=negc, in_=cum[:]), reads=['cum'], key=('st', 0))
                P.op('dve', lambda e: e.tensor_scalar_mul(out=cs[:], in0=cum[:], scalar1=-math.sqrt(128.0)),
                     reads=['cum'], writes=['cs'])
                cur = cs
                curk = 'cs'
                for i in range(3):
                    P.op('dve', lambda e, i=i, cur=cur: e.tensor_copy(out=parts[i][:], in_=cur[:]),
                         reads=[curk], writes=[('part', i)])
                    P.dma('sp', lambda e, i=i: e.dma_start(out=crow[i], in_=parts[i][:]), reads=[('part', i)], key=('st', 1 + i))
                    if i < 2:
                        P.op('dve', lambda e, i=i: e.tensor_copy(out=hf[:], in_=parts[i][:]),
                             reads=[('part', i)], writes=['hf'])
                        P.op('dve', lambda e, cur=cur: e.tensor_tensor(out=r1[:], in0=cur[:], in1=hf[:], op=ALU.subtract),
                             reads=[curk, 'hf'], writes=['r1'])
                        cur = r1
                        curk = 'r1'

        def fox_head_loads(fx, h):
            qT, qk = fx['qTp'].next()
            kT, kk = fx['kTp'].next()
            v, vk = fx['vp'].next()
            nct, nck = fx['ncp'].next()
            cr, crk = fx['crp'].next()
            P.dma('sp', lambda e: e.dma_start(out=qT[:], in_=fqT[h * 128:(h + 1) * 128, :]), writes=[qk], key=qk)
            P.dma('sp', lambda e: e.dma_start(out=kT[:], in_=fkT[h * 128:(h + 1) * 128, :]), writes=[kk], key=kk)
            P.dma('sp', lambda e: e.dma_start(
                out=v[:], in_=fv[:, h * 128:(h + 1) * 128].rearrange("(sb p) c -> p sb c", p=128)), writes=[vk], key=vk)
            P.dma('sp', lambda e: e.dma_start(
                out=nct[:], in_=negc[h, :].rearrange("(sb p) -> p sb", p=128)), writes=[nck], key=nck)
            P.dma('sp', lambda e: e.dma_start(out=cr[:], in_=crow[:, h, :]), writes=[crk], key=crk)
            return (qT, qk, kT, kk, v, vk, nct, nck, cr, crk)

        def fox_setup(ph):
            return dict(pools=attn_pools(ph, "f"),
                        qTp=ph.rot("fqT", [128, S], BF16, 2), kTp=ph.rot("fkT", [128, S], BF16, 2),
                        vp=ph.rot("fv", [128, 16, 128], BF16, 2), ncp=ph.rot("fnc", [128, 16], F32, 2),
                        crp=ph.rot("fcr", [3, S], BF16, 2))

        def phase_attn(l):
            with Phase("attn") as ph:
                fx = fox_setup(ph)
                cqn = ph.sb("cqn", [128, 4, S], BF16)
                ckvn = ph.sb("ckvn", [128, 4, S], BF16)
                norm_fm(ph, cqT, 4, qnorm[l], cqn, 'cqn', 512, tag="nq")
                norm_fm(ph, ckvT, 4, kvnorm[l], ckvn, 'ckvn', 512, tag="nq")
                wuq = ph.sb("wuq", [128, 4, 1536], BF16)
                wukv = ph.sb("wukv", [128, 4, 2048], BF16)
                load_w(wuq, 'wuq', w_uq[l], 4, 0, 1536)
                load_w(wukv, 'wukv', w_ukv[l], 4, 0, 2048)
                cos2 = ph.sb("cos2", [64, S], F32)
                sin2 = ph.sb("sin2", [64, S], F32)
                rotf = ph.sb("rotf", [64, 64], F32)
                rotm = ph.sb("rotm", [64, 64], BF16)
                P.dma('sp', lambda e: e.dma_start(out=cos2[:], in_=c_rope[0]), writes=['cos2'], key=('ld', 0))
                P.dma('sp', lambda e: e.dma_start(out=sin2[:], in_=c_rope[1]), writes=['sin2'], key=('ld', 1))
                P.dma('sp', lambda e: e.dma_start(out=rotf[:], in_=c_rot), writes=['rotf'], key=('ld', 2))
                P.op('dve', lambda e: e.tensor_copy(out=rotm[:], in_=rotf[:]), reads=['rotf'], writes=['rotm'])
                xs_p = ph.rot("xs", [64, 512], F32, 2)
                xb_p = ph.rot("xb", [64, 512], BF16, 2)
                t1_p = ph.rot("t1", [64, 512], F32, 2)
                t2_p = ph.rot("t2", [64, 512], F32, 2)

                def rope_tile(xs, xsk, dst, dkey, tt):
                    xb, xbk = xb_p.next()
                    P.op('act', lambda e: e.copy(out=xb[:], in_=xs[:]), reads=[xsk], writes=[xbk])
                    pb = next_ps()
                    P.op('pe', lambda e: e.matmul(ps[pb][0:64, :], lhsT=rotm[:], rhs=xb[:], start=True, stop=True),
                         reads=['rotm', xbk], writes=[('ps', pb)])
                    t1, t1k = t1_p.next()
                    t2, t2k = t2_p.next()
                    P.op('dve', lambda e: e.tensor_tensor(out=t1[:], in0=xs[:], in1=cos2[:, tt * 512:(tt + 1) * 512], op=ALU.mult),
                         reads=[xsk, 'cos2'], writes=[t1k])
                    P.op('dve', lambda e: e.tensor_tensor(out=t2[:], in0=ps[pb][0:64, :], in1=sin2[:, tt * 512:(tt + 1) * 512], op=ALU.mult),
                         reads=[('ps', pb), 'sin2'], writes=[t2k])
                    P.op('dve', lambda e: e.tensor_tensor(out=dst[0:64, tt * 512:(tt + 1) * 512], in0=t1[:], in1=t2[:], op=ALU.add),
                         reads=[t1k, t2k], writes=[dkey])
                kper = ph.sb("kper", [64, S], BF16)
                for tt in range(4):
                    xs, xsk = xs_p.next()
                    P.dma('sp', lambda e, xs=xs, tt=tt: e.dma_start(out=xs[:], in_=kpeT[:, tt * 512:(tt + 1) * 512]), writes=[xsk], key=xsk)
                    rope_tile(xs, xsk, kper, 'kper', tt)
                pools = attn_pools(ph, "m")
                qnp = ph.rot("qn", [128, S], BF16, 2)
                qpp = ph.rot("qp", [64, S], BF16, 2)
                knp = ph.rot("kn", [128, S], BF16, 2)
                vp = ph.rot("v", [128, 16, 128], BF16, 2)
                for h in range(8):
                    fl = fox_head_loads(fx, h)
                    qn, qnk = qnp.next()
                    qp, qpk = qpp.next()
                    kn, knk = knp.next()
                    v, vk = vp.next()
                    for tt in range(4):
                        pb = next_ps()
                        for kc in range(4):
                            P.op('pe', lambda e, kc=kc, tt=tt, pb=pb, h=h: e.matmul(
                                ps[pb][:], lhsT=wuq[:, kc, h * 192:h * 192 + 128], rhs=cqn[:, kc, tt * 512:(tt + 1) * 512],
                                start=(kc == 0), stop=(kc == 3)), reads=['wuq', ('cqn', tt)], writes=[('ps', pb)])
                        evac(pb, 128, qn[:, tt * 512:(tt + 1) * 512], qnk)
                        pb = next_ps()
                        for kc in range(4):
                            P.op('pe', lambda e, kc=kc, tt=tt, pb=pb, h=h: e.matmul(
                                ps[pb][0:64, :], lhsT=wuq[:, kc, h * 192 + 128:h * 192 + 192], rhs=cqn[:, kc, tt * 512:(tt + 1) * 512],
                                start=(kc == 0), stop=(kc == 3)), reads=['wuq', ('cqn', tt)], writes=[('ps', pb)])
                        xs, xsk = xs_p.next()
                        P.op('act', lambda e, pb=pb, xs=xs: e.copy(out=xs[:], in_=ps[pb][0:64, :]), reads=[('ps', pb)], writes=[xsk])
                        rope_tile(xs, xsk, qp, qpk, tt)
                        pb = next_ps()
                        for kc in range(4):
                            P.op('pe', lambda e, kc=kc, tt=tt, pb=pb, h=h: e.matmul(
                                ps[pb][:], lhsT=wukv[:, kc, h * 256:h * 256 + 128], rhs=ckvn[:, kc, tt * 512:(tt + 1) * 512],
                                start=(kc == 0), stop=(kc == 3)), reads=['wukv', ('ckvn', tt)], writes=[('ps', pb)])
                        evac(pb, 128, kn[:, tt * 512:(tt + 1) * 512], knk)
                    for tg in range(4):
                        pb = next_ps()
                        for j in range(4):
                            tb = tg * 4 + j
                            for kc in range(4):
                                P.op('pe', lambda e, kc=kc, tb=tb, j=j, pb=pb, h=h: e.matmul(
                                    ps[pb][:, j * 128:(j + 1) * 128], lhsT=ckvn[:, kc, tb * 128:(tb + 1) * 128],
                                    rhs=wukv[:, kc, h * 256 + 128:h * 256 + 256], start=(kc == 0), stop=(kc == 3)),
                                    reads=['wukv', ('ckvn', tg)], writes=[('ps', pb)])
                        evac(pb, 128, v[:, tg * 4:(tg + 1) * 4, :].rearrange("p a b -> p (a b)"), vk)
                    (fqT_, fqk, fkT_, fkk, fv_, fvk, fnct, fnck, fcr, fcrk) = fl
                    gens = [attention(ph, fx['pools'], h, 0, fkT_, fkk, fqT_, fqk, None, 'onesb', fcr, fcrk, 3, fv_, fvk, fnct, fnck,
                                      128.0 ** -0.5, [0, 1], (4, 5)),
                            attention(ph, pools, h, 1, kn, knk, qn, qnk, kper, 'kper', qp, qpk, 64, v, vk, None, None,
                                      192.0 ** -0.5, [2, 3], (6, 7))]
                    while gens:
                        for g_ in list(gens):
                            try:
                                next(g_)
                            except StopIteration:
                                gens.remove(g_)
        def phase_gdn_prep(l):
            with Phase("gprep") as ph:
                t8 = lambda n: ph.sb(n, [8, S], F32)
                gbt, gat, blk, beta, e1, sp, g, gc, ngc, eg, bg, dl, edl = [t8(n) for n in
                    ("gbt", "gat", "blk", "beta", "e1", "sp", "g", "gc", "ngc", "eg", "bg", "dl", "edl")]
                dtb = ph.sb("dtb", [8, 1], F32)
                alog = ph.sb("alog", [8, 1], F32)
                negA = ph.sb("negA", [8, 1], F32)
                one8 = ph.sb("one8", [8, 1], F32)
                cdc = ph.sb("cdc", [8, 16], F32)
                P.dma('sp', lambda e: e.dma_start(out=gbt[:], in_=smallT[32:40, :]), writes=['gbt'], key=('ld', 0))
                P.dma('sp', lambda e: e.dma_start(out=gat[:], in_=smallT[64:72, :]), writes=['gat'], key=('ld', 1))
                P.dma('sp', lambda e: e.dma_start(out=blk[:], in_=c_blk), writes=['blk'], key=('ld', 2))
                P.dma('sp', lambda e: e.dma_start(out=dtb[:], in_=gdtb[l]), writes=['dtb'], key=('ld', 3))
                P.dma('sp', lambda e: e.dma_start(out=alog[:], in_=galog[l]), writes=['alog'], key=('ld', 4))
                P.op('dve', lambda e: e.memset(one8[:], 1.0), writes=['one8'])
                P.op('act', lambda e: e.activation(out=beta[:], in_=gbt[:], func=AF.Sigmoid), reads=['gbt'], writes=['beta'])
                P.op('act', lambda e: e.activation(out=e1[:], in_=gat[:], func=AF.Exp, bias=dtb[:, 0:1]), reads=['gat', 'dtb'], writes=['e1'])
                P.op('act', lambda e: e.activation(out=sp[:], in_=e1[:], func=AF.Ln, bias=one8[:, 0:1]), reads=['e1', 'one8'], writes=['sp'])
                P.op('act', lambda e: e.activation(out=negA[:], in_=alog[:], func=AF.Exp), reads=['alog'], writes=['negA'])
                P.op('dve', lambda e: e.tensor_scalar_mul(out=negA[:], in0=negA[:], scalar1=-1.0), reads=['negA'], writes=['negA'])
                P.op('dve', lambda e: e.tensor_scalar_mul(out=g[:], in0=sp[:], scalar1=negA[:, 0:1]), reads=['sp', 'negA'], writes=['g'])
                P.op('dve', lambda e: e.tensor_tensor_scan(out=gc[:], data0=blk[:], data1=g[:], initial=0.0, op0=ALU.mult, op1=ALU.add),
                     reads=['blk', 'g'], writes=['gc'])
                P.op('dve', lambda e: e.tensor_scalar_mul(out=ngc[:], in0=gc[:], scalar1=-1.0), reads=['gc'], writes=['ngc'])
                P.op('act', lambda e: e.activation(out=eg[:], in_=gc[:], func=AF.Exp), reads=['gc'], writes=['eg'])
                P.op('dve', lambda e: e.tensor_tensor(out=bg[:], in0=beta[:], in1=eg[:], op=ALU.mult), reads=['beta', 'eg'], writes=['bg'])
                g3 = lambda t: t[:].rearrange("p (a b) -> p a b", b=128)
                P.op('dve', lambda e: e.tensor_tensor(out=g3(dl), in0=g3(gc)[:, :, 127:128].to_broadcast([8, 16, 128]), in1=g3(gc), op=ALU.subtract),
                     reads=['gc'], writes=['dl'])
                P.op('act', lambda e: e.activation(out=edl[:], in_=dl[:], func=AF.Exp), reads=['dl'], writes=['edl'])
                P.op('act', lambda e: e.activation(out=cdc[:], in_=g3(gc)[:, :, 127], func=AF.Exp), reads=['gc'], writes=['cdc'])
                P.op('dve', lambda e: e.tensor_scalar_mul(out=beta[:], in0=beta[:], scalar1=-1.0), reads=['beta', 'bg'], writes=['nbeta'])
                for i, (t, k) in enumerate([(beta, 'nbeta'), (gc, 'gc'), (ngc, 'ngc'), (bg, 'bg'), (edl, 'edl'), (eg, 'eg')]):
                    P.dma('sp', lambda e, i=i, t=t: e.dma_start(out=gsc[i], in_=t[:]), reads=[k], key=('st', i))
                P.dma('sp', lambda e: e.dma_start(out=gcd, in_=cdc[:]), reads=['cdc'], key=('st', 7))

        def phase_gdn(l):
            with Phase("gdn") as ph:
                rawp = ph.rot("raw", [128, S], F32, 2)
                cvp = ph.rot("cv", [128, S], F32, 2)
                slp = ph.rot("sl", [128, S], F32, 2)
                cwp = ph.rot("cw", [128, 4], F32, 2)
                cvrot = {'i': 0}
                Gb = ph.sb("Gb", [128, S], F32)
                GbU = ph.sb("GbU", [128, S], F32)
                egb = ph.sb("egb", [128, S], F32)
                qn = ph.sb("qn", [128, S], BF16)
                qd2 = [ph.sb(f"qd{i}", [128, S], BF16) for i in range(2)]
                kn = ph.sb("kn", [128, S], BF16)
                vb = ph.sb("vb", [128, S], BF16)
                kbg_t = ph.sb("kbg_t", [128, 16, 128], BF16)
                kdl_t2 = [ph.sb(f"kdl_t{i}", [128, 16, 128], BF16) for i in range(2)]
                vb_t = ph.sb("vb_t", [128, 16, 128], BF16)
                u_all2 = [ph.sb(f"u_all{i}", [128, 16, 128], F32) for i in range(2)]
                wT2 = [ph.sb(f"wT{i}", [128, S], BF16) for i in range(2)]
                intraT2 = [ph.sb(f"intraT{i}", [128, 16, 128], BF16) for i in range(2)]
                oTh = ph.sb("oTh", [128, S], F32)
                colt = {n: ph.sb("c_" + n, [128, 16], F32) for n in ("nbeta", "gc", "ngc", "bg", "edl")}
                cdb2 = [ph.sb(f"cdb{i}", [128, 16], F32) for i in range(2)]
                gon = ph.sb("gon", [128, 1], F32)
                MU = ph.sb("MU", [128, 128], F32)
                ML = ph.sb("ML", [128, 128], F32)
                sqp = ph.rot("sq", [128, 512], F32, 2)
                rnp = ph.rot("rn", [128, 512], F32, 2)
                zsp = ph.rot("zs", [128, 512], BF16, 2)
                o2p = ph.rot("o2", [128, 512], F32, 2)
                obp = ph.rot("ob", [128, 512], BF16, 2)
                S32 = ph.sb("S32", [128, 128], F32)
                Sb = ph.sb("Sb", [128, 128], BF16)
                vnp = ph.rot("vn", [128, 128], BF16, 2)
                NG = 8
                cht = {n: [ph.sb(f"ch_{n}{j}", [128, 128], F32 if n in ("dL", "dU") else BF16) for j in range(NG)] for n in
                       ("dL", "dU", "A0", "B0", "Tt", "T", "P1s", "P2s")}

                def qapb(b, q):
                    return ps[b][:].bitcast(BF16)[:, 0:128]
                lvl = ph.sb("lvl", [128, 7, 2, 128], F32)
                P.dma('sp', lambda e: e.dma_start(out=lvl[:], in_=c_lvl.rearrange("l t p f -> p l t f")), writes=['lvl'], key=('ld', 13))
                Ttb = [ph.sb(f"ch_Ttb{j}", [128, 128], BF16) for j in range(NG)]
                P.dma('sp', lambda e: e.dma_start(out=MU[:], in_=c_tri[0]), writes=['MU'], key=('ld', 0))
                P.dma('sp', lambda e: e.dma_start(out=ML[:], in_=c_tri[1]), writes=['ML'], key=('ld', 1))
                P.dma('sp', lambda e: e.dma_start(out=gon[:], in_=gonorm[l]), writes=['gon'], key=('ld', 2))
                qrot = {'i': 0}

                def next_q():
                    i = qrot['i'] % 6
                    qrot['i'] += 1
                    return i, 0

                def qap(b, q):
                    return ps[b][:, q * 128:(q + 1) * 128]

                cprot = {'i': 0}

                def cp(dst, src, reads, writes):
                    cprot['i'] += 1
                    if cprot['i'] % 2 == 0:
                        P.op('act', lambda e: e.copy(out=dst, in_=src), reads=reads, writes=writes)
                    else:
                        P.op('dve', lambda e: e.tensor_copy(out=dst, in_=src), reads=reads, writes=writes)

                def genA(h):
                    par = str(h % 2)
                    u_all = u_all2[h % 2]
                    wT = wT2[h % 2]
                    intraT = intraT2[h % 2]
                    qd = qd2[h % 2]
                    kdl_t = kdl_t2[h % 2]
                    cdb = cdb2[h % 2]
                    for i, n in enumerate(("nbeta", "gc", "ngc", "bg", "edl")):
                        P.dma('sp', lambda e, i=i, n=n, h=h: e.dma_start(
                            out=colt[n][:], in_=gsc[i, h, :].rearrange("(sb p) -> p sb", p=128)), writes=['c_' + n], key=('ld', 3 + i))
                    P.dma('sp', lambda e, h=h: e.dma_start(out=Gb[:], in_=gsc[1, h:h + 1, :].to_broadcast([128, S])), writes=['Gb'], key=('ld', 8))
                    P.dma('sp', lambda e, h=h: e.dma_start(out=egb[:], in_=gsc[5, h:h + 1, :].to_broadcast([128, S])), writes=['egb'], key=('ld', 9))
                    g3_ = lambda t: t[:].rearrange("p (a b) -> p a b", b=128)
                    P.op('pool', lambda e: e.tensor_tensor(out=g3_(GbU), in0=g3_(Gb), in1=MU[:].unsqueeze(1).to_broadcast([128, 16, 128]), op=ALU.add),
                         reads=['Gb', 'MU'], writes=['GbU'])
                    P.op('pool', lambda e: e.tensor_tensor(out=g3_(Gb), in0=g3_(Gb), in1=ML[:].unsqueeze(1).to_broadcast([128, 16, 128]), op=ALU.add),
                         reads=['Gb', 'ML', 'GbU'], writes=['Gb'])
                    P.dma('sp', lambda e, h=h: e.dma_start(out=cdb[:], in_=gcd[h:h + 1, :].to_broadcast([128, 16])), writes=['cdb' + par], key=('ld', 10, par))
                    for which, c0 in (("q", h * 128), ("k", 1024 + h * 128), ("v", 2048 + h * 128)):
                        raw, rawk = rawp.next()
                        cv, cvk = cvp.next()
                        sl, slk = slp.next()
                        cw, cwk = cwp.next()
                        cvrot['i'] += 1
                        ceng = 'dve'
                        P.dma('sp', lambda e, c0=c0, raw=raw: e.dma_start(out=raw[:], in_=gqkvT[c0:c0 + 128, :]), writes=[rawk], key=rawk)
                        P.dma('sp', lambda e, c0=c0, cw=cw: e.dma_start(out=cw[:], in_=gconv[l, c0:c0 + 128, :]), writes=[cwk], key=cwk)
                        P.op(ceng, lambda e, raw=raw, cv=cv, cw=cw: e.tensor_scalar_mul(out=cv[:], in0=raw[:], scalar1=cw[:, 3:4]), reads=[rawk, cwk], writes=[cvk])
                        for sh in (1, 2, 3):
                            P.op(ceng, lambda e, sh=sh, raw=raw, cv=cv, cw=cw: e.scalar_tensor_tensor(
                                out=cv[:, sh:], in0=raw[:, 0:S - sh], scalar=cw[:, 3 - sh:4 - sh], in1=cv[:, sh:],
                                op0=ALU.mult, op1=ALU.add), reads=[rawk, cwk, cvk], writes=[cvk])
                        P.op('act', lambda e, sl=sl, cv=cv: e.activation(out=sl[:], in_=cv[:], func=AF.Silu), reads=[cvk], writes=[slk])
                        if which == "v":
                            P.op('pool', lambda e, sl=sl: e.tensor_copy(out=vb[:], in_=sl[:]), reads=[slk], writes=['vb'])
                            continue
                        for tt in range(4):
                            tsl = slice(tt * 512, (tt + 1) * 512)
                            sq, sqk = sqp.next()
                            rn, rnk = rnp.next()
                            pb = 6 + tt % 2
                            P.op('act', lambda e, sq=sq, tsl=tsl, sl=sl: e.activation(out=sq[:], in_=sl[:, tsl], func=AF.Square), reads=[slk], writes=[sqk])
                            P.op('pe', lambda e, sq=sq, pb=pb: e.matmul(ps[pb][:], lhsT=onesf[:], rhs=sq[:], start=True, stop=True),
                                 reads=[sqk], writes=[('ps', pb)])
                            P.op('act', lambda e, rn=rn, pb=pb: e.activation(out=rn[:], in_=ps[pb][:], func=AF.Sqrt, bias=epst[:, 0:1]),
                                 reads=[('ps', pb)], writes=[rnk])
                            P.op('dve', lambda e, rn=rn: e.reciprocal(out=rn[:], in_=rn[:]), reads=[rnk], writes=[rnk])
                            if which == "k":
                                P.op('dve', lambda e, rn=rn, tsl=tsl, sl=sl: e.tensor_tensor(out=kn[:, tsl], in0=sl[:, tsl], in1=rn[:], op=ALU.mult),
                                     reads=[slk, rnk], writes=['kn'])
                            else:
                                P.op('dve', lambda e, rn=rn, tsl=tsl, sl=sl: e.scalar_tensor_tensor(
                                    out=sl[:, tsl], in0=sl[:, tsl], scalar=128.0 ** -0.5, in1=rn[:], op0=ALU.mult, op1=ALU.mult),
                                    reads=[slk, rnk], writes=[slk])
                                P.op('act', lambda e, tsl=tsl, sl=sl: e.copy(out=qn[:, tsl], in_=sl[:, tsl]), reads=[slk], writes=['qn'])
                                P.op('dve', lambda e, tsl=tsl, sl=sl: e.tensor_tensor(out=qd[:, tsl], in0=sl[:, tsl], in1=egb[:, tsl], op=ALU.mult),
                                     reads=[slk, 'egb'], writes=['qd' + par])
                    if GSTOP == 1:
                        return
                    yield
                    for tg in range(4):
                        for (src, sk, outs) in ((kn, 'kn', ((kbg_t, 'kbg_t', 'bg'), (kdl_t, 'kdl_t' + par, 'edl'))),
                                                (vb, 'vb', ((vb_t, 'vb_t', 'nbeta'),))):
                            pb = 6 + (qrot['i'] % 2)
                            qrot['i'] += 1
                            for j in range(4):
                                blk_ = tg * 4 + j
                                P.op('pe', lambda e, j=j, blk_=blk_, pb=pb, src=src: e.transpose(
                                    out=ps[pb][:].bitcast(BF16)[:, j * 128:(j + 1) * 128], in_=src[:, blk_ * 128:(blk_ + 1) * 128],
                                    identity=identb[:]), reads=[sk], writes=[('ps', pb)])
                            for (dst, dk_, cn) in outs:
                                sgn = -1.0 if cn == 'nbeta' else 1.0
                                P.op('dve', lambda e, dst=dst, cn=cn, tg=tg, pb=pb, sgn=sgn: e.scalar_tensor_tensor(
                                    out=dst[:, tg * 4:(tg + 1) * 4, :],
                                    in0=ps[pb][:].bitcast(BF16)[:, 0:512].rearrange("p (a b) -> p a b", a=4), scalar=sgn,
                                    in1=colt[cn][:, tg * 4:(tg + 1) * 4].unsqueeze(2).to_broadcast([128, 4, 128]),
                                    op0=ALU.mult, op1=ALU.mult), reads=[('ps', pb), 'c_' + cn], writes=[dk_])
                    if GSTOP == 2:
                        return
                    yield
                    for g0 in range(0, 16, NG):
                        blks = list(range(g0, g0 + NG))
                        for j, b_ in enumerate(blks):
                            bs = slice(b_ * 128, (b_ + 1) * 128)
                            qa = next_q()
                            qb = next_q()
                            P.op('pe', lambda e, bs=bs, qa=qa: e.matmul(qap(*qa), lhsT=kn[:, bs], rhs=kn[:, bs], start=True, stop=True),
                                 reads=['kn'], writes=[('ps', qa[0])])
                            P.op('pe', lambda e, bs=bs, qb=qb: e.matmul(qap(*qb), lhsT=kn[:, bs], rhs=qn[:, bs], start=True, stop=True),
                                 reads=['kn', 'qn'], writes=[('ps', qb[0])])
                            if GSUB < 2:
                                continue
                            if GSUB < 3:
                                continue
                            P.op('act', lambda e, j=j, b_=b_, bs=bs: e.activation(out=cht['dL'][j][:], in_=Gb[:, bs], func=AF.Exp,
                                                                           bias=colt['gc'][:, b_:b_ + 1], scale=-1.0),
                                 reads=['Gb', 'c_gc'], writes=[('dL', j)])
                            P.op('act', lambda e, j=j, b_=b_, bs=bs: e.activation(out=cht['dU'][j][:], in_=GbU[:, bs], func=AF.Exp,
                                                                           bias=colt['ngc'][:, b_:b_ + 1], scale=1.0),
                                 reads=['GbU', 'c_ngc'], writes=[('dU', j)])
                            if GSUB < 4:
                                continue
                            P.op('dve', lambda e, j=j, b_=b_, qa=qa: e.scalar_tensor_tensor(
                                out=cht['A0'][j][:], in0=qap(*qa), scalar=colt['nbeta'][:, b_:b_ + 1], in1=cht['dL'][j][:],
                                op0=ALU.mult, op1=ALU.mult), reads=[('ps', qa[0]), 'c_nbeta', ('dL', j)], writes=[('A0', j)])
                            P.op('dve', lambda e, j=j, b_=b_, qb=qb: e.tensor_tensor(
                                out=intraT[:, b_, :], in0=qap(*qb), in1=cht['dU'][j][:], op=ALU.mult),
                                reads=[('ps', qb[0]), ('dU', j)], writes=['intraT' + par])
                            if GSUB < 5:
                                continue
                            qc = next_q()
                            P.op('pe', lambda e, j=j, qc=qc: e.transpose(out=qapb(*qc), in_=cht['A0'][j][:], identity=identb[:]),
                                 reads=[('A0', j)], writes=[('ps', qc[0])])
                            cp(cht['B0'][j][:], qapb(*qc), [('ps', qc[0])], [('B0', j)])
                            yield
                            P.op('pool', lambda e, j=j: e.tensor_tensor(out=cht['T'][j][:], in0=cht['A0'][j][:], in1=lvl[:, 0, 0, :], op=ALU.mult),
                                 reads=[('A0', j), 'lvl'], writes=[('T', j)])
                            P.op('pool', lambda e, j=j: e.tensor_tensor(out=cht['T'][j][:], in0=cht['T'][j][:], in1=identf[:], op=ALU.add),
                                 reads=[('T', j)], writes=[('T', j)])
                            P.op('pool', lambda e, j=j: e.tensor_tensor(out=cht['Tt'][j][:], in0=cht['B0'][j][:], in1=lvl[:, 0, 1, :], op=ALU.mult),
                                 reads=[('B0', j), 'lvl'], writes=[('Tt', j)])
                            P.op('pool', lambda e, j=j: e.tensor_tensor(out=cht['Tt'][j][:], in0=cht['Tt'][j][:], in1=identf[:], op=ALU.add),
                                 reads=[('Tt', j)], writes=[('Tt', j)])
                        for lv in range(1, 0 if GSTOP == 31 else 7):
                            yield
                            for j in range(NG):
                                qa = next_q()
                                P.op('pe', lambda e, j=j, qa=qa: e.matmul(qap(*qa), lhsT=cht['B0'][j][:], rhs=cht['T'][j][:], start=True, stop=True),
                                     reads=[('B0', j), ('T', j)], writes=[('ps', qa[0])])
                                P.op('act', lambda e, j=j, qa=qa: e.copy(out=cht['P1s'][j][:], in_=qap(*qa)), reads=[('ps', qa[0])], writes=[('P1s', j)])
                            yield
                            for j in range(NG):
                                qb = next_q()
                                P.op('pe', lambda e, j=j, qb=qb: e.matmul(qap(*qb), lhsT=cht['Tt'][j][:], rhs=cht['P1s'][j][:], start=True, stop=True),
                                     reads=[('Tt', j), ('P1s', j)], writes=[('ps', qb[0])])
                                P.op('dve', lambda e, j=j, qb=qb, lv=lv: e.tensor_tensor(out=cht['P2s'][j][:], in0=qap(*qb), in1=lvl[:, lv, 0, :], op=ALU.mult),
                                     reads=[('ps', qb[0]), 'lvl'], writes=[('P2s', j)])
                            yield
                            for j in range(NG):
                                P.op('pool', lambda e, j=j: e.tensor_tensor(out=cht['T'][j][:], in0=cht['T'][j][:], in1=cht['P2s'][j][:], op=ALU.add),
                                     reads=[('T', j), ('P2s', j)], writes=[('T', j)])
                                qc = next_q()
                                P.op('pe', lambda e, j=j, qc=qc: e.transpose(out=qapb(*qc), in_=cht['P2s'][j][:], identity=identb[:]),
                                     reads=[('P2s', j)], writes=[('ps', qc[0])])
                                P.op('dve', lambda e, j=j, qc=qc: e.tensor_tensor(out=cht['Tt'][j][:], in0=qapb(*qc), in1=cht['Tt'][j][:], op=ALU.add),
                                     reads=[('ps', qc[0]), ('Tt', j)], writes=[('Tt', j)])
                        for j, b_ in enumerate(blks if GSTOP not in (31, 32) else []):
                            bs = slice(b_ * 128, (b_ + 1) * 128)
                            qa = next_q()
                            qb = next_q()
                            P.op('pe', lambda e, j=j, b_=b_, qa=qa: e.matmul(qap(*qa), lhsT=cht['Tt'][j][:], rhs=vb_t[:, b_, :], start=True, stop=True),
                                 reads=[('Tt', j), 'vb_t'], writes=[('ps', qa[0])])
                            P.op('pe', lambda e, j=j, b_=b_, qb=qb: e.matmul(qap(*qb), lhsT=kbg_t[:, b_, :], rhs=cht['Tt'][j][:], start=True, stop=True),
                                 reads=[('Tt', j), 'kbg_t'], writes=[('ps', qb[0])])
                            cp(u_all[:, b_, :], qap(*qa), [('ps', qa[0])], ['u_all' + par])
                            cp(wT[:, bs], qap(*qb), [('ps', qb[0])], ['wT' + par])
                    yield


                def genB(h):
                    par = str(h % 2)
                    u_all = u_all2[h % 2]
                    wT = wT2[h % 2]
                    intraT = intraT2[h % 2]
                    qd = qd2[h % 2]
                    kdl_t = kdl_t2[h % 2]
                    cdb = cdb2[h % 2]
                    if GSTOP in (3, 31, 32):
                        return
                    for b_ in range(16):
                        bs = slice(b_ * 128, (b_ + 1) * 128)
                        vn, vnk = vnp.next()
                        if b_ == 0:
                            P.op('dve', lambda e, vn=vn: e.tensor_copy(out=vn[:], in_=u_all[:, 0, :]), reads=['u_all' + par], writes=[vnk])
                        else:
                            qa = next_q()
                            P.op('pe', lambda e, bs=bs, qa=qa: e.matmul(qap(*qa), lhsT=wT[:, bs], rhs=Sb[:], start=True, stop=True),
                                 reads=['wT' + par, 'Sb'], writes=[('ps', qa[0])])
                            P.op('dve', lambda e, vn=vn, b_=b_, qa=qa: e.tensor_tensor(out=vn[:], in0=u_all[:, b_, :], in1=qap(*qa), op=ALU.subtract),
                                 reads=['u_all' + par, ('ps', qa[0])], writes=[vnk])
                        qo = next_q()
                        if b_ > 0:
                            P.op('pe', lambda e, bs=bs, qo=qo: e.matmul(qap(*qo), lhsT=Sb[:], rhs=qd[:, bs], start=True, stop=False),
                                 reads=['Sb', 'qd' + par], writes=[('ps', qo[0])])
                        P.op('pe', lambda e, vn=vn, b_=b_, qo=qo: e.matmul(qap(*qo), lhsT=vn[:], rhs=intraT[:, b_, :], start=(b_ == 0), stop=True),
                             reads=[vnk, 'intraT' + par], writes=[('ps', qo[0])])
                        P.op('act', lambda e, bs=bs, qo=qo: e.copy(out=oTh[:, bs], in_=qap(*qo)), reads=[('ps', qo[0])], writes=['oTh'])
                        if b_ < 15:
                            qs_ = next_q()
                            P.op('pe', lambda e, vn=vn, b_=b_, qs_=qs_: e.matmul(qap(*qs_), lhsT=kdl_t[:, b_, :], rhs=vn[:], start=True, stop=True),
                                 reads=[vnk, 'kdl_t' + par], writes=[('ps', qs_[0])])
                            if b_ == 0:
                                P.op('dve', lambda e, qs_=qs_: e.tensor_copy(out=S32[:], in_=qap(*qs_)), reads=[('ps', qs_[0])], writes=['S32'])
                            else:
                                P.op('dve', lambda e, qs_=qs_, b_=b_: e.scalar_tensor_tensor(
                                    out=S32[:], in0=S32[:], scalar=cdb[:, b_:b_ + 1], in1=qap(*qs_), op0=ALU.mult, op1=ALU.add),
                                    reads=['S32', 'cdb' + par, ('ps', qs_[0])], writes=['S32'])
                            P.op('act', lambda e: e.copy(out=Sb[:], in_=S32[:]), reads=['S32'], writes=['Sb'])
                        yield
                    if GSTOP == 4:
                        return
                    for tt in range(4):
                        tsl = slice(tt * 512, (tt + 1) * 512)
                        sq, sqk = sqp.next()
                        rn, rnk = rnp.next()
                        pb = 6 + tt % 2
                        P.op('act', lambda e, sq=sq, tsl=tsl: e.activation(out=sq[:], in_=oTh[:, tsl], func=AF.Square), reads=['oTh'], writes=[sqk])
                        P.op('pe', lambda e, sq=sq, pb=pb: e.matmul(ps[pb][:], lhsT=onesf[:], rhs=sq[:], start=True, stop=True),
                             reads=[sqk], writes=[('ps', pb)])
                        P.op('act', lambda e, rn=rn, pb=pb: e.activation(out=rn[:], in_=ps[pb][:], func=AF.Sqrt, bias=epst[:, 0:1], scale=1.0 / 128),
                             reads=[('ps', pb)], writes=[rnk])
                        P.op('dve', lambda e, rn=rn: e.reciprocal(out=rn[:], in_=rn[:]), reads=[rnk], writes=[rnk])
                        zs, zsk = zsp.next()
                        P.dma('sp', lambda e, zs=zs, h=h, tsl=tsl: e.dma_start(out=zs[:], in_=gzsT[h * 128:(h + 1) * 128, tsl]), writes=[zsk], key=zsk)
                        o2, o2k = o2p.next()
                        P.op('dve', lambda e, o2=o2, rn=rn, tsl=tsl: e.scalar_tensor_tensor(
                            out=o2[:], in0=oTh[:, tsl], scalar=gon[:, 0:1], in1=rn[:], op0=ALU.mult, op1=ALU.mult),
                            reads=['oTh', 'gon', rnk], writes=[o2k])
                        ob, obk = obp.next()
                        P.op('pool', lambda e, o2=o2, zs=zs, ob=ob: e.tensor_tensor(out=ob[:], in0=o2[:], in1=zs[:], op=ALU.mult),
                             reads=[o2k, zsk], writes=[obk])
                        P.dma('pool', lambda e, ob=ob, h=h, tsl=tsl: e.dma_start(out=oT[2, h * 128:(h + 1) * 128, tsl], in_=ob[:]), reads=[obk], key=obk)
                        yield
                    yield

                def drain(g_):
                    for _ in g_:
                        pass
                drain(genA(0))
                for h in range(GHEADS):
                    gb_ = genB(h)
                    ga_ = genA(h + 1) if h + 1 < GHEADS else None
                    while gb_ is not None or ga_ is not None:
                        if ga_ is not None:
                            for _ in range(2):
                                try:
                                    next(ga_)
                                except StopIteration:
                                    ga_ = None
                                    break
                        if gb_ is not None:
                            try:
                                next(gb_)
                            except StopIteration:
                                gb_ = None


        def phase_merge(l, actT):
            with Phase("merge") as ph:
                obh = [ph.sb(f"obh{b}", [128, 8, 1024], BF16) for b in range(3)]
                wts = ph.rot("wm", [128, 8, 256], BF16, 6)
                gtp = ph.rot("gt", [128, 512], BF16, 6)
                mp = ph.rot("mm", [128, 512], F32, 6)
                sp_ = ph.rot("ms", [128, 512], F32, 2)
                for half in range(2):
                    for b in range(3):
                        for kh in range(2):
                            P.dma('sp', lambda e, b=b, half=half, kh=kh: e.dma_start(
                                out=obh[b][:, kh * 4:(kh + 1) * 4, :],
                                in_=oT[b, kh * 512:(kh + 1) * 512, half * 1024:(half + 1) * 1024].rearrange("(kc p) t -> p kc t", p=128)),
                                writes=[('obh', b)], key=('ld', b))
                    for dg in range(8):
                        wl = []
                        for b in range(3):
                            wt, wk = wts.next()
                            load_w(wt, wk, w_branch[l, b], 8, dg * 256, 256)
                            wl.append((wt, wk))
                        for ct in range(2):
                            dchunk = dg * 2 + ct
                            for t2 in range(2):
                                tt = half * 2 + t2
                                ms = []
                                for b in range(3):
                                    pb = next_ps(0, 6)
                                    wt, wk = wl[b]
                                    for kc in range(8):
                                        P.op('pe', lambda e, kc=kc, ct=ct, t2=t2, pb=pb, wt=wt, b=b: e.matmul(
                                            ps[pb][:], lhsT=wt[:, kc, ct * 128:(ct + 1) * 128], rhs=obh[b][:, kc, t2 * 512:(t2 + 1) * 512],
                                            start=(kc == 0), stop=(kc == 7)), reads=[wk, ('obh', b)], writes=[('ps', pb)])
                                    gt, gk = gtp.next()
                                    r0 = b * 2048 + dchunk * 128
                                    P.dma('sp', lambda e, gt=gt, r0=r0, tt=tt: e.dma_start(
                                        out=gt[:], in_=gateT[r0:r0 + 128, tt * 512:(tt + 1) * 512]), writes=[gk], key=gk)
                                    m_, mk = mp.next()
                                    P.op('dve', lambda e, m_=m_, gt=gt, pb=pb: e.tensor_tensor(out=m_[:], in0=ps[pb][:], in1=gt[:], op=ALU.mult),
                                         reads=[('ps', pb), gk], writes=[mk])
                                    ms.append((m_, mk))
                                s_, sk = sp_.next()
                                P.op('dve', lambda e, s_=s_, ms=ms: e.tensor_tensor(out=s_[:], in0=ms[0][0][:], in1=ms[1][0][:], op=ALU.add),
                                     reads=[ms[0][1], ms[1][1]], writes=[sk])
                                P.op('dve', lambda e, s_=s_, ms=ms, dchunk=dchunk, tt=tt: e.tensor_tensor(
                                    out=actT[:, dchunk, tt * 512:(tt + 1) * 512], in0=s_[:], in1=ms[2][0][:], op=ALU.add),
                                    reads=[sk, ms[2][1]], writes=[('actT', tt)])

        def resid_epi(ph, r0):
            xl = ph.rot("xl", [128, 512], F32, 4)
            xs = ph.rot("xs", [128, 512], F32, 4)

            def epi(pb, m, ct, tt):
                rr = r0 + ct * 128
                t, k = xl.next()
                P.dma('sp', lambda e: e.dma_start(out=t[:], in_=xT[rr:rr + 128, tt * 512:(tt + 1) * 512]), writes=[k], key=k)
                o, ok = xs.next()
                P.op('dve', lambda e: e.tensor_tensor(out=o[:], in0=ps[pb][:], in1=t[:], op=ALU.add), reads=[('ps', pb), k], writes=[ok])
                P.dma('sp', lambda e: e.dma_start(out=xT[rr:rr + 128, tt * 512:(tt + 1) * 512], in_=o[:]), reads=[ok], key=ok)
            return epi

        def phase_wout(l, actT):
            with Phase("wout") as ph:
                wts = ph.rot("w", [128, 16, 512], BF16, 3)
                slots = {}

                def issue(j):
                    wt, wk = wts.next()
                    slots[j] = (wt, wk)
                    load_w(wt, wk, w_out[l], 16, j * 512, 512)
                epis = {}
                xl = ph.rot("xl", [128, 512], F32, 4)
                xs = ph.rot("xs", [128, 512], F32, 4)

                def mk_epi(r0):
                    def epi(pb, m, ct, tt):
                        rr = r0 + ct * 128
                        t, k = xl.next()
                        P.dma('sp', lambda e: e.dma_start(out=t[:], in_=xT[rr:rr + 128, tt * 512:(tt + 1) * 512]), writes=[k], key=k)
                        o, ok = xs.next()
                        P.op('dve', lambda e: e.tensor_tensor(out=o[:], in0=ps[pb][:], in1=t[:], op=ALU.add), reads=[('ps', pb), k], writes=[ok])
                        P.dma('act', lambda e: e.dma_start(out=xT[rr:rr + 128, tt * 512:(tt + 1) * 512], in_=o[:]), reads=[ok], key=ok)
                    return epi
                issue(0)
                issue(1)
                for j in range(4):
                    wt, wk = slots[j]
                    gemm_fm(wt, wk, 16, 512, actT, 'actT', mk_epi(j * 512))
                    if j + 2 < 4:
                        issue(j + 2)

        def phase_ffn_up(l, actT):
            with Phase("ffnup") as ph:
                norm_fm(ph, xT, 16, mlp_norm[l], actT, 'actT', D)
                wts = ph.rot("w", [128, 16, 512], BF16, 3)
                rp = ph.rot("r", [128, 512], F32, 3)
                hp = ph.rot("hb", [128, 512], BF16, 3)
                slots = {}

                def issue(j):
                    wt, wk = wts.next()
                    slots[j] = (wt, wk)
                    load_w(wt, wk, w_up[l], 16, j * 512, 512)

                def mk_epi(r0):
                    def epi(pb, m, ct, tt):
                        r, rk = rp.next()
                        P.op('act', lambda e: e.activation(out=r[:], in_=ps[pb][:], func=AF.Relu), reads=[('ps', pb)], writes=[rk])
                        hb, hk = hp.next()
                        P.op('pool', lambda e: e.tensor_tensor(out=hb[:], in0=r[:], in1=r[:], op=ALU.mult), reads=[rk], writes=[hk])
                        kc_ = (r0 + ct * 128) // 128
                        P.dma('sp', lambda e: e.dma_start(out=hidT[tt, :, kc_, :], in_=hb[:]), reads=[hk], key=hk)
                    return epi
                issue(0)
                issue(1)
                for j in range(16):
                    wt, wk = slots[j]
                    gemm_fm(wt, wk, 16, 512, actT, 'actT', mk_epi(j * 512))
                    if j + 2 < 16:
                        issue(j + 2)

        def phase_ffn_down(l):
            with Phase("ffndn") as ph:
                wd = [ph.sb(f"wd{i}", [128, 64, 512], BF16) for i in range(2)]
                hp = ph.rot("hid", [128, 8, 512], BF16, 3)
                xl = ph.rot("xl", [128, 512], F32, 4)
                xs = ph.rot("xs", [128, 512], F32, 4)

                def issue(eg):
                    load_w(wd[eg % 2], ('wd', eg % 2), w_down[l], 64, eg * 512, 512)
                issue(0)
                it = 0
                for eg in range(4):
                    if eg + 1 < 4:
                        issue(eg + 1)
                    wt, wk = wd[eg % 2], ('wd', eg % 2)
                    for tt in range(4):
                        base = 4 * (it % 2)
                        it += 1
                        for kcg in range(8):
                            ht, hk = hp.next()
                            P.dma('sp', lambda e, ht=ht, kcg=kcg, tt=tt: e.dma_start(
                                out=ht[:], in_=hidT[tt, :, kcg * 8:(kcg + 1) * 8, :]),
                                writes=[hk], key=hk)
                            for j in range(8):
                                kc = kcg * 8 + j
                                for ct in range(4):
                                    P.op('pe', lambda e, kc=kc, j=j, ct=ct, ht=ht, wt=wt, base=base: e.matmul(
                                        ps[base + ct][:], lhsT=wt[:, kc, ct * 128:(ct + 1) * 128], rhs=ht[:, j, :],
                                        start=(kc == 0), stop=(kc == 63)), reads=[wk, hk], writes=[('ps', base + ct)])
                        for ct in range(4):
                            pb = base + ct
                            rr = eg * 512 + ct * 128
                            t, k = xl.next()
                            P.dma('sp', lambda e, t=t, rr=rr, tt=tt: e.dma_start(out=t[:], in_=xT[rr:rr + 128, tt * 512:(tt + 1) * 512]), writes=[k], key=k)
                            o, ok = xs.next()
                            P.op('dve', lambda e, o=o, t=t, pb=pb: e.tensor_tensor(out=o[:], in0=ps[pb][:], in1=t[:], op=ALU.add),
                                 reads=[('ps', pb), k], writes=[ok])
                            P.dma('act', lambda e, o=o, rr=rr, tt=tt: e.dma_start(out=xT[rr:rr + 128, tt * 512:(tt + 1) * 512], in_=o[:]), reads=[ok], key=ok)

        def phase_final():
            with Phase("final") as ph:
                norm_fm(ph, xT, 16, final_norm, None, None, D, tag="nf", final_out=out_d)

        def with_act(fn):
            with ExitStack() as aes:
                actT = aes.enter_context(nc.sbuf_tensor(f"actT_{state['phase']}", [128, 16, S], BF16))
                fn(actT)

        if only == 'gdn':
            phase_gdn(0)
            nl = 0
            stop = 'x'
        else:
            phase_transpose_in()
        for l in range(nl):
            with_act(lambda actT: phase_inproj(l, actT))
            if stop == 'inproj':
                break
            phase_fox_prep(l)
            phase_attn(l)
            if stop == 'mla':
                break
            phase_gdn_prep(l)
            if stop == 'gprep':
                break
            phase_gdn(l)
            if stop == 'gdn':
                break

            def mo(actT):
                phase_merge(l, actT)
                phase_wout(l, actT)
            with_act(mo)
            if stop == 'wout':
                break
            with_act(lambda actT: phase_ffn_up(l, actT))
            phase_ffn_down(l)
        if stop is None:
            phase_final()
        print("total ops", P.nops, "sems", P.nsem, "cnt", P.cnt)
    return nc


def host_inputs(inputs):
    f = lambda a: np.ascontiguousarray(np.asarray(a, dtype=np.float32))
    m = {}
    m["attn_norm"] = f(np.asarray(inputs["attn_norm"]).reshape(NL, 16, 128).transpose(0, 2, 1))
    m["w_in"] = f(inputs["w_in"])
    m["fox_fgate_bias"] = f(np.asarray(inputs["fox_fgate_bias"]).reshape(NL, 8, 1))
    m["mla_q_norm"] = f(np.asarray(inputs["mla_q_norm"]).reshape(NL, 4, 128).transpose(0, 2, 1))
    m["mla_kv_norm"] = f(np.asarray(inputs["mla_kv_norm"]).reshape(NL, 4, 128).transpose(0, 2, 1))
    m["w_mla_uq"] = f(np.asarray(inputs["w_mla_uq"]).reshape(NL, 512, 1536))
    m["w_mla_ukv"] = f(np.asarray(inputs["w_mla_ukv"]).reshape(NL, 512, 2048))
    m["gdn_conv"] = f(np.asarray(inputs["gdn_conv"]).transpose(0, 2, 1))
    m["gdn_a_log"] = f(np.asarray(inputs["gdn_a_log"]).reshape(NL, 8, 1))
    m["gdn_dt_bias"] = f(np.asarray(inputs["gdn_dt_bias"]).reshape(NL, 8, 1))
    m["gdn_out_norm"] = f(np.asarray(inputs["gdn_out_norm"]).reshape(NL, 128, 1))
    m["w_branch"] = f(inputs["w_branch"])
    m["w_out"] = f(inputs["w_out"])
    m["mlp_norm"] = f(np.asarray(inputs["mlp_norm"]).reshape(NL, 16, 128).transpose(0, 2, 1))
    m["w_up"] = f(inputs["w_up"])
    m["w_down"] = f(inputs["w_down"])
    m["final_norm"] = f(np.asarray(inputs["final_norm"]).reshape(16, 128).T)
    m["c_ident"] = np.eye(128, dtype=np.float32)
    s_idx = np.arange(128)[None, :, None] + 128 * np.arange(4)[:, None, None]
    t_idx = np.arange(512)[None, None, :]
    m["c_mask"] = np.where(s_idx > t_idx, -30000.0, 0.0).astype(np.float32)
    inv_freq = (np.float32(10000.0) ** (-np.arange(0, 64, 2, dtype=np.float32) / np.float32(64))).astype(np.float32)
    ang = (np.arange(S, dtype=np.float32)[None, :] * inv_freq[:, None]).astype(np.float32)
    cos, sin = np.cos(ang).astype(np.float32), np.sin(ang).astype(np.float32)
    m["c_rope"] = np.stack([np.concatenate([cos, cos], 0), np.concatenate([sin, sin], 0)]).astype(np.float32)
    rot = np.zeros((64, 64), np.float32)
    for i in range(32):
        rot[i + 32, i] = -1.0
        rot[i, i + 32] = 1.0
    m["c_rot"] = rot
    blk = np.ones((8, S), np.float32)
    blk[:, ::128] = 0.0
    m["c_blk"] = blk
    a = np.arange(128)[:, None]
    b = np.arange(128)[None, :]
    lv = []
    for sz in (1, 2, 4, 8, 16, 32, 64):
        ms = ((a // (2 * sz) == b // (2 * sz)) & (a % (2 * sz) >= sz) & (b % (2 * sz) < sz)).astype(np.float32)
        lv.append(np.stack([ms, ms.T]))
    m["c_lvl"] = np.ascontiguousarray(np.stack(lv))
    m["c_tri"] = np.stack([np.where(b >= a, 0.0, -30000.0), np.where(b < a, 0.0, 30000.0)]).astype(np.float32)
    return m


_CACHE = {}


def kernel(**inputs):
    shared = host_inputs(inputs)
    x = np.asarray(inputs["x"], dtype=np.float32)
    if 'nc' not in _CACHE:
        _CACHE['nc'] = build()
    nc = _CACHE['nc']
    in_maps = []
    for b in range(8):
        mm = dict(shared)
        mm["x"] = np.ascontiguousarray(x[b])
        in_maps.append(mm)
    res = run_bass_kernel_spmd(nc, in_maps, core_ids=list(range(8)))
    return np.stack([np.asarray(r["out"], dtype=np.float32) for r in res.results], axis=0)
```

```python
import math
import os
from contextlib import ExitStack

import numpy as np
import ml_dtypes
import concourse.bass as bass
import concourse.mybir as mybir
from concourse.bass_utils import run_bass_kernel_spmd

F32 = mybir.dt.float32
BF16 = mybir.dt.bfloat16
AF = mybir.ActivationFunctionType
ALU = mybir.AluOpType

ENGS = ['pe', 'act', 'dve', 'pool', 'sp']
GSTOP = int(os.environ.get('GSTOP', '0'))
GSUB = int(os.environ.get('GSUB', '9'))
GHEADS = int(os.environ.get('GHEADS', '8'))
EPOCH = 10 ** 9
SAME_ENGINE_SYNC = True

S = 2048
D = 2048
NL = 4
DIN = 14424
DFF = 8192
EPS = 1e-6
C_FQ, C_FK, C_FV, C_FF = 0, 1024, 2048, 3072
C_CQ, C_CKV, C_KPE = 3080, 3592, 4104
C_GQKV, C_GZ, C_GB, C_GA, C_GATE = 4168, 7240, 8264, 8272, 8280


class Prog:
    def __init__(self, nc, es):
        self.nc = nc
        self.es = es
        self.ops = {e: [] for e in ENGS}
        self.cnt = {e: 0 for e in ENGS}
        self.esems = {e: [] for e in ENGS}
        self.known = {e: {} for e in ENGS}
        self.last_w = {}
        self.readers = {}
        self.dsem = {}
        self.sem_owner = {}
        self.nsem = 0
        self.nops = 0
        self.retired = []
        self.pool = []
        self.phase_keys = {}

    def _newsem(self, name, owner=None):
        s = self.es.enter_context(self.nc.semaphore(f"{name}_{self.nsem}"))
        self.nsem += 1
        self.sem_owner[id(s)] = owner
        return s

    def _tick(self, e):
        k = self.cnt[e]
        self.cnt[e] += 1
        ep = k // EPOCH
        while len(self.esems[e]) <= ep:
            self.esems[e].append(self._newsem(f"s_{e}", e))
        return (self.esems[e][ep], k % EPOCH + 1)

    def _filter(self, e, need):
        waits = []
        for sid, (s, v) in need.items():
            if self.sem_owner.get(sid) == e and (e == 'pe' or not SAME_ENGINE_SYNC):
                continue
            if self.known[e].get(sid, 0) >= v:
                continue
            self.known[e][sid] = v
            waits.append((s, v))
        return waits

    def _deps(self, e, reads, writes):
        need = {}

        def add(ev):
            s, v = ev
            if v > need.get(id(s), (None, 0))[1]:
                need[id(s)] = (s, v)
        for r in reads:
            if r in self.last_w:
                add(self.last_w[r])
        for w in writes:
            if w in self.last_w:
                add(self.last_w[w])
            for ev in self.readers.get(w, {}).values():
                add(ev)
        return self._filter(e, need)

    def _commit(self, ev, reads, writes):
        s, v = ev
        for r in reads:
            d = self.readers.setdefault(r, {})
            if v > d.get(id(s), (None, 0))[1]:
                d[id(s)] = ev
        for w in writes:
            self.last_w[w] = ev
            self.readers[w] = {}

    def op(self, e, fn, reads=(), writes=()):
        writes = list(writes) + [r for r in reads if isinstance(r, tuple) and r[0] == 'ps' and r not in writes]
        waits = self._deps(e, reads, writes)
        ev = self._tick(e)
        self.ops[e].append((fn, waits, ev, 1))
        self._commit(ev, reads, writes)
        self.nops += 1

    def dma(self, q, fn, reads=(), writes=(), key=None):
        waits = self._deps(q, reads, writes)
        if key not in self.phase_keys:
            self.phase_keys[key] = len(self.phase_keys)
        idx = self.phase_keys[key]
        while len(self.pool) <= idx:
            self.pool.append([self._newsem("d"), 0])
        ds = self.pool[idx]
        ds[1] += 16
        ev = (ds[0], ds[1])
        self.ops[q].append((fn, waits, ev, 16))
        self._commit(ev, reads, writes)
        self.nops += 1

    def barrier(self):
        evs = {}
        for e in ENGS:
            if self.cnt[e] > 0:
                k = self.cnt[e] - 1
                s = self.esems[e][k // EPOCH]
                evs[id(s)] = (s, k % EPOCH + 1)
        for (s, c) in self.pool:
            if c > 0:
                evs[id(s)] = (s, c)
        self.phase_keys = {}
        for e in ENGS:
            waits = self._filter(e, dict(evs))
            if waits:
                self.ops[e].append((None, waits, None, 0))
        self.last_w = {}
        self.readers = {}

    def simcheck(self):
        if not hasattr(self, 'simsem'):
            self.simsem = {}
        ptr = {e: 0 for e in ENGS}
        prog = True
        while prog:
            prog = False
            for e in ENGS:
                while ptr[e] < len(self.ops[e]):
                    fn, waits, ev, amt = self.ops[e][ptr[e]]
                    if all(self.simsem.get(id(s_), 0) >= v for s_, v in waits):
                        if ev is not None:
                            self.simsem[id(ev[0])] = self.simsem.get(id(ev[0]), 0) + amt
                            assert self.simsem[id(ev[0])] == ev[1], (e, ptr[e], self.simsem[id(ev[0])], ev[1])
                        ptr[e] += 1
                        prog = True
                    else:
                        break
        for e in ENGS:
            assert ptr[e] == len(self.ops[e]), f"DEADLOCK on {e} at {ptr[e]}/{len(self.ops[e])}: {self.ops[e][ptr[e]][1]}"

    def emit(self):
        nc = self.nc
        self.simcheck()
        with nc.Block() as block:
            def mk(e):
                def body(eng):
                    for (fn, waits, ev, amt) in self.ops[e]:
                        for (s, v) in waits:
                            eng.wait_ge(s, v)
                        if fn is None:
                            continue
                        ins = fn(eng)
                        ins.then_inc(ev[0], amt)
                return body
            block.tensor(mk('pe'))
            block.scalar(mk('act'))
            block.vector(mk('dve'))
            block.gpsimd(mk('pool'))
            block.sync(mk('sp'))
        self.ops = {e: [] for e in ENGS}


class Rot:
    def __init__(self, tiles, name):
        self.tiles = tiles
        self.name = name
        self.i = 0

    def next(self):
        j = self.i % len(self.tiles)
        self.i += 1
        return self.tiles[j], (self.name, j)


def build(nl=NL, dbg=(), stop=None, only=None, ext_in=()):
    nc = bass.Bass("TRN2", target_bir_lowering=False)
    din = lambda name, shape, dt=F32: nc.dram_tensor(name, list(shape), dt, kind="ExternalInput").ap()
    x_in = din("x", [S, D])
    attn_norm = din("attn_norm", [NL, 128, 16])
    w_in = din("w_in", [NL, D, DIN])
    fbias = din("fox_fgate_bias", [NL, 8, 1])
    qnorm = din("mla_q_norm", [NL, 128, 4])
    kvnorm = din("mla_kv_norm", [NL, 128, 4])
    w_uq = din("w_mla_uq", [NL, 512, 1536])
    w_ukv = din("w_mla_ukv", [NL, 512, 2048])
    gconv = din("gdn_conv", [NL, 3072, 4])
    galog = din("gdn_a_log", [NL, 8, 1])
    gdtb = din("gdn_dt_bias", [NL, 8, 1])
    gonorm = din("gdn_out_norm", [NL, 128, 1])
    w_branch = din("w_branch", [NL, 3, 1024, D])
    w_out = din("w_out", [NL, D, D])
    mlp_norm = din("mlp_norm", [NL, 128, 16])
    w_up = din("w_up", [NL, D, DFF])
    w_down = din("w_down", [NL, DFF, D])
    final_norm = din("final_norm", [128, 16])
    c_ident = din("c_ident", [128, 128])
    c_mask = din("c_mask", [4, 128, 512])
    c_rope = din("c_rope", [2, 64, S])
    c_rot = din("c_rot", [64, 64])
    c_blk = din("c_blk", [8, S])
    c_lvl = din("c_lvl", [7, 2, 128, 128])
    c_tri = din("c_tri", [2, 128, 128])
    out_d = nc.dram_tensor("out", [S, D], F32, kind="ExternalOutput").ap()

    def scratch(name, shape, dt):
        kind = "ExternalOutput" if name in dbg else ("ExternalInput" if name in ext_in else "Internal")
        return nc.dram_tensor(name, list(shape), dt, kind=kind).ap()
    xT = scratch("xT", [D, S], F32)
    fqT = scratch("fqT", [1024, S], BF16)
    fkT = scratch("fkT", [1024, S], BF16)
    fv = scratch("fv", [S, 1024], BF16)
    smallT = scratch("smallT", [96, S], F32)
    kpeT = scratch("kpeT", [64, S], F32)
    cqT = scratch("cqT", [512, S], F32)
    ckvT = scratch("ckvT", [512, S], F32)
    gqkvT = scratch("gqkvT", [3072, S], F32)
    gzsT = scratch("gzsT", [1024, S], BF16)
    gateT = scratch("gateT", [6144, S], BF16)
    negc = scratch("negc", [8, S], F32)
    crow = scratch("crow", [3, 8, S], BF16)
    oT = scratch("oT", [3, 1024, S], BF16)
    hidT = scratch("hidT", [4, 128, 64, 512], BF16)
    gsc = scratch("gsc", [6, 8, S], F32)
    gcd = scratch("gcd", [8, 16], F32)

    es = ExitStack()
    with es:
        P = Prog(nc, es)
        es.enter_context(nc.allow_non_contiguous_dma(reason="small strided parameter loads"))
        es.enter_context(nc.allow_low_precision(reason="bf16 matmul operands, fp32 accumulation"))
        ps = [es.enter_context(nc.psum_tensor(f"ps{i}", [128, 512], F32)) for i in range(8)]
        gsb = lambda name, shape, dt: es.enter_context(nc.sbuf_tensor(name, list(shape), dt))
        identf = gsb("identf", [128, 128], F32)
        identb = gsb("identb", [128, 128], BF16)
        onesf = gsb("onesf", [128, 128], F32)
        onesb = gsb("onesb", [128, 128], BF16)
        epst = gsb("epst", [128, 1], F32)
        maskb = gsb("maskb", [128, 4, 512], BF16)

        state = {'phase': 0, 'done': False}

        class Phase:
            def __init__(self, name):
                self.name = name

            def __enter__(self):
                self.pes = ExitStack()
                self.pes.__enter__()
                self.cache = {}
                return self

            def sb(self, name, shape, dt):
                if name not in self.cache:
                    self.cache[name] = self.pes.enter_context(
                        nc.sbuf_tensor(f"{self.name}_{name}_{state['phase']}", list(shape), dt))
                return self.cache[name]

            def rot(self, name, shape, dt, n):
                if ('rot', name) not in self.cache:
                    self.cache[('rot', name)] = Rot([self.sb(f"{name}{i}", shape, dt) for i in range(n)], name)
                return self.cache[('rot', name)]

            def __exit__(self, *a):
                P.barrier()
                P.emit()
                self.pes.__exit__(None, None, None)
                state['phase'] += 1
                return False

        with Phase("init") as ph:
            mstage = ph.sb("mstage", [128, 4, 512], F32)
            P.dma('sp', lambda e: e.dma_start(out=identf[:], in_=c_ident), writes=['identf'], key=('ld', 0))
            P.dma('sp', lambda e: e.dma_start(out=mstage[:], in_=c_mask.rearrange("j p t -> p j t")),
                  writes=['mstage'], key=('ld', 1))
            P.op('dve', lambda e: e.tensor_copy(out=identb[:], in_=identf[:]), reads=['identf'], writes=['identb'])
            P.op('dve', lambda e: e.memset(onesf[:], 1.0), writes=['onesf'])
            P.op('dve', lambda e: e.memset(onesb[:], 1.0), writes=['onesb'])
            P.op('dve', lambda e: e.memset(epst[:], EPS), writes=['epst'])
            P.op('dve', lambda e: e.tensor_copy(out=maskb[:], in_=mstage[:]), reads=['mstage'], writes=['maskb'])

        psrot = {'i': 0}

        def next_ps(lo=0, n=4):
            j = lo + psrot['i'] % n
            psrot['i'] += 1
            return j

        def phase_transpose_in():
            with Phase("tin") as ph:
                xt = ph.rot("xt", [128, D], F32, 2)
                st = ph.rot("st", [128, 4, 128], F32, 4)
                for tb in range(16):
                    xtile, xk = xt.next()
                    P.dma('sp', lambda e, tb=tb, xtile=xtile: e.dma_start(out=xtile[:], in_=x_in[tb * 128:(tb + 1) * 128, :]),
                          writes=[xk], key=xk)
                    for kg in range(4):
                        pb = next_ps()
                        for j in range(4):
                            kc = kg * 4 + j
                            P.op('pe', lambda e, kc=kc, j=j, pb=pb, xtile=xtile: e.transpose(
                                out=ps[pb][:, j * 128:(j + 1) * 128], in_=xtile[:, kc * 128:(kc + 1) * 128],
                                identity=identf[:]), reads=[xk], writes=[('ps', pb)])
                        stile, sk = st.next()
                        eng = 'act' if kg % 2 == 0 else 'dve'
                        if eng == 'act':
                            P.op('act', lambda e, pb=pb, stile=stile: e.copy(
                                out=stile[:], in_=ps[pb][:].rearrange("p (a b) -> p a b", a=4)),
                                reads=[('ps', pb)], writes=[sk])
                        else:
                            P.op('dve', lambda e, pb=pb, stile=stile: e.tensor_copy(
                                out=stile[:], in_=ps[pb][:].rearrange("p (a b) -> p a b", a=4)),
                                reads=[('ps', pb)], writes=[sk])
                        P.dma('sp', lambda e, kg=kg, tb=tb, stile=stile: e.dma_start(
                            out=xT[kg * 512:(kg + 1) * 512, tb * 128:(tb + 1) * 128].rearrange("(a p) t -> p a t", p=128),
                            in_=stile[:]), reads=[sk], key=sk)

        def norm_fm(ph, srcT, nk, gain_ap, dst, dkey, dn, tag="n", final_out=None):
            ld = ph.rot(tag + "ld", [128, 512], F32, 4)
            sq = ph.rot(tag + "sq", [128, 512], F32, 2)
            rstd = ph.sb(tag + "rstd", [128, S], F32)
            gt = ph.sb(tag + "gain", [128, nk], F32)
            gk = tag + 'gain'
            P.dma('sp', lambda e: e.dma_start(out=gt[:], in_=gain_ap), writes=[gk], key=('ld', 9))
            for tt in range(4):
                pb = 4 + tt % 2
                for kc in range(nk):
                    t, k = ld.next()
                    P.dma('sp', lambda e, t=t, kc=kc, tt=tt: e.dma_start(
                        out=t[:], in_=srcT[kc * 128:(kc + 1) * 128, tt * 512:(tt + 1) * 512]), writes=[k], key=k)
                    q, qk = sq.next()
                    P.op('act', lambda e, t=t, q=q: e.activation(out=q[:], in_=t[:], func=AF.Square),
                         reads=[k], writes=[qk])
                    P.op('pe', lambda e, q=q, pb=pb, kc=kc: e.matmul(ps[pb][:], lhsT=onesf[:], rhs=q[:],
                                                                      start=(kc == 0), stop=(kc == nk - 1)),
                         reads=[qk], writes=[('ps', pb)])
                P.op('act', lambda e, pb=pb, tt=tt: e.activation(out=rstd[:, tt * 512:(tt + 1) * 512], in_=ps[pb][:],
                                                                 func=AF.Sqrt, bias=epst[:, 0:1], scale=1.0 / dn),
                     reads=[('ps', pb)], writes=[(tag + 'rstd', tt)])
                P.op('dve', lambda e, tt=tt: e.reciprocal(out=rstd[:, tt * 512:(tt + 1) * 512],
                                                          in_=rstd[:, tt * 512:(tt + 1) * 512]),
                     reads=[(tag + 'rstd', tt)], writes=[(tag + 'rstd', tt)])
            if final_out is not None:
                yt = ph.rot(tag + "y", [128, 512], F32, 2)
                ost = ph.rot(tag + "ost", [128, 4, 128], F32, 3)
            for tt in range(4):
                for kc in range(nk):
                    t, k = ld.next()
                    P.dma('sp', lambda e, t=t, kc=kc, tt=tt: e.dma_start(
                        out=t[:], in_=srcT[kc * 128:(kc + 1) * 128, tt * 512:(tt + 1) * 512]), writes=[k], key=k)
                    if final_out is None:
                        P.op('dve', lambda e, t=t, kc=kc, tt=tt: e.scalar_tensor_tensor(
                            out=dst[:, kc, tt * 512:(tt + 1) * 512], in0=t[:], scalar=gt[:, kc:kc + 1],
                            in1=rstd[:, tt * 512:(tt + 1) * 512], op0=ALU.mult, op1=ALU.mult),
                            reads=[k, gk, (tag + 'rstd', tt)], writes=[(dkey, tt)])
                    else:
                        y, yk = yt.next()
                        P.op('dve', lambda e, t=t, kc=kc, tt=tt, y=y: e.scalar_tensor_tensor(
                            out=y[:], in0=t[:], scalar=gt[:, kc:kc + 1],
                            in1=rstd[:, tt * 512:(tt + 1) * 512], op0=ALU.mult, op1=ALU.mult),
                            reads=[k, gk, (tag + 'rstd', tt)], writes=[yk])
                        pb = next_ps()
                        for j in range(4):
                            P.op('pe', lambda e, y=y, j=j, pb=pb: e.transpose(
                                out=ps[pb][:, j * 128:(j + 1) * 128], in_=y[:, j * 128:(j + 1) * 128], identity=identf[:]),
                                reads=[yk], writes=[('ps', pb)])
                        o, ok = ost.next()
                        P.op('act', lambda e, o=o, pb=pb: e.copy(out=o[:], in_=ps[pb][:].rearrange("p (a b) -> p a b", a=4)),
                             reads=[('ps', pb)], writes=[ok])
                        P.dma('sp', lambda e, o=o, tt=tt, kc=kc: e.dma_start(
                            out=final_out[tt * 512:(tt + 1) * 512, kc * 128:(kc + 1) * 128].rearrange("(a p) d -> p a d", p=128),
                            in_=o[:]), reads=[ok], key=ok)

        def load_w(wt, wk, w2d, nk, c0, ncols, dcol=0):
            step = 8
            for k0 in range(0, nk, step):
                k1 = min(nk, k0 + step)
                P.dma('pool', lambda e, k0=k0, k1=k1: e.dma_start(
                    out=wt[:, k0:k1, dcol:dcol + ncols],
                    in_=w2d[k0 * 128:k1 * 128, c0:c0 + ncols].rearrange("(kc p) c -> p kc c", p=128)),
                    writes=[wk], key=wk)

        def gemm_fm(wt, wk, nk, ncols, act, akey, epi):
            for ct in range((ncols + 127) // 128):
                m = min(128, ncols - ct * 128)
                for tt in range(4):
                    pb = next_ps()
                    for kc in range(nk):
                        P.op('pe', lambda e, kc=kc, ct=ct, tt=tt, pb=pb, m=m: e.matmul(
                            ps[pb][0:m, :], lhsT=wt[:, kc, ct * 128:ct * 128 + m],
                            rhs=act[:, kc, tt * 512:(tt + 1) * 512], start=(kc == 0), stop=(kc == nk - 1)),
                            reads=[wk, (akey, tt)], writes=[('ps', pb)])
                    epi(pb, m, ct, tt)

        def gemm_tm(wt, wk, nk, ncols, act, akey, epi):
            for tb in range(16):
                pb = next_ps()
                for kc in range(nk):
                    P.op('pe', lambda e, kc=kc, tb=tb, pb=pb: e.matmul(
                        ps[pb][:, 0:ncols], lhsT=act[:, kc, tb * 128:(tb + 1) * 128], rhs=wt[:, kc, 0:ncols],
                        start=(kc == 0), stop=(kc == nk - 1)),
                        reads=[wk, (akey, tb // 4)], writes=[('ps', pb)])
                epi(pb, tb)

        evrot = {'i': 0}

        def evac(pb, m, dst_tile, dkey, func=None, ncols=512):
            if func is not None:
                P.op('act', lambda e: e.activation(out=dst_tile[0:m, 0:ncols], in_=ps[pb][0:m, 0:ncols], func=func),
                     reads=[('ps', pb)], writes=[dkey])
                return
            evrot['i'] += 1
            if evrot['i'] % 2 == 0:
                P.op('act', lambda e: e.copy(out=dst_tile[0:m, 0:ncols], in_=ps[pb][0:m, 0:ncols]),
                     reads=[('ps', pb)], writes=[dkey])
            else:
                P.op('dve', lambda e: e.tensor_copy(out=dst_tile[0:m, 0:ncols], in_=ps[pb][0:m, 0:ncols]),
                     reads=[('ps', pb)], writes=[dkey])

        def phase_inproj(l, actT):
            with Phase("inproj") as ph:
                norm_fm(ph, xT, 16, attn_norm[l], actT, 'actT', D)
                wts = ph.rot("w", [128, 16, 512], BF16, 3)
                stb = ph.rot("stb", [128, 512], BF16, 4)
                stf = ph.rot("stf", [128, 512], F32, 4)
                W = w_in[l]
                jobs = []

                def fm_store(dstT, r0, dt, func=None):
                    def epi_factory(ncols):
                        def epi(pb, m, ct, tt):
                            t, k = (stb if dt == BF16 else stf).next()
                            evac(pb, m, t, k, func)
                            P.dma('sp', lambda e: e.dma_start(
                                out=dstT[r0 + ct * 128:r0 + ct * 128 + m, tt * 512:(tt + 1) * 512], in_=t[0:m, :]),
                                reads=[k], key=k)
                        return epi
                    return epi_factory

                def add_fm(c0, ncols, dstT, r0, dt, func=None):
                    for j in range(0, ncols, 512):
                        n = min(512, ncols - j)
                        jobs.append(('fm', c0 + j, n, fm_store(dstT, r0 + j, dt, func)(n)))
                add_fm(C_FQ, 1024, fqT, 0, BF16)
                add_fm(C_FK, 1024, fkT, 0, BF16)
                for j in range(2):
                    def epi_tm(pb, tb, j=j):
                        t, k = stb.next()
                        evac(pb, 128, t, k)
                        P.dma('sp', lambda e: e.dma_start(
                            out=fv[tb * 128:(tb + 1) * 128, j * 512:(j + 1) * 512], in_=t[:]), reads=[k], key=k)
                    jobs.append(('tm', C_FV + j * 512, 512, epi_tm))
                jobs.append(('small', None, 96, fm_store(smallT, 0, F32)(96)))
                add_fm(C_KPE, 64, kpeT, 0, F32)
                add_fm(C_CQ, 512, cqT, 0, F32)
                add_fm(C_CKV, 512, ckvT, 0, F32)
                add_fm(C_GQKV, 3072, gqkvT, 0, F32)
                add_fm(C_GZ, 1024, gzsT, 0, BF16, AF.Silu)
                add_fm(C_GATE, 6144, gateT, 0, BF16, AF.Sigmoid)

                slots = {}

                def issue_load(ji):
                    kind, c0, n, epi = jobs[ji]
                    wt, wk = wts.next()
                    slots[ji] = (wt, wk)
                    if kind == 'small':
                        load_w(wt, wk, W, 16, C_FF, 8, 0)
                        load_w(wt, wk, W, 16, C_GB, 8, 32)
                        load_w(wt, wk, W, 16, C_GA, 8, 64)
                    else:
                        load_w(wt, wk, W, 16, c0, n)
                issue_load(0)
                issue_load(1)
                for ji in range(len(jobs)):
                    kind, c0, n, epi = jobs[ji]
                    wt, wk = slots[ji]
                    if kind == 'tm':
                        gemm_tm(wt, wk, 16, n, actT, 'actT', epi)
                    else:
                        gemm_fm(wt, wk, 16, n, actT, 'actT', epi)
                    if ji + 2 < len(jobs):
                        issue_load(ji + 2)

        def attention(ph, pools, h, branch, kT, kk, qT, qk, ek, ekk, eq, eqk, KX, v, vk, bias_t, bk, scale, sbanks, obanks):
            pT_pool, rl_pool, ob_pool = pools
            srot = {'i': 0}
            pending = [None]
            for tt in range(4):
                nsb = 4 * (tt + 1)
                pO, pL = obanks
                for sb in range(nsb):
                    pb = sbanks[srot['i'] % len(sbanks)]
                    srot['i'] += 1
                    diag = sb >= 4 * tt
                    P.op('pe', lambda e, sb=sb, tt=tt, pb=pb: e.matmul(
                        ps[pb][:], lhsT=kT[:, sb * 128:(sb + 1) * 128], rhs=qT[:, tt * 512:(tt + 1) * 512],
                        start=True, stop=False), reads=[kk, qk], writes=[('ps', pb)])
                    if ek is None:
                        lhs_fn = lambda sb: onesb[0:KX, 0:128]
                    else:
                        lhs_fn = lambda sb: ek[0:KX, sb * 128:(sb + 1) * 128]
                    P.op('pe', lambda e, sb=sb, tt=tt, pb=pb, diag=diag, lhs_fn=lhs_fn: e.matmul(
                        ps[pb][:], lhsT=lhs_fn(sb), rhs=eq[0:KX, tt * 512:(tt + 1) * 512],
                        start=False, stop=(not diag)), reads=[ekk, eqk], writes=[('ps', pb)])
                    if diag:
                        P.op('pe', lambda e, sb=sb, tt=tt, pb=pb: e.matmul(
                            ps[pb][:], lhsT=identb[:], rhs=maskb[:, sb - 4 * tt, :], start=False, stop=True),
                            reads=['maskb'], writes=[('ps', pb)])
                    pt, ptk = pT_pool.next()
                    if bias_t is not None:
                        P.op('act', lambda e, pb=pb, pt=pt, sb=sb: e.activation(
                            out=pt[:], in_=ps[pb][:], func=AF.Exp, bias=bias_t[:, sb:sb + 1], scale=scale),
                            reads=[('ps', pb), bk], writes=[ptk])
                    else:
                        P.op('act', lambda e, pb=pb, pt=pt: e.activation(
                            out=pt[:], in_=ps[pb][:], func=AF.Exp, scale=scale),
                            reads=[('ps', pb)], writes=[ptk])
                    def pv(sb=sb, pt=pt, ptk=ptk, pO=pO, pL=pL, nsb=nsb):
                        P.op('pe', lambda e: e.matmul(
                            ps[pO][:], lhsT=v[:, sb, :], rhs=pt[:], start=(sb == 0), stop=(sb == nsb - 1)),
                            reads=[vk, ptk], writes=[('ps', pO)])
                        P.op('pe', lambda e: e.matmul(
                            ps[pL][:], lhsT=onesb[:], rhs=pt[:], start=(sb == 0), stop=(sb == nsb - 1)),
                            reads=[ptk], writes=[('ps', pL)])
                    if pending[0] is not None:
                        pending[0]()
                    pending[0] = pv
                    yield
                if pending[0] is not None:
                    pending[0]()
                    pending[0] = None
                rl, rlk = rl_pool.next()
                P.op('dve', lambda e, rl=rl, pL=pL: e.reciprocal(out=rl[:], in_=ps[pL][:]),
                     reads=[('ps', pL)], writes=[rlk])
                ob, obk = ob_pool.next()
                P.op('dve', lambda e, rl=rl, ob=ob, pO=pO: e.tensor_tensor(out=ob[:], in0=ps[pO][:], in1=rl[:], op=ALU.mult),
                     reads=[('ps', pO), rlk], writes=[obk])
                P.dma('pool', lambda e, ob=ob, tt=tt: e.dma_start(
                    out=oT[branch, h * 128:(h + 1) * 128, tt * 512:(tt + 1) * 512], in_=ob[:]), reads=[obk], key=obk)
                yield

        def attn_pools(ph, tag=""):
            return (ph.rot(tag + "pT", [128, 512], BF16, 3), ph.rot(tag + "rl", [128, 512], F32, 2), ph.rot(tag + "ob", [128, 512], BF16, 2))

        def phase_fox_prep(l):
            with Phase("fprep") as ph:
                ff = ph.sb("ff", [8, S], F32)
                nb = ph.sb("nb", [8, 1], F32)
                one8 = ph.sb("one8", [8, 1], F32)
                ones8 = ph.sb("ones8", [8, S], F32)
                e1 = ph.sb("e1", [8, S], F32)
                l1 = ph.sb("l1", [8, S], F32)
                cum = ph.sb("cum", [8, S], F32)
                cs = ph.sb("cs", [8, S], F32)
                r1 = ph.sb("r1", [8, S], F32)
                hf = ph.sb("hf", [8, S], F32)
                parts = [ph.sb(f"part{i}", [8, S], BF16) for i in range(3)]
                P.dma('sp', lambda e: e.dma_start(out=ff[:], in_=smallT[0:8, :]), writes=['ff'], key=('ld', 0))
                P.dma('sp', lambda e: e.dma_start(out=nb[:], in_=fbias[l]), writes=['nb'], key=('ld', 1))
                P.op('dve', lambda e: e.tensor_scalar_mul(out=nb[:], in0=nb[:], scalar1=-1.0), reads=['nb'], writes=['nb'])
                P.op('dve', lambda e: e.memset(one8[:], 1.0), writes=['one8'])
                P.op('dve', lambda e: e.memset(ones8[:], 1.0), writes=['ones8'])
                P.op('act', lambda e: e.activation(out=e1[:], in_=ff[:], func=AF.Exp, bias=nb[:, 0:1], scale=-1.0),
                     reads=['ff', 'nb'], writes=['e1'])
                P.op('act', lambda e: e.activation(out=l1[:], in_=e1[:], func=AF.Ln, bias=one8[:, 0:1]),
                     reads=['e1', 'one8'], writes=['l1'])
                P.op('dve', lambda e: e.tensor_tensor_scan(out=cum[:], data0=ones8[:], data1=l1[:], initial=0.0,
                                                           op0=ALU.mult, op1=ALU.add),
                     reads=['ones8', 'l1'], writes=['cum'])
                P.dma('sp', lambda e: e.dma_start(out=negc, in_=cum[:]), reads=['cum'], key=('st', 0))
                P.op('dve', lambda e: e.tensor_scalar_mul(out=cs[:], in0=cum[:], scalar1=-math.sqrt(128.0)),
                     reads=['cum'], writes=['cs'])
                cur = cs
                curk = 'cs'
                for i in range(3):
                    P.op('dve', lambda e, i=i, cur=cur: e.tensor_copy(out=parts[i][:], in_=cur[:]),
                         reads=[curk], writes=[('part', i)])
                    P.dma('sp', lambda e, i=i: e.dma_start(out=crow[i], in_=parts[i][:]), reads=[('part', i)], key=('st', 1 + i))
                    if i < 2:
                        P.op('dve', lambda e, i=i: e.tensor_copy(out=hf[:], in_=parts[i][:]),
                             reads=[('part', i)], writes=['hf'])
                        P.op('dve', lambda e, cur=cur: e.tensor_tensor(out=r1[:], in0=cur[:], in1=hf[:], op=ALU.subtract),
                             reads=[curk, 'hf'], writes=['r1'])
                        cur = r1
                        curk = 'r1'

        def fox_head_loads(fx, h):
            qT, qk = fx['qTp'].next()
            kT, kk = fx['kTp'].next()
            v, vk = fx['vp'].next()
            nct, nck = fx['ncp'].next()
            cr, crk = fx['crp'].next()
            P.dma('sp', lambda e: e.dma_start(out=qT[:], in_=fqT[h * 128:(h + 1) * 128, :]), writes=[qk], key=qk)
            P.dma('sp', lambda e: e.dma_start(out=kT[:], in_=fkT[h * 128:(h + 1) * 128, :]), writes=[kk], key=kk)
            P.dma('sp', lambda e: e.dma_start(
                out=v[:], in_=fv[:, h * 128:(h + 1) * 128].rearrange("(sb p) c -> p sb c", p=128)), writes=[vk], key=vk)
            P.dma('sp', lambda e: e.dma_start(
                out=nct[:], in_=negc[h, :].rearrange("(sb p) -> p sb", p=128)), writes=[nck], key=nck)
            P.dma('sp', lambda e: e.dma_start(out=cr[:], in_=crow[:, h, :]), writes=[crk], key=crk)
            return (qT, qk, kT, kk, v, vk, nct, nck, cr, crk)

        def fox_setup(ph):
            return dict(pools=attn_pools(ph, "f"),
                        qTp=ph.rot("fqT", [128, S], BF16, 2), kTp=ph.rot("fkT", [128, S], BF16, 2),
                        vp=ph.rot("fv", [128, 16, 128], BF16, 2), ncp=ph.rot("fnc", [128, 16], F32, 2),
                        crp=ph.rot("fcr", [3, S], BF16, 2))

        def phase_attn(l):
            with Phase("attn") as ph:
                fx = fox_setup(ph)
                cqn = ph.sb("cqn", [128, 4, S], BF16)
                ckvn = ph.sb("ckvn", [128, 4, S], BF16)
                norm_fm(ph, cqT, 4, qnorm[l], cqn, 'cqn', 512, tag="nq")
                norm_fm(ph, ckvT, 4, kvnorm[l], ckvn, 'ckvn', 512, tag="nq")
                wuq = ph.sb("wuq", [128, 4, 1536], BF16)
                wukv = ph.sb("wukv", [128, 4, 2048], BF16)
                load_w(wuq, 'wuq', w_uq[l], 4, 0, 1536)
                load_w(wukv, 'wukv', w_ukv[l], 4, 0, 2048)
                cos2 = ph.sb("cos2", [64, S], F32)
                sin2 = ph.sb("sin2", [64, S], F32)
                rotf = ph.sb("rotf", [64, 64], F32)
                rotm = ph.sb("rotm", [64, 64], BF16)
                P.dma('sp', lambda e: e.dma_start(out=cos2[:], in_=c_rope[0]), writes=['cos2'], key=('ld', 0))
                P.dma('sp', lambda e: e.dma_start(out=sin2[:], in_=c_rope[1]), writes=['sin2'], key=('ld', 1))
                P.dma('sp', lambda e: e.dma_start(out=rotf[:], in_=c_rot), writes=['rotf'], key=('ld', 2))
                P.op('dve', lambda e: e.tensor_copy(out=rotm[:], in_=rotf[:]), reads=['rotf'], writes=['rotm'])
                xs_p = ph.rot("xs", [64, 512], F32, 2)
                xb_p = ph.rot("xb", [64, 512], BF16, 2)
                t1_p = ph.rot("t1", [64, 512], F32, 2)
                t2_p = ph.rot("t2", [64, 512], F32, 2)

                def rope_tile(xs, xsk, dst, dkey, tt):
                    xb, xbk = xb_p.next()
                    P.op('act', lambda e: e.copy(out=xb[:], in_=xs[:]), reads=[xsk], writes=[xbk])
                    pb = next_ps()
                    P.op('pe', lambda e: e.matmul(ps[pb][0:64, :], lhsT=rotm[:], rhs=xb[:], start=True, stop=True),
                         reads=['rotm', xbk], writes=[('ps', pb)])
                    t1, t1k = t1_p.next()
                    t2, t2k = t2_p.next()
                    P.op('dve', lambda e: e.tensor_tensor(out=t1[:], in0=xs[:], in1=cos2[:, tt * 512:(tt + 1) * 512], op=ALU.mult),
                         reads=[xsk, 'cos2'], writes=[t1k])
                    P.op('dve', lambda e: e.tensor_tensor(out=t2[:], in0=ps[pb][0:64, :], in1=sin2[:, tt * 512:(tt + 1) * 512], op=ALU.mult),
                         reads=[('ps', pb), 'sin2'], writes=[t2k])
                    P.op('dve', lambda e: e.tensor_tensor(out=dst[0:64, tt * 512:(tt + 1) * 512], in0=t1[:], in1=t2[:], op=ALU.add),
                         reads=[t1k, t2k], writes=[dkey])
                kper = ph.sb("kper", [64, S], BF16)
                for tt in range(4):
                    xs, xsk = xs_p.next()
                    P.dma('sp', lambda e, xs=xs, tt=tt: e.dma_start(out=xs[:], in_=kpeT[:, tt * 512:(tt + 1) * 512]), writes=[xsk], key=xsk)
                    rope_tile(xs, xsk, kper, 'kper', tt)
                pools = attn_pools(ph, "m")
                qnp = ph.rot("qn", [128, S], BF16, 2)
                qpp = ph.rot("qp", [64, S], BF16, 2)
                knp = ph.rot("kn", [128, S], BF16, 2)
                vp = ph.rot("v", [128, 16, 128], BF16, 2)
                for h in range(8):
                    fl = fox_head_loads(fx, h)
                    qn, qnk = qnp.next()
                    qp, qpk = qpp.next()
                    kn, knk = knp.next()
                    v, vk = vp.next()
                    for tt in range(4):
                        pb = next_ps()
                        for kc in range(4):
                            P.op('pe', lambda e, kc=kc, tt=tt, pb=pb, h=h: e.matmul(
                                ps[pb][:], lhsT=wuq[:, kc, h * 192:h * 192 + 128], rhs=cqn[:, kc, tt * 512:(tt + 1) * 512],
                                start=(kc == 0), stop=(kc == 3)), reads=['wuq', ('cqn', tt)], writes=[('ps', pb)])
                        evac(pb, 128, qn[:, tt * 512:(tt + 1) * 512], qnk)
                        pb = next_ps()
                        for kc in range(4):
                            P.op('pe', lambda e, kc=kc, tt=tt, pb=pb, h=h: e.matmul(
                                ps[pb][0:64, :], lhsT=wuq[:, kc, h * 192 + 128:h * 192 + 192], rhs=cqn[:, kc, tt * 512:(tt + 1) * 512],
                                start=(kc == 0), stop=(kc == 3)), reads=['wuq', ('cqn', tt)], writes=[('ps', pb)])
                        xs, xsk = xs_p.next()
                        P.op('act', lambda e, pb=pb, xs=xs: e.copy(out=xs[:], in_=ps[pb][0:64, :]), reads=[('ps', pb)], writes=[xsk])
                        rope_tile(xs, xsk, qp, qpk, tt)
                        pb = next_ps()
                        for kc in range(4):
                            P.op('pe', lambda e, kc=kc, tt=tt, pb=pb, h=h: e.matmul(
                                ps[pb][:], lhsT=wukv[:, kc, h * 256:h * 256 + 128], rhs=ckvn[:, kc, tt * 512:(tt + 1) * 512],
                                start=(kc == 0), stop=(kc == 3)), reads=['wukv', ('ckvn', tt)], writes=[('ps', pb)])
                        evac(pb, 128, kn[:, tt * 512:(tt + 1) * 512], knk)
                    for tg in range(4):
                        pb = next_ps()
                        for j in range(4):
                            tb = tg * 4 + j
                            for kc in range(4):
                                P.op('pe', lambda e, kc=kc, tb=tb, j=j, pb=pb, h=h: e.matmul(
                                    ps[pb][:, j * 128:(j + 1) * 128], lhsT=ckvn[:, kc, tb * 128:(tb + 1) * 128],
                                    rhs=wukv[:, kc, h * 256 + 128:h * 256 + 256], start=(kc == 0), stop=(kc == 3)),
                                    reads=['wukv', ('ckvn', tg)], writes=[('ps', pb)])
                        evac(pb, 128, v[:, tg * 4:(tg + 1) * 4, :].rearrange("p a b -> p (a b)"), vk)
                    (fqT_, fqk, fkT_, fkk, fv_, fvk, fnct, fnck, fcr, fcrk) = fl
                    gens = [attention(ph, fx['pools'], h, 0, fkT_, fkk, fqT_, fqk, None, 'onesb', fcr, fcrk, 3, fv_, fvk, fnct, fnck,
                                      128.0 ** -0.5, [0, 1], (4, 5)),
                            attention(ph, pools, h, 1, kn, knk, qn, qnk, kper, 'kper', qp, qpk, 64, v, vk, None, None,
                                      192.0 ** -0.5, [2, 3], (6, 7))]
                    while gens:
                        for g_ in list(gens):
                            try:
                                next(g_)
                            except StopIteration:
                                gens.remove(g_)
        def phase_gdn_prep(l):
            with Phase("gprep") as ph:
                t8 = lambda n: ph.sb(n, [8, S], F32)
                gbt, gat, blk, beta, e1, sp, g, gc, ngc, eg, bg, dl, edl = [t8(n) for n in
                    ("gbt", "gat", "blk", "beta", "e1", "sp", "g", "gc", "ngc", "eg", "bg", "dl", "edl")]
                dtb = ph.sb("dtb", [8, 1], F32)
                alog = ph.sb("alog", [8, 1], F32)
                negA = ph.sb("negA", [8, 1], F32)
                one8 = ph.sb("one8", [8, 1], F32)
                cdc = ph.sb("cdc", [8, 16], F32)
                P.dma('sp', lambda e: e.dma_start(out=gbt[:], in_=smallT[32:40, :]), writes=['gbt'], key=('ld', 0))
                P.dma('sp', lambda e: e.dma_start(out=gat[:], in_=smallT[64:72, :]), writes=['gat'], key=('ld', 1))
                P.dma('sp', lambda e: e.dma_start(out=blk[:], in_=c_blk), writes=['blk'], key=('ld', 2))
                P.dma('sp', lambda e: e.dma_start(out=dtb[:], in_=gdtb[l]), writes=['dtb'], key=('ld', 3))
                P.dma('sp', lambda e: e.dma_start(out=alog[:], in_=galog[l]), writes=['alog'], key=('ld', 4))
                P.op('dve', lambda e: e.memset(one8[:], 1.0), writes=['one8'])
                P.op('act', lambda e: e.activation(out=beta[:], in_=gbt[:], func=AF.Sigmoid), reads=['gbt'], writes=['beta'])
                P.op('act', lambda e: e.activation(out=e1[:], in_=gat[:], func=AF.Exp, bias=dtb[:, 0:1]), reads=['gat', 'dtb'], writes=['e1'])
                P.op('act', lambda e: e.activation(out=sp[:], in_=e1[:], func=AF.Ln, bias=one8[:, 0:1]), reads=['e1', 'one8'], writes=['sp'])
                P.op('act', lambda e: e.activation(out=negA[:], in_=alog[:], func=AF.Exp), reads=['alog'], writes=['negA'])
                P.op('dve', lambda e: e.tensor_scalar_mul(out=negA[:], in0=negA[:], scalar1=-1.0), reads=['negA'], writes=['negA'])
                P.op('dve', lambda e: e.tensor_scalar_mul(out=g[:], in0=sp[:], scalar1=negA[:, 0:1]), reads=['sp', 'negA'], writes=['g'])
                P.op('dve', lambda e: e.tensor_tensor_scan(out=gc[:], data0=blk[:], data1=g[:], initial=0.0, op0=ALU.mult, op1=ALU.add),
                     reads=['blk', 'g'], writes=['gc'])
                P.op('dve', lambda e: e.tensor_scalar_mul(out=ngc[:], in0=gc[:], scalar1=-1.0), reads=['gc'], writes=['ngc'])
                P.op('act', lambda e: e.activation(out=eg[:], in_=gc[:], func=AF.Exp), reads=['gc'], writes=['eg'])
                P.op('dve', lambda e: e.tensor_tensor(out=bg[:], in0=beta[:], in1=eg[:], op=ALU.mult), reads=['beta', 'eg'], writes=['bg'])
                g3 = lambda t: t[:].rearrange("p (a b) -> p a b", b=128)
                P.op('dve', lambda e: e.tensor_tensor(out=g3(dl), in0=g3(gc)[:, :, 127:128].to_broadcast([8, 16, 128]), in1=g3(gc), op=ALU.subtract),
                     reads=['gc'], writes=['dl'])
                P.op('act', lambda e: e.activation(out=edl[:], in_=dl[:], func=AF.Exp), reads=['dl'], writes=['edl'])
                P.op('act', lambda e: e.activation(out=cdc[:], in_=g3(gc)[:, :, 127], func=AF.Exp), reads=['gc'], writes=['cdc'])
                P.op('dve', lambda e: e.tensor_scalar_mul(out=beta[:], in0=beta[:], scalar1=-1.0), reads=['beta', 'bg'], writes=['nbeta'])
                for i, (t, k) in enumerate([(beta, 'nbeta'), (gc, 'gc'), (ngc, 'ngc'), (bg, 'bg'), (edl, 'edl'), (eg, 'eg')]):
                    P.dma('sp', lambda e, i=i, t=t: e.dma_start(out=gsc[i], in_=t[:]), reads=[k], key=('st', i))
                P.dma('sp', lambda e: e.dma_start(out=gcd, in_=cdc[:]), reads=['cdc'], key=('st', 7))

        def phase_gdn(l):
            with Phase("gdn") as ph:
                rawp = ph.rot("raw", [128, S], F32, 2)
                cvp = ph.rot("cv", [128, S], F32, 2)
                slp = ph.rot("sl", [128, S], F32, 2)
                cwp = ph.rot("cw", [128, 4], F32, 2)
                cvrot = {'i': 0}
                Gb = ph.sb("Gb", [128, S], F32)
                GbU = ph.sb("GbU", [128, S], F32)
                egb = ph.sb("egb", [128, S], F32)
                qn = ph.sb("qn", [128, S], BF16)
                qd2 = [ph.sb(f"qd{i}", [128, S], BF16) for i in range(2)]
                kn = ph.sb("kn", [128, S], BF16)
                vb = ph.sb("vb", [128, S], BF16)
                kbg_t = ph.sb("kbg_t", [128, 16, 128], BF16)
                kdl_t2 = [ph.sb(f"kdl_t{i}", [128, 16, 128], BF16) for i in range(2)]
                vb_t = ph.sb("vb_t", [128, 16, 128], BF16)
                u_all2 = [ph.sb(f"u_all{i}", [128, 16, 128], F32) for i in range(2)]
                wT2 = [ph.sb(f"wT{i}", [128, S], BF16) for i in range(2)]
                intraT2 = [ph.sb(f"intraT{i}", [128, 16, 128], BF16) for i in range(2)]
                oTh = ph.sb("oTh", [128, S], F32)
                colt = {n: ph.sb("c_" + n, [128, 16], F32) for n in ("nbeta", "gc", "ngc", "bg", "edl")}
                cdb2 = [ph.sb(f"cdb{i}", [128, 16], F32) for i in range(2)]
                gon = ph.sb("gon", [128, 1], F32)
                MU = ph.sb("MU", [128, 128], F32)
                ML = ph.sb("ML", [128, 128], F32)
                sqp = ph.rot("sq", [128, 512], F32, 2)
                rnp = ph.rot("rn", [128, 512], F32, 2)
                zsp = ph.rot("zs", [128, 512], BF16, 2)
                o2p = ph.rot("o2", [128, 512], F32, 2)
                obp = ph.rot("ob", [128, 512], BF16, 2)
                S32 = ph.sb("S32", [128, 128], F32)
                Sb = ph.sb("Sb", [128, 128], BF16)
                vnp = ph.rot("vn", [128, 128], BF16, 2)
                NG = 8
                cht = {n: [ph.sb(f"ch_{n}{j}", [128, 128], F32 if n in ("dL", "dU") else BF16) for j in range(NG)] for n in
                       ("dL", "dU", "A0", "B0", "Tt", "T", "P1s", "P2s")}

                def qapb(b, q):
                    return ps[b][:].bitcast(BF16)[:, 0:128]
                lvl = ph.sb("lvl", [128, 7, 2, 128], F32)
                P.dma('sp', lambda e: e.dma_start(out=lvl[:], in_=c_lvl.rearrange("l t p f -> p l t f")), writes=['lvl'], key=('ld', 13))
                Ttb = [ph.sb(f"ch_Ttb{j}", [128, 128], BF16) for j in range(NG)]
                P.dma('sp', lambda e: e.dma_start(out=MU[:], in_=c_tri[0]), writes=['MU'], key=('ld', 0))
                P.dma('sp', lambda e: e.dma_start(out=ML[:], in_=c_tri[1]), writes=['ML'], key=('ld', 1))
                P.dma('sp', lambda e: e.dma_start(out=gon[:], in_=gonorm[l]), writes=['gon'], key=('ld', 2))
                qrot = {'i': 0}

                def next_q():
                    i = qrot['i'] % 6
                    qrot['i'] += 1
                    return i, 0

                def qap(b, q):
                    return ps[b][:, q * 128:(q + 1) * 128]

                cprot = {'i': 0}

                def cp(dst, src, reads, writes):
                    cprot['i'] += 1
                    if cprot['i'] % 2 == 0:
                        P.op('act', lambda e: e.copy(out=dst, in_=src), reads=reads, writes=writes)
                    else:
                        P.op('dve', lambda e: e.tensor_copy(out=dst, in_=src), reads=reads, writes=writes)

                def genA(h):
                    par = str(h % 2)
                    u_all = u_all2[h % 2]
                    wT = wT2[h % 2]
                    intraT = intraT2[h % 2]
                    qd = qd2[h % 2]
                    kdl_t = kdl_t2[h % 2]
                    cdb = cdb2[h % 2]
                    for i, n in enumerate(("nbeta", "gc", "ngc", "bg", "edl")):
                        P.dma('sp', lambda e, i=i, n=n, h=h: e.dma_start(
                            out=colt[n][:], in_=gsc[i, h, :].rearrange("(sb p) -> p sb", p=128)), writes=['c_' + n], key=('ld', 3 + i))
                    P.dma('sp', lambda e, h=h: e.dma_start(out=Gb[:], in_=gsc[1, h:h + 1, :].to_broadcast([128, S])), writes=['Gb'], key=('ld', 8))
                    P.dma('sp', lambda e, h=h: e.dma_start(out=egb[:], in_=gsc[5, h:h + 1, :].to_broadcast([128, S])), writes=['egb'], key=('ld', 9))
                    g3_ = lambda t: t[:].rearrange("p (a b) -> p a b", b=128)
                    P.op('pool', lambda e: e.tensor_tensor(out=g3_(GbU), in0=g3_(Gb), in1=MU[:].unsqueeze(1).to_broadcast([128, 16, 128]), op=ALU.add),
                         reads=['Gb', 'MU'], writes=['GbU'])
                    P.op('pool', lambda e: e.tensor_tensor(out=g3_(Gb), in0=g3_(Gb), in1=ML[:].unsqueeze(1).to_broadcast([128, 16, 128]), op=ALU.add),
                         reads=['Gb', 'ML', 'GbU'], writes=['Gb'])
                    P.dma('sp', lambda e, h=h: e.dma_start(out=cdb[:], in_=gcd[h:h + 1, :].to_broadcast([128, 16])), writes=['cdb' + par], key=('ld', 10, par))
                    for which, c0 in (("q", h * 128), ("k", 1024 + h * 128), ("v", 2048 + h * 128)):
                        raw, rawk = rawp.next()
                        cv, cvk = cvp.next()
                        sl, slk = slp.next()
                        cw, cwk = cwp.next()
                        cvrot['i'] += 1
                        ceng = 'dve'
                        P.dma('sp', lambda e, c0=c0, raw=raw: e.dma_start(out=raw[:], in_=gqkvT[c0:c0 + 128, :]), writes=[rawk], key=rawk)
                        P.dma('sp', lambda e, c0=c0, cw=cw: e.dma_start(out=cw[:], in_=gconv[l, c0:c0 + 128, :]), writes=[cwk], key=cwk)
                        P.op(ceng, lambda e, raw=raw, cv=cv, cw=cw: e.tensor_scalar_mul(out=cv[:], in0=raw[:], scalar1=cw[:, 3:4]), reads=[rawk, cwk], writes=[cvk])
                        for sh in (1, 2, 3):
                            P.op(ceng, lambda e, sh=sh, raw=raw, cv=cv, cw=cw: e.scalar_tensor_tensor(
                                out=cv[:, sh:], in0=raw[:, 0:S - sh], scalar=cw[:, 3 - sh:4 - sh], in1=cv[:, sh:],
                                op0=ALU.mult, op1=ALU.add), reads=[rawk, cwk, cvk], writes=[cvk])
                        P.op('act', lambda e, sl=sl, cv=cv: e.activation(out=sl[:], in_=cv[:], func=AF.Silu), reads=[cvk], writes=[slk])
                        if which == "v":
                            P.op('pool', lambda e, sl=sl: e.tensor_copy(out=vb[:], in_=sl[:]), reads=[slk], writes=['vb'])
                            continue
                        for tt in range(4):
                            tsl = slice(tt * 512, (tt + 1) * 512)
                            sq, sqk = sqp.next()
                            rn, rnk = rnp.next()
                            pb = 6 + tt % 2
                            P.op('act', lambda e, sq=sq, tsl=tsl, sl=sl: e.activation(out=sq[:], in_=sl[:, tsl], func=AF.Square), reads=[slk], writes=[sqk])
                            P.op('pe', lambda e, sq=sq, pb=pb: e.matmul(ps[pb][:], lhsT=onesf[:], rhs=sq[:], start=True, stop=True),
                                 reads=[sqk], writes=[('ps', pb)])
                            P.op('act', lambda e, rn=rn, pb=pb: e.activation(out=rn[:], in_=ps[pb][:], func=AF.Sqrt, bias=epst[:, 0:1]),
                                 reads=[('ps', pb)], writes=[rnk])
                            P.op('dve', lambda e, rn=rn: e.reciprocal(out=rn[:], in_=rn[:]), reads=[rnk], writes=[rnk])
                            if which == "k":
                                P.op('dve', lambda e, rn=rn, tsl=tsl, sl=sl: e.tensor_tensor(out=kn[:, tsl], in0=sl[:, tsl], in1=rn[:], op=ALU.mult),
                                     reads=[slk, rnk], writes=['kn'])
                            else:
                                P.op('dve', lambda e, rn=rn, tsl=tsl, sl=sl: e.scalar_tensor_tensor(
                                    out=sl[:, tsl], in0=sl[:, tsl], scalar=128.0 ** -0.5, in1=rn[:], op0=ALU.mult, op1=ALU.mult),
                                    reads=[slk, rnk], writes=[slk])
                                P.op('act', lambda e, tsl=tsl, sl=sl: e.copy(out=qn[:, tsl], in_=sl[:, tsl]), reads=[slk], writes=['qn'])
                                P.op('dve', lambda e, tsl=tsl, sl=sl: e.tensor_tensor(out=qd[:, tsl], in0=sl[:, tsl], in1=egb[:, tsl], op=ALU.mult),
                                     reads=[slk, 'egb'], writes=['qd' + par])
                    if GSTOP == 1:
                        return
                    yield
                    for tg in range(4):
                        for (src, sk, outs) in ((kn, 'kn', ((kbg_t, 'kbg_t', 'bg'), (kdl_t, 'kdl_t' + par, 'edl'))),
                                                (vb, 'vb', ((vb_t, 'vb_t', 'nbeta'),))):
                            pb = 6 + (qrot['i'] % 2)
                            qrot['i'] += 1
                            for j in range(4):
                                blk_ = tg * 4 + j
                                P.op('pe', lambda e, j=j, blk_=blk_, pb=pb, src=src: e.transpose(
                                    out=ps[pb][:].bitcast(BF16)[:, j * 128:(j + 1) * 128], in_=src[:, blk_ * 128:(blk_ + 1) * 128],
                                    identity=identb[:]), reads=[sk], writes=[('ps', pb)])
                            for (dst, dk_, cn) in outs:
                                sgn = -1.0 if cn == 'nbeta' else 1.0
                                P.op('dve', lambda e, dst=dst, cn=cn, tg=tg, pb=pb, sgn=sgn: e.scalar_tensor_tensor(
                                    out=dst[:, tg * 4:(tg + 1) * 4, :],
                                    in0=ps[pb][:].bitcast(BF16)[:, 0:512].rearrange("p (a b) -> p a b", a=4), scalar=sgn,
                                    in1=colt[cn][:, tg * 4:(tg + 1) * 4].unsqueeze(2).to_broadcast([128, 4, 128]),
                                    op0=ALU.mult, op1=ALU.mult), reads=[('ps', pb), 'c_' + cn], writes=[dk_])
                    if GSTOP == 2:
                        return
                    yield
                    for g0 in range(0, 16, NG):
                        blks = list(range(g0, g0 + NG))
                        for j, b_ in enumerate(blks):
                            bs = slice(b_ * 128, (b_ + 1) * 128)
                            qa = next_q()
                            qb = next_q()
                            P.op('pe', lambda e, bs=bs, qa=qa: e.matmul(qap(*qa), lhsT=kn[:, bs], rhs=kn[:, bs], start=True, stop=True),
                                 reads=['kn'], writes=[('ps', qa[0])])
                            P.op('pe', lambda e, bs=bs, qb=qb: e.matmul(qap(*qb), lhsT=kn[:, bs], rhs=qn[:, bs], start=True, stop=True),
                                 reads=['kn', 'qn'], writes=[('ps', qb[0])])
                            if GSUB < 2:
                                continue
                            if GSUB < 3:
                                continue
                            P.op('act', lambda e, j=j, b_=b_, bs=bs: e.activation(out=cht['dL'][j][:], in_=Gb[:, bs], func=AF.Exp,
                                                                           bias=colt['gc'][:, b_:b_ + 1], scale=-1.0),
                                 reads=['Gb', 'c_gc'], writes=[('dL', j)])
                            P.op('act', lambda e, j=j, b_=b_, bs=bs: e.activation(out=cht['dU'][j][:], in_=GbU[:, bs], func=AF.Exp,
                                                                           bias=colt['ngc'][:, b_:b_ + 1], scale=1.0),
                                 reads=['GbU', 'c_ngc'], writes=[('dU', j)])
                            if GSUB < 4:
                                continue
                            P.op('dve', lambda e, j=j, b_=b_, qa=qa: e.scalar_tensor_tensor(
                                out=cht['A0'][j][:], in0=qap(*qa), scalar=colt['nbeta'][:, b_:b_ + 1], in1=cht['dL'][j][:],
                                op0=ALU.mult, op1=ALU.mult), reads=[('ps', qa[0]), 'c_nbeta', ('dL', j)], writes=[('A0', j)])
                            P.op('dve', lambda e, j=j, b_=b_, qb=qb: e.tensor_tensor(
                                out=intraT[:, b_, :], in0=qap(*qb), in1=cht['dU'][j][:], op=ALU.mult),
                                reads=[('ps', qb[0]), ('dU', j)], writes=['intraT' + par])
                            if GSUB < 5:
                                continue
                            qc = next_q()
                            P.op('pe', lambda e, j=j, qc=qc: e.transpose(out=qapb(*qc), in_=cht['A0'][j][:], identity=identb[:]),
                                 reads=[('A0', j)], writes=[('ps', qc[0])])
                            cp(cht['B0'][j][:], qapb(*qc), [('ps', qc[0])], [('B0', j)])
                            yield
                            P.op('pool', lambda e, j=j: e.tensor_tensor(out=cht['T'][j][:], in0=cht['A0'][j][:], in1=lvl[:, 0, 0, :], op=ALU.mult),
                                 reads=[('A0', j), 'lvl'], writes=[('T', j)])
                            P.op('pool', lambda e, j=j: e.tensor_tensor(out=cht['T'][j][:], in0=cht['T'][j][:], in1=identf[:], op=ALU.add),
                                 reads=[('T', j)], writes=[('T', j)])
                            P.op('pool', lambda e, j=j: e.tensor_tensor(out=cht['Tt'][j][:], in0=cht['B0'][j][:], in1=lvl[:, 0, 1, :], op=ALU.mult),
                                 reads=[('B0', j), 'lvl'], writes=[('Tt', j)])
                            P.op('pool', lambda e, j=j: e.tensor_tensor(out=cht['Tt'][j][:], in0=cht['Tt'][j][:], in1=identf[:], op=ALU.add),
                                 reads=[('Tt', j)], writes=[('Tt', j)])
                        for lv in range(1, 0 if GSTOP == 31 else 7):
                            yield
                            for j in range(NG):
                                qa = next_q()
                                P.op('pe', lambda e, j=j, qa=qa: e.matmul(qap(*qa), lhsT=cht['B0'][j][:], rhs=cht['T'][j][:], start=True, stop=True),
                                     reads=[('B0', j), ('T', j)], writes=[('ps', qa[0])])
                                P.op('act', lambda e, j=j, qa=qa: e.copy(out=cht['P1s'][j][:], in_=qap(*qa)), reads=[('ps', qa[0])], writes=[('P1s', j)])
                            yield
                            for j in range(NG):
                                qb = next_q()
                                P.op('pe', lambda e, j=j, qb=qb: e.matmul(qap(*qb), lhsT=cht['Tt'][j][:], rhs=cht['P1s'][j][:], start=True, stop=True),
                                     reads=[('Tt', j), ('P1s', j)], writes=[('ps', qb[0])])
                                P.op('dve', lambda e, j=j, qb=qb, lv=lv: e.tensor_tensor(out=cht['P2s'][j][:], in0=qap(*qb), in1=lvl[:, lv, 0, :], op=ALU.mult),
                                     reads=[('ps', qb[0]), 'lvl'], writes=[('P2s', j)])
                            yield
                            for j in range(NG):
                                P.op('pool', lambda e, j=j: e.tensor_tensor(out=cht['T'][j][:], in0=cht['T'][j][:], in1=cht['P2s'][j][:], op=ALU.add),
                                     reads=[('T', j), ('P2s', j)], writes=[('T', j)])
                                qc = next_q()
                                P.op('pe', lambda e, j=j, qc=qc: e.transpose(out=qapb(*qc), in_=cht['P2s'][j][:], identity=identb[:]),
                                     reads=[('P2s', j)], writes=[('ps', qc[0])])
                                P.op('dve', lambda e, j=j, qc=qc: e.tensor_tensor(out=cht['Tt'][j][:], in0=qapb(*qc), in1=cht['Tt'][j][:], op=ALU.add),
                                     reads=[('ps', qc[0]), ('Tt', j)], writes=[('Tt', j)])
                        for j, b_ in enumerate(blks if GSTOP not in (31, 32) else []):
                            bs = slice(b_ * 128, (b_ + 1) * 128)
                            qa = next_q()
                            qb = next_q()
                            P.op('pe', lambda e, j=j, b_=b_, qa=qa: e.matmul(qap(*qa), lhsT=cht['Tt'][j][:], rhs=vb_t[:, b_, :], start=True, stop=True),
                                 reads=[('Tt', j), 'vb_t'], writes=[('ps', qa[0])])
                            P.op('pe', lambda e, j=j, b_=b_, qb=qb: e.matmul(qap(*qb), lhsT=kbg_t[:, b_, :], rhs=cht['Tt'][j][:], start=True, stop=True),
                                 reads=[('Tt', j), 'kbg_t'], writes=[('ps', qb[0])])
                            cp(u_all[:, b_, :], qap(*qa), [('ps', qa[0])], ['u_all' + par])
                            cp(wT[:, bs], qap(*qb), [('ps', qb[0])], ['wT' + par])
                    yield


                def genB(h):
                    par = str(h % 2)
                    u_all = u_all2[h % 2]
                    wT = wT2[h % 2]
                    intraT = intraT2[h % 2]
                    qd = qd2[h % 2]
                    kdl_t = kdl_t2[h % 2]
                    cdb = cdb2[h % 2]
                    if GSTOP in (3, 31, 32):
                        return
                    for b_ in range(16):
                        bs = slice(b_ * 128, (b_ + 1) * 128)
                        vn, vnk = vnp.next()
                        if b_ == 0:
                            P.op('dve', lambda e, vn=vn: e.tensor_copy(out=vn[:], in_=u_all[:, 0, :]), reads=['u_all' + par], writes=[vnk])
                        else:
                            qa = next_q()
                            P.op('pe', lambda e, bs=bs, qa=qa: e.matmul(qap(*qa), lhsT=wT[:, bs], rhs=Sb[:], start=True, stop=True),
                                 reads=['wT' + par, 'Sb'], writes=[('ps', qa[0])])
                            P.op('dve', lambda e, vn=vn, b_=b_, qa=qa: e.tensor_tensor(out=vn[:], in0=u_all[:, b_, :], in1=qap(*qa), op=ALU.subtract),
                                 reads=['u_all' + par, ('ps', qa[0])], writes=[vnk])
                        qo = next_q()
                        if b_ > 0:
                            P.op('pe', lambda e, bs=bs, qo=qo: e.matmul(qap(*qo), lhsT=Sb[:], rhs=qd[:, bs], start=True, stop=False),
                                 reads=['Sb', 'qd' + par], writes=[('ps', qo[0])])
                        P.op('pe', lambda e, vn=vn, b_=b_, qo=qo: e.matmul(qap(*qo), lhsT=vn[:], rhs=intraT[:, b_, :], start=(b_ == 0), stop=True),
                             reads=[vnk, 'intraT' + par], writes=[('ps', qo[0])])
                        P.op('act', lambda e, bs=bs, qo=qo: e.copy(out=oTh[:, bs], in_=qap(*qo)), reads=[('ps', qo[0])], writes=['oTh'])
                        if b_ < 15:
                            qs_ = next_q()
                            P.op('pe', lambda e, vn=vn, b_=b_, qs_=qs_: e.matmul(qap(*qs_), lhsT=kdl_t[:, b_, :], rhs=vn[:], start=True, stop=True),
                                 reads=[vnk, 'kdl_t' + par], writes=[('ps', qs_[0])])
                            if b_ == 0:
                                P.op('dve', lambda e, qs_=qs_: e.tensor_copy(out=S32[:], in_=qap(*qs_)), reads=[('ps', qs_[0])], writes=['S32'])
                            else:
                                P.op('dve', lambda e, qs_=qs_, b_=b_: e.scalar_tensor_tensor(
                                    out=S32[:], in0=S32[:], scalar=cdb[:, b_:b_ + 1], in1=qap(*qs_), op0=ALU.mult, op1=ALU.add),
                                    reads=['S32', 'cdb' + par, ('ps', qs_[0])], writes=['S32'])
                            P.op('act', lambda e: e.copy(out=Sb[:], in_=S32[:]), reads=['S32'], writes=['Sb'])
                        yield
                    if GSTOP == 4:
                        return
                    for tt in range(4):
                        tsl = slice(tt * 512, (tt + 1) * 512)
                        sq, sqk = sqp.next()
                        rn, rnk = rnp.next()
                        pb = 6 + tt % 2
                        P.op('act', lambda e, sq=sq, tsl=tsl: e.activation(out=sq[:], in_=oTh[:, tsl], func=AF.Square), reads=['oTh'], writes=[sqk])
                        P.op('pe', lambda e, sq=sq, pb=pb: e.matmul(ps[pb][:], lhsT=onesf[:], rhs=sq[:], start=True, stop=True),
                             reads=[sqk], writes=[('ps', pb)])
                        P.op('act', lambda e, rn=rn, pb=pb: e.activation(out=rn[:], in_=ps[pb][:], func=AF.Sqrt, bias=epst[:, 0:1], scale=1.0 / 128),
                             reads=[('ps', pb)], writes=[rnk])
                        P.op('dve', lambda e, rn=rn: e.reciprocal(out=rn[:], in_=rn[:]), reads=[rnk], writes=[rnk])
                        zs, zsk = zsp.next()
                        P.dma('sp', lambda e, zs=zs, h=h, tsl=tsl: e.dma_start(out=zs[:], in_=gzsT[h * 128:(h + 1) * 128, tsl]), writes=[zsk], key=zsk)
                        o2, o2k = o2p.next()
                        P.op('dve', lambda e, o2=o2, rn=rn, tsl=tsl: e.scalar_tensor_tensor(
                            out=o2[:], in0=oTh[:, tsl], scalar=gon[:, 0:1], in1=rn[:], op0=ALU.mult, op1=ALU.mult),
                            reads=['oTh', 'gon', rnk], writes=[o2k])
                        ob, obk = obp.next()
                        P.op('pool', lambda e, o2=o2, zs=zs, ob=ob: e.tensor_tensor(out=ob[:], in0=o2[:], in1=zs[:], op=ALU.mult),
                             reads=[o2k, zsk], writes=[obk])
                        P.dma('pool', lambda e, ob=ob, h=h, tsl=tsl: e.dma_start(out=oT[2, h * 128:(h + 1) * 128, tsl], in_=ob[:]), reads=[obk], key=obk)
                        yield
                    yield

                def drain(g_):
                    for _ in g_:
                        pass
                drain(genA(0))
                for h in range(GHEADS):
                    gb_ = genB(h)
                    ga_ = genA(h + 1) if h + 1 < GHEADS else None
                    while gb_ is not None or ga_ is not None:
                        if ga_ is not None:
                            for _ in range(2):
                                try:
                                    next(ga_)
                                except StopIteration:
                                    ga_ = None
                                    break
                        if gb_ is not None:
                            try:
                                next(gb_)
                            except StopIteration:
                                gb_ = None


        def phase_merge(l, actT):
            with Phase("merge") as ph:
                obh = [ph.sb(f"obh{b}", [128, 8, 1024], BF16) for b in range(3)]
                wts = ph.rot("wm", [128, 8, 256], BF16, 6)
                gtp = ph.rot("gt", [128, 512], BF16, 6)
                mp = ph.rot("mm", [128, 512], F32, 6)
                sp_ = ph.rot("ms", [128, 512], F32, 2)
                for half in range(2):
                    for b in range(3):
                        for kh in range(2):
                            P.dma('sp', lambda e, b=b, half=half, kh=kh: e.dma_start(
                                out=obh[b][:, kh * 4:(kh + 1) * 4, :],
                                in_=oT[b, kh * 512:(kh + 1) * 512, half * 1024:(half + 1) * 1024].rearrange("(kc p) t -> p kc t", p=128)),
                                writes=[('obh', b)], key=('ld', b))
                    for dg in range(8):
                        wl = []
                        for b in range(3):
                            wt, wk = wts.next()
                            load_w(wt, wk, w_branch[l, b], 8, dg * 256, 256)
                            wl.append((wt, wk))
                        for ct in range(2):
                            dchunk = dg * 2 + ct
                            for t2 in range(2):
                                tt = half * 2 + t2
                                ms = []
                                for b in range(3):
                                    pb = next_ps(0, 6)
                                    wt, wk = wl[b]
                                    for kc in range(8):
                                        P.op('pe', lambda e, kc=kc, ct=ct, t2=t2, pb=pb, wt=wt, b=b: e.matmul(
                                            ps[pb][:], lhsT=wt[:, kc, ct * 128:(ct + 1) * 128], rhs=obh[b][:, kc, t2 * 512:(t2 + 1) * 512],
                                            start=(kc == 0), stop=(kc == 7)), reads=[wk, ('obh', b)], writes=[('ps', pb)])
                                    gt, gk = gtp.next()
                                    r0 = b * 2048 + dchunk * 128
                                    P.dma('sp', lambda e, gt=gt, r0=r0, tt=tt: e.dma_start(
                                        out=gt[:], in_=gateT[r0:r0 + 128, tt * 512:(tt + 1) * 512]), writes=[gk], key=gk)
                                    m_, mk = mp.next()
                                    P.op('dve', lambda e, m_=m_, gt=gt, pb=pb: e.tensor_tensor(out=m_[:], in0=ps[pb][:], in1=gt[:], op=ALU.mult),
                                         reads=[('ps', pb), gk], writes=[mk])
                                    ms.append((m_, mk))
                                s_, sk = sp_.next()
                                P.op('dve', lambda e, s_=s_, ms=ms: e.tensor_tensor(out=s_[:], in0=ms[0][0][:], in1=ms[1][0][:], op=ALU.add),
                                     reads=[ms[0][1], ms[1][1]], writes=[sk])
                                P.op('dve', lambda e, s_=s_, ms=ms, dchunk=dchunk, tt=tt: e.tensor_tensor(
                                    out=actT[:, dchunk, tt * 512:(tt + 1) * 512], in0=s_[:], in1=ms[2][0][:], op=ALU.add),
                                    reads=[sk, ms[2][1]], writes=[('actT', tt)])

        def resid_epi(ph, r0):
            xl = ph.rot("xl", [128, 512], F32, 4)
            xs = ph.rot("xs", [128, 512], F32, 4)

            def epi(pb, m, ct, tt):
                rr = r0 + ct * 128
                t, k = xl.next()
                P.dma('sp', lambda e: e.dma_start(out=t[:], in_=xT[rr:rr + 128, tt * 512:(tt + 1) * 512]), writes=[k], key=k)
                o, ok = xs.next()
                P.op('dve', lambda e: e.tensor_tensor(out=o[:], in0=ps[pb][:], in1=t[:], op=ALU.add), reads=[('ps', pb), k], writes=[ok])
                P.dma('sp', lambda e: e.dma_start(out=xT[rr:rr + 128, tt * 512:(tt + 1) * 512], in_=o[:]), reads=[ok], key=ok)
            return epi

        def phase_wout(l, actT):
            with Phase("wout") as ph:
                wts = ph.rot("w", [128, 16, 512], BF16, 3)
                slots = {}

                def issue(j):
                    wt, wk = wts.next()
                    slots[j] = (wt, wk)
                    load_w(wt, wk, w_out[l], 16, j * 512, 512)
                epis = {}
                xl = ph.rot("xl", [128, 512], F32, 4)
                xs = ph.rot("xs", [128, 512], F32, 4)

                def mk_epi(r0):
                    def epi(pb, m, ct, tt):
                        rr = r0 + ct * 128
                        t, k = xl.next()
                        P.dma('sp', lambda e: e.dma_start(out=t[:], in_=xT[rr:rr + 128, tt * 512:(tt + 1) * 512]), writes=[k], key=k)
                        o, ok = xs.next()
                        P.op('dve', lambda e: e.tensor_tensor(out=o[:], in0=ps[pb][:], in1=t[:], op=ALU.add), reads=[('ps', pb), k], writes=[ok])
                        P.dma('act', lambda e: e.dma_start(out=xT[rr:rr + 128, tt * 512:(tt + 1) * 512], in_=o[:]), reads=[ok], key=ok)
                    return epi
                issue(0)
                issue(1)
                for j in range(4):
                    wt, wk = slots[j]
                    gemm_fm(wt, wk, 16, 512, actT, 'actT', mk_epi(j * 512))
                    if j + 2 < 4:
                        issue(j + 2)

        def phase_ffn_up(l, actT):
            with Phase("ffnup") as ph:
                norm_fm(ph, xT, 16, mlp_norm[l], actT, 'actT', D)
                wts = ph.rot("w", [128, 16, 512], BF16, 3)
                rp = ph.rot("r", [128, 512], F32, 3)
                hp = ph.rot("hb", [128, 512], BF16, 3)
                slots = {}

                def issue(j):
                    wt, wk = wts.next()
                    slots[j] = (wt, wk)
                    load_w(wt, wk, w_up[l], 16, j * 512, 512)

                def mk_epi(r0):
                    def epi(pb, m, ct, tt):
                        r, rk = rp.next()
                        P.op('act', lambda e: e.activation(out=r[:], in_=ps[pb][:], func=AF.Relu), reads=[('ps', pb)], writes=[rk])
                        hb, hk = hp.next()
                        P.op('pool', lambda e: e.tensor_tensor(out=hb[:], in0=r[:], in1=r[:], op=ALU.mult), reads=[rk], writes=[hk])
                        kc_ = (r0 + ct * 128) // 128
                        P.dma('sp', lambda e: e.dma_start(out=hidT[tt, :, kc_, :], in_=hb[:]), reads=[hk], key=hk)
                    return epi
                issue(0)
                issue(1)
                for j in range(16):
                    wt, wk = slots[j]
                    gemm_fm(wt, wk, 16, 512, actT, 'actT', mk_epi(j * 512))
                    if j + 2 < 16:
                        issue(j + 2)

        def phase_ffn_down(l):
            with Phase("ffndn") as ph:
                wd = [ph.sb(f"wd{i}", [128, 64, 512], BF16) for i in range(2)]
                hp = ph.rot("hid", [128, 8, 512], BF16, 3)
                xl = ph.rot("xl", [128, 512], F32, 4)
                xs = ph.rot("xs", [128, 512], F32, 4)

                def issue(eg):
                    load_w(wd[eg % 2], ('wd', eg % 2), w_down[l], 64, eg * 512, 512)
                issue(0)
                it = 0
                for eg in range(4):
                    if eg + 1 < 4:
                        issue(eg + 1)
                    wt, wk = wd[eg % 2], ('wd', eg % 2)
                    for tt in range(4):
                        base = 4 * (it % 2)
                        it += 1
                        for kcg in range(8):
                            ht, hk = hp.next()
                            P.dma('sp', lambda e, ht=ht, kcg=kcg, tt=tt: e.dma_start(
                                out=ht[:], in_=hidT[tt, :, kcg * 8:(kcg + 1) * 8, :]),
                                writes=[hk], key=hk)
                            for j in range(8):
                                kc = kcg * 8 + j
                                for ct in range(4):
                                    P.op('pe', lambda e, kc=kc, j=j, ct=ct, ht=ht, wt=wt, base=base: e.matmul(
                                        ps[base + ct][:], lhsT=wt[:, kc, ct * 128:(ct + 1) * 128], rhs=ht[:, j, :],
                                        start=(kc == 0), stop=(kc == 63)), reads=[wk, hk], writes=[('ps', base + ct)])
                        for ct in range(4):
                            pb = base + ct
                            rr = eg * 512 + ct * 128
                            t, k = xl.next()
                            P.dma('sp', lambda e, t=t, rr=rr, tt=tt: e.dma_start(out=t[:], in_=xT[rr:rr + 128, tt * 512:(tt + 1) * 512]), writes=[k], key=k)
                            o, ok = xs.next()
                            P.op('dve', lambda e, o=o, t=t, pb=pb: e.tensor_tensor(out=o[:], in0=ps[pb][:], in1=t[:], op=ALU.add),
                                 reads=[('ps', pb), k], writes=[ok])
                            P.dma('act', lambda e, o=o, rr=rr, tt=tt: e.dma_start(out=xT[rr:rr + 128, tt * 512:(tt + 1) * 512], in_=o[:]), reads=[ok], key=ok)

        def phase_final():
            with Phase("final") as ph:
                norm_fm(ph, xT, 16, final_norm, None, None, D, tag="nf", final_out=out_d)

        def with_act(fn):
            with ExitStack() as aes:
                actT = aes.enter_context(nc.sbuf_tensor(f"actT_{state['phase']}", [128, 16, S], BF16))
                fn(actT)

        if only == 'gdn':
            phase_gdn(0)
            nl = 0
            stop = 'x'
        else:
            phase_transpose_in()
        for l in range(nl):
            with_act(lambda actT: phase_inproj(l, actT))
            if stop == 'inproj':
                break
            phase_fox_prep(l)
            phase_attn(l)
            if stop == 'mla':
                break
            phase_gdn_prep(l)
            if stop == 'gprep':
                break
            phase_gdn(l)
            if stop == 'gdn':
                break

            def mo(actT):
                phase_merge(l, actT)
                phase_wout(l, actT)
            with_act(mo)
            if stop == 'wout':
                break
            with_act(lambda actT: phase_ffn_up(l, actT))
            phase_ffn_down(l)
        if stop is None:
            phase_final()
        print("total ops", P.nops, "sems", P.nsem, "cnt", P.cnt)
    return nc


def host_inputs(inputs):
    f = lambda a: np.ascontiguousarray(np.asarray(a, dtype=np.float32))
    m = {}
    m["attn_norm"] = f(np.asarray(inputs["attn_norm"]).reshape(NL, 16, 128).transpose(0, 2, 1))
    m["w_in"] = f(inputs["w_in"])
    m["fox_fgate_bias"] = f(np.asarray(inputs["fox_fgate_bias"]).reshape(NL, 8, 1))
    m["mla_q_norm"] = f(np.asarray(inputs["mla_q_norm"]).reshape(NL, 4, 128).transpose(0, 2, 1))
    m["mla_kv_norm"] = f(np.asarray(inputs["mla_kv_norm"]).reshape(NL, 4, 128).transpose(0, 2, 1))
    m["w_mla_uq"] = f(np.asarray(inputs["w_mla_uq"]).reshape(NL, 512, 1536))
    m["w_mla_ukv"] = f(np.asarray(inputs["w_mla_ukv"]).reshape(NL, 512, 2048))
    m["gdn_conv"] = f(np.asarray(inputs["gdn_conv"]).transpose(0, 2, 1))
    m["gdn_a_log"] = f(np.asarray(inputs["gdn_a_log"]).reshape(NL, 8, 1))
    m["gdn_dt_bias"] = f(np.asarray(inputs["gdn_dt_bias"]).reshape(NL, 8, 1))
    m["gdn_out_norm"] = f(np.asarray(inputs["gdn_out_norm"]).reshape(NL, 128, 1))
    m["w_branch"] = f(inputs["w_branch"])
    m["w_out"] = f(inputs["w_out"])
    m["mlp_norm"] = f(np.asarray(inputs["mlp_norm"]).reshape(NL, 16, 128).transpose(0, 2, 1))
    m["w_up"] = f(inputs["w_up"])
    m["w_down"] = f(inputs["w_down"])
    m["final_norm"] = f(np.asarray(inputs["final_norm"]).reshape(16, 128).T)
    m["c_ident"] = np.eye(128, dtype=np.float32)
    s_idx = np.arange(128)[None, :, None] + 128 * np.arange(4)[:, None, None]
    t_idx = np.arange(512)[None, None, :]
    m["c_mask"] = np.where(s_idx > t_idx, -30000.0, 0.0).astype(np.float32)
    inv_freq = (np.float32(10000.0) ** (-np.arange(0, 64, 2, dtype=np.float32) / np.float32(64))).astype(np.float32)
    ang = (np.arange(S, dtype=np.float32)[None, :] * inv_freq[:, None]).astype(np.float32)
    cos, sin = np.cos(ang).astype(np.float32), np.sin(ang).astype(np.float32)
    m["c_rope"] = np.stack([np.concatenate([cos, cos], 0), np.concatenate([sin, sin], 0)]).astype(np.float32)
    rot = np.zeros((64, 64), np.float32)
    for i in range(32):
        rot[i + 32, i] = -1.0
        rot[i, i + 32] = 1.0
    m["c_rot"] = rot
    blk = np.ones((8, S), np.float32)
    blk[:, ::128] = 0.0
    m["c_blk"] = blk
    a = np.arange(128)[:, None]
    b = np.arange(128)[None, :]
    lv = []
    for sz in (1, 2, 4, 8, 16, 32, 64):
        ms = ((a // (2 * sz) == b // (2 * sz)) & (a % (2 * sz) >= sz) & (b % (2 * sz) < sz)).astype(np.float32)
        lv.append(np.stack([ms, ms.T]))
    m["c_lvl"] = np.ascontiguousarray(np.stack(lv))
    m["c_tri"] = np.stack([np.where(b >= a, 0.0, -30000.0), np.where(b < a, 0.0, 30000.0)]).astype(np.float32)
    return m


_CACHE = {}


def kernel(**inputs):
    shared = host_inputs(inputs)
    x = np.asarray(inputs["x"], dtype=np.float32)
    if 'nc' not in _CACHE:
        _CACHE['nc'] = build()
    nc = _CACHE['nc']
    in_maps = []
    for b in range(8):
        mm = dict(shared)
        mm["x"] = np.ascontiguousarray(x[b])
        in_maps.append(mm)
    res = run_bass_kernel_spmd(nc, in_maps, core_ids=list(range(8)))
    return np.stack([np.asarray(r["out"], dtype=np.float32) for r in res.results], axis=0)
```

```python
import math
import os
from contextlib import ExitStack

import numpy as np
import ml_dtypes
import concourse.bass as bass
import concourse.mybir as mybir
from concourse.bass_utils import run_bass_kernel_spmd

F32 = mybir.dt.float32
BF16 = mybir.dt.bfloat16
AF = mybir.ActivationFunctionType
ALU = mybir.AluOpType

ENGS = ['pe', 'act', 'dve', 'pool', 'sp']
GSTOP = int(os.environ.get('GSTOP', '0'))
GSUB = int(os.environ.get('GSUB', '9'))
GHEADS = int(os.environ.get('GHEADS', '8'))
EPOCH = 10 ** 9
SAME_ENGINE_SYNC = True

S = 2048
D = 2048
NL = 4
DIN = 14424
DFF = 8192
EPS = 1e-6
C_FQ, C_FK, C_FV, C_FF = 0, 1024, 2048, 3072
C_CQ, C_CKV, C_KPE = 3080, 3592, 4104
C_GQKV, C_GZ, C_GB, C_GA, C_GATE = 4168, 7240, 8264, 8272, 8280


class Prog:
    def __init__(self, nc, es):
        self.nc = nc
        self.es = es
        self.ops = {e: [] for e in ENGS}
        self.cnt = {e: 0 for e in ENGS}
        self.esems = {e: [] for e in ENGS}
        self.known = {e: {} for e in ENGS}
        self.last_w = {}
        self.readers = {}
        self.dsem = {}
        self.sem_owner = {}
        self.nsem = 0
        self.nops = 0
        self.retired = []
        self.pool = []
        self.phase_keys = {}

    def _newsem(self, name, owner=None):
        s = self.es.enter_context(self.nc.semaphore(f"{name}_{self.nsem}"))
        self.nsem += 1
        self.sem_owner[id(s)] = owner
        return s

    def _tick(self, e):
        k = self.cnt[e]
        self.cnt[e] += 1
        ep = k // EPOCH
        while len(self.esems[e]) <= ep:
            self.esems[e].append(self._newsem(f"s_{e}", e))
        return (self.esems[e][ep], k % EPOCH + 1)

    def _filter(self, e, need):
        waits = []
        for sid, (s, v) in need.items():
            if self.sem_owner.get(sid) == e and (e == 'pe' or not SAME_ENGINE_SYNC):
                continue
            if self.known[e].get(sid, 0) >= v:
                continue
            self.known[e][sid] = v
            waits.append((s, v))
        return waits

    def _deps(self, e, reads, writes):
        need = {}

        def add(ev):
            s, v = ev
            if v > need.get(id(s), (None, 0))[1]:
                need[id(s)] = (s, v)
        for r in reads:
            if r in self.last_w:
                add(self.last_w[r])
        for w in writes:
            if w in self.last_w:
                add(self.last_w[w])
            for ev in self.readers.get(w, {}).values():
                add(ev)
        return self._filter(e, need)

    def _commit(self, ev, reads, writes):
        s, v = ev
        for r in reads:
            d = self.readers.setdefault(r, {})
            if v > d.get(id(s), (None, 0))[1]:
                d[id(s)] = ev
        for w in writes:
            self.last_w[w] = ev
            self.readers[w] = {}

    def op(self, e, fn, reads=(), writes=()):
        writes = list(writes) + [r for r in reads if isinstance(r, tuple) and r[0] == 'ps' and r not in writes]
        waits = self._deps(e, reads, writes)
        ev = self._tick(e)
        self.ops[e].append((fn, waits, ev, 1))
        self._commit(ev, reads, writes)
        self.nops += 1

    def dma(self, q, fn, reads=(), writes=(), key=None):
        waits = self._deps(q, reads, writes)
        if key not in self.phase_keys:
            self.phase_keys[key] = len(self.phase_keys)
        idx = self.phase_keys[key]
        while len(self.pool) <= idx:
            self.pool.append([self._newsem("d"), 0])
        ds = self.pool[idx]
        ds[1] += 16
        ev = (ds[0], ds[1])
        self.ops[q].append((fn, waits, ev, 16))
        self._commit(ev, reads, writes)
        self.nops += 1

    def barrier(self):
        evs = {}
        for e in ENGS:
            if self.cnt[e] > 0:
                k = self.cnt[e] - 1
                s = self.esems[e][k // EPOCH]
                evs[id(s)] = (s, k % EPOCH + 1)
        for (s, c) in self.pool:
            if c > 0:
                evs[id(s)] = (s, c)
        self.phase_keys = {}
        for e in ENGS:
            waits = self._filter(e, dict(evs))
            if waits:
                self.ops[e].append((None, waits, None, 0))
        self.last_w = {}
        self.readers = {}

    def simcheck(self):
        if not hasattr(self, 'simsem'):
            self.simsem = {}
        ptr = {e: 0 for e in ENGS}
        prog = True
        while prog:
            prog = False
            for e in ENGS:
                while ptr[e] < len(self.ops[e]):
                    fn, waits, ev, amt = self.ops[e][ptr[e]]
                    if all(self.simsem.get(id(s_), 0) >= v for s_, v in waits):
                        if ev is not None:
                            self.simsem[id(ev[0])] = self.simsem.get(id(ev[0]), 0) + amt
                            assert self.simsem[id(ev[0])] == ev[1], (e, ptr[e], self.simsem[id(ev[0])], ev[1])
                        ptr[e] += 1
                        prog = True
                    else:
                        break
        for e in ENGS:
            assert ptr[e] == len(self.ops[e]), f"DEADLOCK on {e} at {ptr[e]}/{len(self.ops[e])}: {self.ops[e][ptr[e]][1]}"

    def emit(self):
        nc = self.nc
        self.simcheck()
        with nc.Block() as block:
            def mk(e):
                def body(eng):
                    for (fn, waits, ev, amt) in self.ops[e]:
                        for (s, v) in waits:
                            eng.wait_ge(s, v)
                        if fn is None:
                            continue
                        ins = fn(eng)
                        ins.then_inc(ev[0], amt)
                return body
            block.tensor(mk('pe'))
            block.scalar(mk('act'))
            block.vector(mk('dve'))
            block.gpsimd(mk('pool'))
            block.sync(mk('sp'))
        self.ops = {e: [] for e in ENGS}


class Rot:
    def __init__(self, tiles, name):
        self.tiles = tiles
        self.name = name
        self.i = 0

    def next(self):
        j = self.i % len(self.tiles)
        self.i += 1
        return self.tiles[j], (self.name, j)


def build(nl=NL, dbg=(), stop=None, only=None, ext_in=()):
    nc = bass.Bass("TRN2", target_bir_lowering=False)
    din = lambda name, shape, dt=F32: nc.dram_tensor(name, list(shape), dt, kind="ExternalInput").ap()
    x_in = din("x", [S, D])
    attn_norm = din("attn_norm", [NL, 128, 16])
    w_in = din("w_in", [NL, D, DIN])
    fbias = din("fox_fgate_bias", [NL, 8, 1])
    qnorm = din("mla_q_norm", [NL, 128, 4])
    kvnorm = din("mla_kv_norm", [NL, 128, 4])
    w_uq = din("w_mla_uq", [NL, 512, 1536])
    w_ukv = din("w_mla_ukv", [NL, 512, 2048])
    gconv = din("gdn_conv", [NL, 3072, 4])
    galog = din("gdn_a_log", [NL, 8, 1])
    gdtb = din("gdn_dt_bias", [NL, 8, 1])
    gonorm = din("gdn_out_norm", [NL, 128, 1])
    w_branch = din("w_branch", [NL, 3, 1024, D])
    w_out = din("w_out", [NL, D, D])
    mlp_norm = din("mlp_norm", [NL, 128, 16])
    w_up = din("w_up", [NL, D, DFF])
    w_down = din("w_down", [NL, DFF, D])
    final_norm = din("final_norm", [128, 16])
    c_ident = din("c_ident", [128, 128])
    c_mask = din("c_mask", [4, 128, 512])
    c_rope = din("c_rope", [2, 64, S])
    c_rot = din("c_rot", [64, 64])
    c_blk = din("c_blk", [8, S])
    c_lvl = din("c_lvl", [7, 2, 128, 128])
    c_tri = din("c_tri", [2, 128, 128])
    out_d = nc.dram_tensor("out", [S, D], F32, kind="ExternalOutput").ap()

    def scratch(name, shape, dt):
        kind = "ExternalOutput" if name in dbg else ("ExternalInput" if name in ext_in else "Internal")
        return nc.dram_tensor(name, list(shape), dt, kind=kind).ap()
    xT = scratch("xT", [D, S], F32)
    fqT = scratch("fqT", [1024, S], BF16)
    fkT = scratch("fkT", [1024, S], BF16)
    fv = scratch("fv", [S, 1024], BF16)
    smallT = scratch("smallT", [96, S], F32)
    kpeT = scratch("kpeT", [64, S], F32)
    cqT = scratch("cqT", [512, S], F32)
    ckvT = scratch("ckvT", [512, S], F32)
    gqkvT = scratch("gqkvT", [3072, S], F32)
    gzsT = scratch("gzsT", [1024, S], BF16)
    gateT = scratch("gateT", [6144, S], BF16)
    negc = scratch("negc", [8, S], F32)
    crow = scratch("crow", [3, 8, S], BF16)
    oT = scratch("oT", [3, 1024, S], BF16)
    hidT = scratch("hidT", [4, 128, 64, 512], BF16)
    gsc = scratch("gsc", [6, 8, S], F32)
    gcd = scratch("gcd", [8, 16], F32)

    es = ExitStack()
    with es:
        P = Prog(nc, es)
        es.enter_context(nc.allow_non_contiguous_dma(reason="small strided parameter loads"))
        es.enter_context(nc.allow_low_precision(reason="bf16 matmul operands, fp32 accumulation"))
        ps = [es.enter_context(nc.psum_tensor(f"ps{i}", [128, 512], F32)) for i in range(8)]
        gsb = lambda name, shape, dt: es.enter_context(nc.sbuf_tensor(name, list(shape), dt))
        identf = gsb("identf", [128, 128], F32)
        identb = gsb("identb", [128, 128], BF16)
        onesf = gsb("onesf", [128, 128], F32)
        onesb = gsb("onesb", [128, 128], BF16)
        epst = gsb("epst", [128, 1], F32)
        maskb = gsb("maskb", [128, 4, 512], BF16)

        state = {'phase': 0, 'done': False}

        class Phase:
            def __init__(self, name):
                self.name = name

            def __enter__(self):
                self.pes = ExitStack()
                self.pes.__enter__()
                self.cache = {}
                return self

            def sb(self, name, shape, dt):
                if name not in self.cache:
                    self.cache[name] = self.pes.enter_context(
                        nc.sbuf_tensor(f"{self.name}_{name}_{state['phase']}", list(shape), dt))
                return self.cache[name]

            def rot(self, name, shape, dt, n):
                if ('rot', name) not in self.cache:
                    self.cache[('rot', name)] = Rot([self.sb(f"{name}{i}", shape, dt) for i in range(n)], name)
                return self.cache[('rot', name)]

            def __exit__(self, *a):
                P.barrier()
                P.emit()
                self.pes.__exit__(None, None, None)
                state['phase'] += 1
                return False

        with Phase("init") as ph:
            mstage = ph.sb("mstage", [128, 4, 512], F32)
            P.dma('sp', lambda e: e.dma_start(out=identf[:], in_=c_ident), writes=['identf'], key=('ld', 0))
            P.dma('sp', lambda e: e.dma_start(out=mstage[:], in_=c_mask.rearrange("j p t -> p j t")),
                  writes=['mstage'], key=('ld', 1))
            P.op('dve', lambda e: e.tensor_copy(out=identb[:], in_=identf[:]), reads=['identf'], writes=['identb'])
            P.op('dve', lambda e: e.memset(onesf[:], 1.0), writes=['onesf'])
            P.op('dve', lambda e: e.memset(onesb[:], 1.0), writes=['onesb'])
            P.op('dve', lambda e: e.memset(epst[:], EPS), writes=['epst'])
            P.op('dve', lambda e: e.tensor_copy(out=maskb[:], in_=mstage[:]), reads=['mstage'], writes=['maskb'])

        psrot = {'i': 0}

        def next_ps(lo=0, n=4):
            j = lo + psrot['i'] % n
            psrot['i'] += 1
            return j

        def phase_transpose_in():
            with Phase("tin") as ph:
                xt = ph.rot("xt", [128, D], F32, 2)
                st = ph.rot("st", [128, 4, 128], F32, 4)
                for tb in range(16):
                    xtile, xk = xt.next()
                    P.dma('sp', lambda e, tb=tb, xtile=xtile: e.dma_start(out=xtile[:], in_=x_in[tb * 128:(tb + 1) * 128, :]),
                          writes=[xk], key=xk)
                    for kg in range(4):
                        pb = next_ps()
                        for j in range(4):
                            kc = kg * 4 + j
                            P.op('pe', lambda e, kc=kc, j=j, pb=pb, xtile=xtile: e.transpose(
                                out=ps[pb][:, j * 128:(j + 1) * 128], in_=xtile[:, kc * 128:(kc + 1) * 128],
                                identity=identf[:]), reads=[xk], writes=[('ps', pb)])
                        stile, sk = st.next()
                        eng = 'act' if kg % 2 == 0 else 'dve'
                        if eng == 'act':
                            P.op('act', lambda e, pb=pb, stile=stile: e.copy(
                                out=stile[:], in_=ps[pb][:].rearrange("p (a b) -> p a b", a=4)),
                                reads=[('ps', pb)], writes=[sk])
                        else:
                            P.op('dve', lambda e, pb=pb, stile=stile: e.tensor_copy(
                                out=stile[:], in_=ps[pb][:].rearrange("p (a b) -> p a b", a=4)),
                                reads=[('ps', pb)], writes=[sk])
                        P.dma('sp', lambda e, kg=kg, tb=tb, stile=stile: e.dma_start(
                            out=xT[kg * 512:(kg + 1) * 512, tb * 128:(tb + 1) * 128].rearrange("(a p) t -> p a t", p=128),
                            in_=stile[:]), reads=[sk], key=sk)

        def norm_fm(ph, srcT, nk, gain_ap, dst, dkey, dn, tag="n", final_out=None, nld=3, nsq=3):
            ld = ph.rot(tag + "ld", [128, S], F32, nld)
            sq = ph.rot(tag + "sq", [128, 512], F32, nsq)
            rstd = ph.sb(tag + "rstd", [128, S], F32)
            gt = ph.sb(tag + "gain", [128, nk], F32)
            gk = tag + 'gain'
            P.dma('sp', lambda e: e.dma_start(out=gt[:], in_=gain_ap), writes=[gk], key=('ld', 9))
            for kc in range(nk):
                t, k = ld.next()
                P.dma('sp', lambda e, t=t, kc=kc: e.dma_start(out=t[:], in_=srcT[kc * 128:(kc + 1) * 128, :]), writes=[k], key=k)
                for tt in range(4):
                    pb = 4 + tt
                    q, qk = sq.next()
                    P.op('act', lambda e, t=t, q=q, tt=tt: e.activation(out=q[:], in_=t[:, tt * 512:(tt + 1) * 512], func=AF.Square),
                         reads=[k], writes=[qk])
                    P.op('pe', lambda e, q=q, pb=pb, kc=kc: e.matmul(ps[pb][:], lhsT=onesf[:], rhs=q[:],
                                                                      start=(kc == 0), stop=(kc == nk - 1)),
                         reads=[qk], writes=[('ps', pb)])
            for tt in range(4):
                pb = 4 + tt
                P.op('act', lambda e, pb=pb, tt=tt: e.activation(out=rstd[:, tt * 512:(tt + 1) * 512], in_=ps[pb][:],
                                                                 func=AF.Sqrt, bias=epst[:, 0:1], scale=1.0 / dn),
                     reads=[('ps', pb)], writes=[(tag + 'rstd', tt)])
                P.op('dve', lambda e, tt=tt: e.reciprocal(out=rstd[:, tt * 512:(tt + 1) * 512],
                                                          in_=rstd[:, tt * 512:(tt + 1) * 512]),
                     reads=[(tag + 'rstd', tt)], writes=[(tag + 'rstd', tt)])
            if final_out is not None:
                yt = ph.rot(tag + "y", [128, 512], F32, 2)
                ost = ph.rot(tag + "ost", [128, 4, 128], F32, 3)
            for kc in range(nk):
                t, k = ld.next()
                P.dma('sp', lambda e, t=t, kc=kc: e.dma_start(out=t[:], in_=srcT[kc * 128:(kc + 1) * 128, :]), writes=[k], key=k)
                for tt in range(4):
                    if final_out is None:
                        P.op('dve', lambda e, t=t, kc=kc, tt=tt: e.scalar_tensor_tensor(
                            out=dst[:, kc, tt * 512:(tt + 1) * 512], in0=t[:, tt * 512:(tt + 1) * 512], scalar=gt[:, kc:kc + 1],
                            in1=rstd[:, tt * 512:(tt + 1) * 512], op0=ALU.mult, op1=ALU.mult),
                            reads=[k, gk, (tag + 'rstd', tt)], writes=[(dkey, tt)])
                    else:
                        y, yk = yt.next()
                        P.op('dve', lambda e, t=t, kc=kc, tt=tt, y=y: e.scalar_tensor_tensor(
                            out=y[:], in0=t[:, tt * 512:(tt + 1) * 512], scalar=gt[:, kc:kc + 1],
                            in1=rstd[:, tt * 512:(tt + 1) * 512], op0=ALU.mult, op1=ALU.mult),
                            reads=[k, gk, (tag + 'rstd', tt)], writes=[yk])
                        pb = next_ps()
                        for j in range(4):
                            P.op('pe', lambda e, y=y, j=j, pb=pb: e.transpose(
                                out=ps[pb][:, j * 128:(j + 1) * 128], in_=y[:, j * 128:(j + 1) * 128], identity=identf[:]),
                                reads=[yk], writes=[('ps', pb)])
                        o, ok = ost.next()
                        P.op('act', lambda e, o=o, pb=pb: e.copy(out=o[:], in_=ps[pb][:].rearrange("p (a b) -> p a b", a=4)),
                             reads=[('ps', pb)], writes=[ok])
                        P.dma('act', lambda e, o=o, tt=tt, kc=kc: e.dma_start(
                            out=final_out[tt * 512:(tt + 1) * 512, kc * 128:(kc + 1) * 128].rearrange("(a p) d -> p a d", p=128),
                            in_=o[:]), reads=[ok], key=ok)

        def load_w(wt, wk, w2d, nk, c0, ncols, dcol=0):
            step = 8
            for k0 in range(0, nk, step):
                k1 = min(nk, k0 + step)
                P.dma('pool', lambda e, k0=k0, k1=k1: e.dma_start(
                    out=wt[:, k0:k1, dcol:dcol + ncols],
                    in_=w2d[k0 * 128:k1 * 128, c0:c0 + ncols].rearrange("(kc p) c -> p kc c", p=128)),
                    writes=[wk], key=wk)

        def gemm_fm(wt, wk, nk, ncols, act, akey, epi):
            for ct in range((ncols + 127) // 128):
                m = min(128, ncols - ct * 128)
                for tt in range(4):
                    pb = next_ps()
                    for kc in range(nk):
                        P.op('pe', lambda e, kc=kc, ct=ct, tt=tt, pb=pb, m=m: e.matmul(
                            ps[pb][0:m, :], lhsT=wt[:, kc, ct * 128:ct * 128 + m],
                            rhs=act[:, kc, tt * 512:(tt + 1) * 512], start=(kc == 0), stop=(kc == nk - 1)),
                            reads=[wk, (akey, tt)], writes=[('ps', pb)])
                    epi(pb, m, ct, tt)

        def gemm_tm(wt, wk, nk, ncols, act, akey, epi):
            for tb in range(16):
                pb = next_ps()
                for kc in range(nk):
                    P.op('pe', lambda e, kc=kc, tb=tb, pb=pb: e.matmul(
                        ps[pb][:, 0:ncols], lhsT=act[:, kc, tb * 128:(tb + 1) * 128], rhs=wt[:, kc, 0:ncols],
                        start=(kc == 0), stop=(kc == nk - 1)),
                        reads=[wk, (akey, tb // 4)], writes=[('ps', pb)])
                epi(pb, tb)

        evrot = {'i': 0}

        def evac(pb, m, dst_tile, dkey, func=None, ncols=512):
            if func is not None:
                P.op('act', lambda e: e.activation(out=dst_tile[0:m, 0:ncols], in_=ps[pb][0:m, 0:ncols], func=func),
                     reads=[('ps', pb)], writes=[dkey])
                return
            evrot['i'] += 1
            if evrot['i'] % 2 == 0:
                P.op('act', lambda e: e.copy(out=dst_tile[0:m, 0:ncols], in_=ps[pb][0:m, 0:ncols]),
                     reads=[('ps', pb)], writes=[dkey])
            else:
                P.op('dve', lambda e: e.tensor_copy(out=dst_tile[0:m, 0:ncols], in_=ps[pb][0:m, 0:ncols]),
                     reads=[('ps', pb)], writes=[dkey])

        def phase_inproj(l, actT):
            with Phase("inproj") as ph:
                norm_fm(ph, xT, 16, attn_norm[l], actT, 'actT', D)
                wts = ph.rot("w", [128, 16, 512], BF16, 3)
                stb = ph.rot("stb", [128, 512], BF16, 4)
                stf = ph.rot("stf", [128, 512], F32, 4)
                W = w_in[l]
                jobs = []

                def fm_store(dstT, r0, dt, func=None):
                    def epi_factory(ncols):
                        def epi(pb, m, ct, tt):
                            t, k = (stb if dt == BF16 else stf).next()
                            evac(pb, m, t, k, func)
                            P.dma('sp', lambda e: e.dma_start(
                                out=dstT[r0 + ct * 128:r0 + ct * 128 + m, tt * 512:(tt + 1) * 512], in_=t[0:m, :]),
                                reads=[k], key=k)
                        return epi
                    return epi_factory

                def add_fm(c0, ncols, dstT, r0, dt, func=None):
                    for j in range(0, ncols, 512):
                        n = min(512, ncols - j)
                        jobs.append(('fm', c0 + j, n, fm_store(dstT, r0 + j, dt, func)(n)))
                add_fm(C_FQ, 1024, fqT, 0, BF16)
                add_fm(C_FK, 1024, fkT, 0, BF16)
                for j in range(2):
                    def epi_tm(pb, tb, j=j):
                        t, k = stb.next()
                        evac(pb, 128, t, k)
                        P.dma('sp', lambda e: e.dma_start(
                            out=fv[tb * 128:(tb + 1) * 128, j * 512:(j + 1) * 512], in_=t[:]), reads=[k], key=k)
                    jobs.append(('tm', C_FV + j * 512, 512, epi_tm))
                jobs.append(('small', None, 96, fm_store(smallT, 0, F32)(96)))
                add_fm(C_KPE, 64, kpeT, 0, F32)
                add_fm(C_CQ, 512, cqT, 0, F32)
                add_fm(C_CKV, 512, ckvT, 0, F32)
                add_fm(C_GQKV, 3072, gqkvT, 0, F32)
                add_fm(C_GZ, 1024, gzsT, 0, BF16, AF.Silu)
                add_fm(C_GATE, 6144, gateT, 0, BF16, AF.Sigmoid)

                slots = {}

                def issue_load(ji):
                    kind, c0, n, epi = jobs[ji]
                    wt, wk = wts.next()
                    slots[ji] = (wt, wk)
                    if kind == 'small':
                        load_w(wt, wk, W, 16, C_FF, 8, 0)
                        load_w(wt, wk, W, 16, C_GB, 8, 32)
                        load_w(wt, wk, W, 16, C_GA, 8, 64)
                    else:
                        load_w(wt, wk, W, 16, c0, n)
                issue_load(0)
                issue_load(1)
                for ji in range(len(jobs)):
                    kind, c0, n, epi = jobs[ji]
                    wt, wk = slots[ji]
                    if kind == 'tm':
                        gemm_tm(wt, wk, 16, n, actT, 'actT', epi)
                    else:
                        gemm_fm(wt, wk, 16, n, actT, 'actT', epi)
                    if ji + 2 < len(jobs):
                        issue_load(ji + 2)

        def attention(ph, pools, h, branch, kT, kk, qT, qk, ek, ekk, eq, eqk, KX, v, vk, bias_t, bk, scale, sbanks, obanks):
            pT_pool, rl_pool, ob_pool = pools
            srot = {'i': 0}
            pending = [None]
            for tt in range(4):
                nsb = 4 * (tt + 1)
                pO, pL = obanks
                for sb in range(nsb):
                    pb = sbanks[srot['i'] % len(sbanks)]
                    srot['i'] += 1
                    diag = sb >= 4 * tt
                    P.op('pe', lambda e, sb=sb, tt=tt, pb=pb: e.matmul(
                        ps[pb][:], lhsT=kT[:, sb * 128:(sb + 1) * 128], rhs=qT[:, tt * 512:(tt + 1) * 512],
                        start=True, stop=False), reads=[kk, qk], writes=[('ps', pb)])
                    if ek is None:
                        lhs_fn = lambda sb: onesb[0:KX, 0:128]
                    else:
                        lhs_fn = lambda sb: ek[0:KX, sb * 128:(sb + 1) * 128]
                    P.op('pe', lambda e, sb=sb, tt=tt, pb=pb, diag=diag, lhs_fn=lhs_fn: e.matmul(
                        ps[pb][:], lhsT=lhs_fn(sb), rhs=eq[0:KX, tt * 512:(tt + 1) * 512],
                        start=False, stop=(not diag)), reads=[ekk, eqk], writes=[('ps', pb)])
                    if diag:
                        P.op('pe', lambda e, sb=sb, tt=tt, pb=pb: e.matmul(
                            ps[pb][:], lhsT=identb[:], rhs=maskb[:, sb - 4 * tt, :], start=False, stop=True),
                            reads=['maskb'], writes=[('ps', pb)])
                    pt, ptk = pT_pool.next()
                    if bias_t is not None:
                        P.op('act', lambda e, pb=pb, pt=pt, sb=sb: e.activation(
                            out=pt[:], in_=ps[pb][:], func=AF.Exp, bias=bias_t[:, sb:sb + 1], scale=scale),
                            reads=[('ps', pb), bk], writes=[ptk])
                    else:
                        P.op('act', lambda e, pb=pb, pt=pt: e.activation(
                            out=pt[:], in_=ps[pb][:], func=AF.Exp, scale=scale),
                            reads=[('ps', pb)], writes=[ptk])
                    def pv(sb=sb, pt=pt, ptk=ptk, pO=pO, pL=pL, nsb=nsb):
                        P.op('pe', lambda e: e.matmul(
                            ps[pO][:], lhsT=v[:, sb, :], rhs=pt[:], start=(sb == 0), stop=(sb == nsb - 1)),
                            reads=[vk, ptk], writes=[('ps', pO)])
                        P.op('pe', lambda e: e.matmul(
                            ps[pL][:], lhsT=onesb[:], rhs=pt[:], start=(sb == 0), stop=(sb == nsb - 1)),
                            reads=[ptk], writes=[('ps', pL)])
                    if pending[0] is not None:
                        pending[0]()
                    pending[0] = pv
                    yield
                if pending[0] is not None:
                    pending[0]()
                    pending[0] = None
                rl, rlk = rl_pool.next()
                P.op('dve', lambda e, rl=rl, pL=pL: e.reciprocal(out=rl[:], in_=ps[pL][:]),
                     reads=[('ps', pL)], writes=[rlk])
                ob, obk = ob_pool.next()
                P.op('dve', lambda e, rl=rl, ob=ob, pO=pO: e.tensor_tensor(out=ob[:], in0=ps[pO][:], in1=rl[:], op=ALU.mult),
                     reads=[('ps', pO), rlk], writes=[obk])
                P.dma('pool', lambda e, ob=ob, tt=tt: e.dma_start(
                    out=oT[branch, h * 128:(h + 1) * 128, tt * 512:(tt + 1) * 512], in_=ob[:]), reads=[obk], key=obk)
                yield

        def attn_pools(ph, tag=""):
            return (ph.rot(tag + "pT", [128, 512], BF16, 3), ph.rot(tag + "rl", [128, 512], F32, 2), ph.rot(tag + "ob", [128, 512], BF16, 2))

        def phase_fox_prep(l):
            with Phase("fprep") as ph:
                ff = ph.sb("ff", [8, S], F32)
                nb = ph.sb("nb", [8, 1], F32)
                one8 = ph.sb("one8", [8, 1], F32)
                ones8 = ph.sb("ones8", [8, S], F32)
                e1 = ph.sb("e1", [8, S], F32)
                l1 = ph.sb("l1", [8, S], F32)
                cum = ph.sb("cum", [8, S], F32)
                cs = ph.sb("cs", [8, S], F32)
                r1 = ph.sb("r1", [8, S], F32)
                hf = ph.sb("hf", [8, S], F32)
                parts = [ph.sb(f"part{i}", [8, S], BF16) for i in range(3)]
                P.dma('sp', lambda e: e.dma_start(out=ff[:], in_=smallT[0:8, :]), writes=['ff'], key=('ld', 0))
                P.dma('sp', lambda e: e.dma_start(out=nb[:], in_=fbias[l]), writes=['nb'], key=('ld', 1))
                P.op('dve', lambda e: e.tensor_scalar_mul(out=nb[:], in0=nb[:], scalar1=-1.0), reads=['nb'], writes=['nb'])
                P.op('dve', lambda e: e.memset(one8[:], 1.0), writes=['one8'])
                P.op('dve', lambda e: e.memset(ones8[:], 1.0), writes=['ones8'])
                P.op('act', lambda e: e.activation(out=e1[:], in_=ff[:], func=AF.Exp, bias=nb[:, 0:1], scale=-1.0),
                     reads=['ff', 'nb'], writes=['e1'])
                P.op('act', lambda e: e.activation(out=l1[:], in_=e1[:], func=AF.Ln, bias=one8[:, 0:1]),
                     reads=['e1', 'one8'], writes=['l1'])
                P.op('dve', lambda e: e.tensor_tensor_scan(out=cum[:], data0=ones8[:], data1=l1[:], initial=0.0,
                                                           op0=ALU.mult, op1=ALU.add),
                     reads=['ones8', 'l1'], writes=['cum'])
                P.dma('sp', lambda e: e.dma_start(out=negc, in_=cum[:]), reads=['cum'], key=('st', 0))
                P.op('dve', lambda e: e.tensor_scalar_mul(out=cs[:], in0=cum[:], scalar1=-math.sqrt(128.0)),
                     reads=['cum'], writes=['cs'])
                cur = cs
                curk = 'cs'
                for i in range(3):
                    P.op('dve', lambda e, i=i, cur=cur: e.tensor_copy(out=parts[i][:], in_=cur[:]),
                         reads=[curk], writes=[('part', i)])
                    P.dma('sp', lambda e, i=i: e.dma_start(out=crow[i], in_=parts[i][:]), reads=[('part', i)], key=('st', 1 + i))
                    if i < 2:
                        P.op('dve', lambda e, i=i: e.tensor_copy(out=hf[:], in_=parts[i][:]),
                             reads=[('part', i)], writes=['hf'])
                        P.op('dve', lambda e, cur=cur: e.tensor_tensor(out=r1[:], in0=cur[:], in1=hf[:], op=ALU.subtract),
                             reads=[curk, 'hf'], writes=['r1'])
                        cur = r1
                        curk = 'r1'

        def fox_head_loads(fx, h):
            qT, qk = fx['qTp'].next()
            kT, kk = fx['kTp'].next()
            v, vk = fx['vp'].next()
            nct, nck = fx['ncp'].next()
            cr, crk = fx['crp'].next()
            P.dma('sp', lambda e: e.dma_start(out=qT[:], in_=fqT[h * 128:(h + 1) * 128, :]), writes=[qk], key=qk)
            P.dma('sp', lambda e: e.dma_start(out=kT[:], in_=fkT[h * 128:(h + 1) * 128, :]), writes=[kk], key=kk)
            P.dma('sp', lambda e: e.dma_start(
                out=v[:], in_=fv[:, h * 128:(h + 1) * 128].rearrange("(sb p) c -> p sb c", p=128)), writes=[vk], key=vk)
            P.dma('sp', lambda e: e.dma_start(
                out=nct[:], in_=negc[h, :].rearrange("(sb p) -> p sb", p=128)), writes=[nck], key=nck)
            P.dma('sp', lambda e: e.dma_start(out=cr[:], in_=crow[:, h, :]), writes=[crk], key=crk)
            return (qT, qk, kT, kk, v, vk, nct, nck, cr, crk)

        def fox_setup(ph):
            return dict(pools=attn_pools(ph, "f"),
                        qTp=ph.rot("fqT", [128, S], BF16, 2), kTp=ph.rot("fkT", [128, S], BF16, 2),
                        vp=ph.rot("fv", [128, 16, 128], BF16, 2), ncp=ph.rot("fnc", [128, 16], F32, 2),
                        crp=ph.rot("fcr", [3, S], BF16, 2))

        def phase_attn(l):
            with Phase("attn") as ph:
                fx = fox_setup(ph)
                cqn = ph.sb("cqn", [128, 4, S], BF16)
                ckvn = ph.sb("ckvn", [128, 4, S], BF16)
                norm_fm(ph, cqT, 4, qnorm[l], cqn, 'cqn', 512, tag="nq", nld=1, nsq=2)
                norm_fm(ph, ckvT, 4, kvnorm[l], ckvn, 'ckvn', 512, tag="nq", nld=1, nsq=2)
                wuq = ph.sb("wuq", [128, 4, 1536], BF16)
                wukv = ph.sb("wukv", [128, 4, 2048], BF16)
                load_w(wuq, 'wuq', w_uq[l], 4, 0, 1536)
                load_w(wukv, 'wukv', w_ukv[l], 4, 0, 2048)
                cos2 = ph.sb("cos2", [64, S], F32)
                sin2 = ph.sb("sin2", [64, S], F32)
                rotf = ph.sb("rotf", [64, 64], F32)
                rotm = ph.sb("rotm", [64, 64], BF16)
                P.dma('sp', lambda e: e.dma_start(out=cos2[:], in_=c_rope[0]), writes=['cos2'], key=('ld', 0))
                P.dma('sp', lambda e: e.dma_start(out=sin2[:], in_=c_rope[1]), writes=['sin2'], key=('ld', 1))
                P.dma('sp', lambda e: e.dma_start(out=rotf[:], in_=c_rot), writes=['rotf'], key=('ld', 2))
                P.op('dve', lambda e: e.tensor_copy(out=rotm[:], in_=rotf[:]), reads=['rotf'], writes=['rotm'])
                xs_p = ph.rot("xs", [64, 512], F32, 2)
                xb_p = ph.rot("xb", [64, 512], BF16, 2)
                t1_p = ph.rot("t1", [64, 512], F32, 2)
                t2_p = ph.rot("t2", [64, 512], F32, 2)

                def rope_tile(xs, xsk, dst, dkey, tt):
                    xb, xbk = xb_p.next()
                    P.op('act', lambda e: e.copy(out=xb[:], in_=xs[:]), reads=[xsk], writes=[xbk])
                    pb = next_ps()
                    P.op('pe', lambda e: e.matmul(ps[pb][0:64, :], lhsT=rotm[:], rhs=xb[:], start=True, stop=True),
                         reads=['rotm', xbk], writes=[('ps', pb)])
                    t1, t1k = t1_p.next()
                    t2, t2k = t2_p.next()
                    P.op('dve', lambda e: e.tensor_tensor(out=t1[:], in0=xs[:], in1=cos2[:, tt * 512:(tt + 1) * 512], op=ALU.mult),
                         reads=[xsk, 'cos2'], writes=[t1k])
                    P.op('dve', lambda e: e.tensor_tensor(out=t2[:], in0=ps[pb][0:64, :], in1=sin2[:, tt * 512:(tt + 1) * 512], op=ALU.mult),
                         reads=[('ps', pb), 'sin2'], writes=[t2k])
                    P.op('dve', lambda e: e.tensor_tensor(out=dst[0:64, tt * 512:(tt + 1) * 512], in0=t1[:], in1=t2[:], op=ALU.add),
                         reads=[t1k, t2k], writes=[dkey])
                kper = ph.sb("kper", [64, S], BF16)
                for tt in range(4):
                    xs, xsk = xs_p.next()
                    P.dma('sp', lambda e, xs=xs, tt=tt: e.dma_start(out=xs[:], in_=kpeT[:, tt * 512:(tt + 1) * 512]), writes=[xsk], key=xsk)
                    rope_tile(xs, xsk, kper, 'kper', tt)
                pools = attn_pools(ph, "m")
                qnp = ph.rot("qn", [128, S], BF16, 2)
                qpp = ph.rot("qp", [64, S], BF16, 2)
                knp = ph.rot("kn", [128, S], BF16, 2)
                vp = ph.rot("v", [128, 16, 128], BF16, 2)
                for h in range(8):
                    fl = fox_head_loads(fx, h)
                    qn, qnk = qnp.next()
                    qp, qpk = qpp.next()
                    kn, knk = knp.next()
                    v, vk = vp.next()
                    for tt in range(4):
                        pb = next_ps()
                        for kc in range(4):
                            P.op('pe', lambda e, kc=kc, tt=tt, pb=pb, h=h: e.matmul(
                                ps[pb][:], lhsT=wuq[:, kc, h * 192:h * 192 + 128], rhs=cqn[:, kc, tt * 512:(tt + 1) * 512],
                                start=(kc == 0), stop=(kc == 3)), reads=['wuq', ('cqn', tt)], writes=[('ps', pb)])
                        evac(pb, 128, qn[:, tt * 512:(tt + 1) * 512], qnk)
                        pb = next_ps()
                        for kc in range(4):
                            P.op('pe', lambda e, kc=kc, tt=tt, pb=pb, h=h: e.matmul(
                                ps[pb][0:64, :], lhsT=wuq[:, kc, h * 192 + 128:h * 192 + 192], rhs=cqn[:, kc, tt * 512:(tt + 1) * 512],
                                start=(kc == 0), stop=(kc == 3)), reads=['wuq', ('cqn', tt)], writes=[('ps', pb)])
                        xs, xsk = xs_p.next()
                        P.op('act', lambda e, pb=pb, xs=xs: e.copy(out=xs[:], in_=ps[pb][0:64, :]), reads=[('ps', pb)], writes=[xsk])
                        rope_tile(xs, xsk, qp, qpk, tt)
                        pb = next_ps()
                        for kc in range(4):
                            P.op('pe', lambda e, kc=kc, tt=tt, pb=pb, h=h: e.matmul(
                                ps[pb][:], lhsT=wukv[:, kc, h * 256:h * 256 + 128], rhs=ckvn[:, kc, tt * 512:(tt + 1) * 512],
                                start=(kc == 0), stop=(kc == 3)), reads=['wukv', ('ckvn', tt)], writes=[('ps', pb)])
                        evac(pb, 128, kn[:, tt * 512:(tt + 1) * 512], knk)
                    for tg in range(4):
                        pb = next_ps()
                        for j in range(4):
                            tb = tg * 4 + j
                            for kc in range(4):
                                P.op('pe', lambda e, kc=kc, tb=tb, j=j, pb=pb, h=h: e.matmul(
                                    ps[pb][:, j * 128:(j + 1) * 128], lhsT=ckvn[:, kc, tb * 128:(tb + 1) * 128],
                                    rhs=wukv[:, kc, h * 256 + 128:h * 256 + 256], start=(kc == 0), stop=(kc == 3)),
                                    reads=['wukv', ('ckvn', tg)], writes=[('ps', pb)])
                        evac(pb, 128, v[:, tg * 4:(tg + 1) * 4, :].rearrange("p a b -> p (a b)"), vk)
                    (fqT_, fqk, fkT_, fkk, fv_, fvk, fnct, fnck, fcr, fcrk) = fl
                    gens = [attention(ph, fx['pools'], h, 0, fkT_, fkk, fqT_, fqk, None, 'onesb', fcr, fcrk, 3, fv_, fvk, fnct, fnck,
                                      128.0 ** -0.5, [0, 1], (4, 5)),
                            attention(ph, pools, h, 1, kn, knk, qn, qnk, kper, 'kper', qp, qpk, 64, v, vk, None, None,
                                      192.0 ** -0.5, [2, 3], (6, 7))]
                    while gens:
                        for g_ in list(gens):
                            try:
                                next(g_)
                            except StopIteration:
                                gens.remove(g_)
        def phase_gdn_prep(l):
            with Phase("gprep") as ph:
                t8 = lambda n: ph.sb(n, [8, S], F32)
                gbt, gat, blk, beta, e1, sp, g, gc, ngc, eg, bg, dl, edl = [t8(n) for n in
                    ("gbt", "gat", "blk", "beta", "e1", "sp", "g", "gc", "ngc", "eg", "bg", "dl", "edl")]
                dtb = ph.sb("dtb", [8, 1], F32)
                alog = ph.sb("alog", [8, 1], F32)
                negA = ph.sb("negA", [8, 1], F32)
                one8 = ph.sb("one8", [8, 1], F32)
                cdc = ph.sb("cdc", [8, 16], F32)
                P.dma('sp', lambda e: e.dma_start(out=gbt[:], in_=smallT[32:40, :]), writes=['gbt'], key=('ld', 0))
                P.dma('sp', lambda e: e.dma_start(out=gat[:], in_=smallT[64:72, :]), writes=['gat'], key=('ld', 1))
                P.dma('sp', lambda e: e.dma_start(out=blk[:], in_=c_blk), writes=['blk'], key=('ld', 2))
                P.dma('sp', lambda e: e.dma_start(out=dtb[:], in_=gdtb[l]), writes=['dtb'], key=('ld', 3))
                P.dma('sp', lambda e: e.dma_start(out=alog[:], in_=galog[l]), writes=['alog'], key=('ld', 4))
                P.op('dve', lambda e: e.memset(one8[:], 1.0), writes=['one8'])
                P.op('act', lambda e: e.activation(out=beta[:], in_=gbt[:], func=AF.Sigmoid), reads=['gbt'], writes=['beta'])
                P.op('act', lambda e: e.activation(out=e1[:], in_=gat[:], func=AF.Exp, bias=dtb[:, 0:1]), reads=['gat', 'dtb'], writes=['e1'])
                P.op('act', lambda e: e.activation(out=sp[:], in_=e1[:], func=AF.Ln, bias=one8[:, 0:1]), reads=['e1', 'one8'], writes=['sp'])
                P.op('act', lambda e: e.activation(out=negA[:], in_=alog[:], func=AF.Exp), reads=['alog'], writes=['negA'])
                P.op('dve', lambda e: e.tensor_scalar_mul(out=negA[:], in0=negA[:], scalar1=-1.0), reads=['negA'], writes=['negA'])
                P.op('dve', lambda e: e.tensor_scalar_mul(out=g[:], in0=sp[:], scalar1=negA[:, 0:1]), reads=['sp', 'negA'], writes=['g'])
                P.op('dve', lambda e: e.tensor_tensor_scan(out=gc[:], data0=blk[:], data1=g[:], initial=0.0, op0=ALU.mult, op1=ALU.add),
                     reads=['blk', 'g'], writes=['gc'])
                P.op('dve', lambda e: e.tensor_scalar_mul(out=ngc[:], in0=gc[:], scalar1=-1.0), reads=['gc'], writes=['ngc'])
                P.op('act', lambda e: e.activation(out=eg[:], in_=gc[:], func=AF.Exp), reads=['gc'], writes=['eg'])
                P.op('dve', lambda e: e.tensor_tensor(out=bg[:], in0=beta[:], in1=eg[:], op=ALU.mult), reads=['beta', 'eg'], writes=['bg'])
                g3 = lambda t: t[:].rearrange("p (a b) -> p a b", b=128)
                P.op('dve', lambda e: e.tensor_tensor(out=g3(dl), in0=g3(gc)[:, :, 127:128].to_broadcast([8, 16, 128]), in1=g3(gc), op=ALU.subtract),
                     reads=['gc'], writes=['dl'])
                P.op('act', lambda e: e.activation(out=edl[:], in_=dl[:], func=AF.Exp), reads=['dl'], writes=['edl'])
                P.op('act', lambda e: e.activation(out=cdc[:], in_=g3(gc)[:, :, 127], func=AF.Exp), reads=['gc'], writes=['cdc'])
                P.op('dve', lambda e: e.tensor_scalar_mul(out=beta[:], in0=beta[:], scalar1=-1.0), reads=['beta', 'bg'], writes=['nbeta'])
                for i, (t, k) in enumerate([(beta, 'nbeta'), (gc, 'gc'), (ngc, 'ngc'), (bg, 'bg'), (edl, 'edl'), (eg, 'eg')]):
                    P.dma('sp', lambda e, i=i, t=t: e.dma_start(out=gsc[i], in_=t[:]), reads=[k], key=('st', i))
                P.dma('sp', lambda e: e.dma_start(out=gcd, in_=cdc[:]), reads=['cdc'], key=('st', 7))

        def phase_gdn(l):
            with Phase("gdn") as ph:
                rawp = ph.rot("raw", [128, S], F32, 2)
                cvp = ph.rot("cv", [128, S], F32, 2)
                slp = ph.rot("sl", [128, S], F32, 2)
                cwp = ph.rot("cw", [128, 4], F32, 2)
                cvrot = {'i': 0}
                Gb = ph.sb("Gb", [128, S], F32)
                GbU = ph.sb("GbU", [128, S], F32)
                egb = ph.sb("egb", [128, S], F32)
                qn = ph.sb("qn", [128, S], BF16)
                qd2 = [ph.sb(f"qd{i}", [128, S], BF16) for i in range(2)]
                kn = ph.sb("kn", [128, S], BF16)
                vb = ph.sb("vb", [128, S], BF16)
                kbg_t = ph.sb("kbg_t", [128, 16, 128], BF16)
                kdl_t2 = [ph.sb(f"kdl_t{i}", [128, 16, 128], BF16) for i in range(2)]
                vb_t = ph.sb("vb_t", [128, 16, 128], BF16)
                u_all2 = [ph.sb(f"u_all{i}", [128, 16, 128], F32) for i in range(2)]
                wT2 = [ph.sb(f"wT{i}", [128, S], BF16) for i in range(2)]
                intraT2 = [ph.sb(f"intraT{i}", [128, 16, 128], BF16) for i in range(2)]
                oTh = ph.sb("oTh", [128, S], F32)
                colt = {n: ph.sb("c_" + n, [128, 16], F32) for n in ("nbeta", "gc", "ngc", "bg", "edl")}
                cdb2 = [ph.sb(f"cdb{i}", [128, 16], F32) for i in range(2)]
                gon = ph.sb("gon", [128, 1], F32)
                MU = ph.sb("MU", [128, 128], F32)
                ML = ph.sb("ML", [128, 128], F32)
                sqp = ph.rot("sq", [128, 512], F32, 2)
                rnp = ph.rot("rn", [128, 512], F32, 2)
                zsp = ph.rot("zs", [128, 512], BF16, 2)
                o2p = ph.rot("o2", [128, 512], F32, 2)
                obp = ph.rot("ob", [128, 512], BF16, 2)
                S32 = ph.sb("S32", [128, 128], F32)
                Sb = ph.sb("Sb", [128, 128], BF16)
                vnp = ph.rot("vn", [128, 128], BF16, 2)
                NG = 8
                cht = {n: [ph.sb(f"ch_{n}{j}", [128, 128], F32 if n in ("dL", "dU") else BF16) for j in range(NG)] for n in
                       ("dL", "dU", "A0", "B0", "Tt", "T", "P1s", "P2s")}

                def qapb(b, q):
                    return ps[b][:].bitcast(BF16)[:, 0:128]
                lvl = ph.sb("lvl", [128, 7, 2, 128], F32)
                P.dma('sp', lambda e: e.dma_start(out=lvl[:], in_=c_lvl.rearrange("l t p f -> p l t f")), writes=['lvl'], key=('ld', 13))
                Ttb = [ph.sb(f"ch_Ttb{j}", [128, 128], BF16) for j in range(NG)]
                P.dma('sp', lambda e: e.dma_start(out=MU[:], in_=c_tri[0]), writes=['MU'], key=('ld', 0))
                P.dma('sp', lambda e: e.dma_start(out=ML[:], in_=c_tri[1]), writes=['ML'], key=('ld', 1))
                P.dma('sp', lambda e: e.dma_start(out=gon[:], in_=gonorm[l]), writes=['gon'], key=('ld', 2))
                qrot = {'i': 0}

                def next_q():
                    i = qrot['i'] % 6
                    qrot['i'] += 1
                    return i, 0

                def qap(b, q):
                    return ps[b][:, q * 128:(q + 1) * 128]

                cprot = {'i': 0}

                def cp(dst, src, reads, writes):
                    cprot['i'] += 1
                    if cprot['i'] % 2 == 0:
                        P.op('act', lambda e: e.copy(out=dst, in_=src), reads=reads, writes=writes)
                    else:
                        P.op('dve', lambda e: e.tensor_copy(out=dst, in_=src), reads=reads, writes=writes)

                def genA(h):
                    par = str(h % 2)
                    u_all = u_all2[h % 2]
                    wT = wT2[h % 2]
                    intraT = intraT2[h % 2]
                    qd = qd2[h % 2]
                    kdl_t = kdl_t2[h % 2]
                    cdb = cdb2[h % 2]
                    for i, n in enumerate(("nbeta", "gc", "ngc", "bg", "edl")):
                        P.dma('sp', lambda e, i=i, n=n, h=h: e.dma_start(
                            out=colt[n][:], in_=gsc[i, h, :].rearrange("(sb p) -> p sb", p=128)), writes=['c_' + n], key=('ld', 3 + i))
                    P.dma('sp', lambda e, h=h: e.dma_start(out=Gb[:], in_=gsc[1, h:h + 1, :].to_broadcast([128, S])), writes=['Gb'], key=('ld', 8))
                    P.dma('sp', lambda e, h=h: e.dma_start(out=egb[:], in_=gsc[5, h:h + 1, :].to_broadcast([128, S])), writes=['egb'], key=('ld', 9))
                    g3_ = lambda t: t[:].rearrange("p (a b) -> p a b", b=128)
                    P.op('pool', lambda e: e.tensor_tensor(out=g3_(GbU), in0=g3_(Gb), in1=MU[:].unsqueeze(1).to_broadcast([128, 16, 128]), op=ALU.add),
                         reads=['Gb', 'MU'], writes=['GbU'])
                    P.op('pool', lambda e: e.tensor_tensor(out=g3_(Gb), in0=g3_(Gb), in1=ML[:].unsqueeze(1).to_broadcast([128, 16, 128]), op=ALU.add),
                         reads=['Gb', 'ML', 'GbU'], writes=['Gb'])
                    P.dma('sp', lambda e, h=h: e.dma_start(out=cdb[:], in_=gcd[h:h + 1, :].to_broadcast([128, 16])), writes=['cdb' + par], key=('ld', 10, par))
                    for which, c0 in (("q", h * 128), ("k", 1024 + h * 128), ("v", 2048 + h * 128)):
                        raw, rawk = rawp.next()
                        cv, cvk = cvp.next()
                        sl, slk = slp.next()
                        cw, cwk = cwp.next()
                        cvrot['i'] += 1
                        ceng = 'dve'
                        P.dma('sp', lambda e, c0=c0, raw=raw: e.dma_start(out=raw[:], in_=gqkvT[c0:c0 + 128, :]), writes=[rawk], key=rawk)
                        P.dma('sp', lambda e, c0=c0, cw=cw: e.dma_start(out=cw[:], in_=gconv[l, c0:c0 + 128, :]), writes=[cwk], key=cwk)
                        P.op(ceng, lambda e, raw=raw, cv=cv, cw=cw: e.tensor_scalar_mul(out=cv[:], in0=raw[:], scalar1=cw[:, 3:4]), reads=[rawk, cwk], writes=[cvk])
                        for sh in (1, 2, 3):
                            P.op(ceng, lambda e, sh=sh, raw=raw, cv=cv, cw=cw: e.scalar_tensor_tensor(
                                out=cv[:, sh:], in0=raw[:, 0:S - sh], scalar=cw[:, 3 - sh:4 - sh], in1=cv[:, sh:],
                                op0=ALU.mult, op1=ALU.add), reads=[rawk, cwk, cvk], writes=[cvk])
                        P.op('act', lambda e, sl=sl, cv=cv: e.activation(out=sl[:], in_=cv[:], func=AF.Silu), reads=[cvk], writes=[slk])
                        if which == "v":
                            P.op('pool', lambda e, sl=sl: e.tensor_copy(out=vb[:], in_=sl[:]), reads=[slk], writes=['vb'])
                            continue
                        for tt in range(4):
                            tsl = slice(tt * 512, (tt + 1) * 512)
                            sq, sqk = sqp.next()
                            rn, rnk = rnp.next()
                            pb = 6 + tt % 2
                            P.op('act', lambda e, sq=sq, tsl=tsl, sl=sl: e.activation(out=sq[:], in_=sl[:, tsl], func=AF.Square), reads=[slk], writes=[sqk])
                            P.op('pe', lambda e, sq=sq, pb=pb: e.matmul(ps[pb][:], lhsT=onesf[:], rhs=sq[:], start=True, stop=True),
                                 reads=[sqk], writes=[('ps', pb)])
                            P.op('act', lambda e, rn=rn, pb=pb: e.activation(out=rn[:], in_=ps[pb][:], func=AF.Sqrt, bias=epst[:, 0:1]),
                                 reads=[('ps', pb)], writes=[rnk])
                            P.op('dve', lambda e, rn=rn: e.reciprocal(out=rn[:], in_=rn[:]), reads=[rnk], writes=[rnk])
                            if which == "k":
                                P.op('dve', lambda e, rn=rn, tsl=tsl, sl=sl: e.tensor_tensor(out=kn[:, tsl], in0=sl[:, tsl], in1=rn[:], op=ALU.mult),
                                     reads=[slk, rnk], writes=['kn'])
                            else:
                                P.op('dve', lambda e, rn=rn, tsl=tsl, sl=sl: e.scalar_tensor_tensor(
                                    out=sl[:, tsl], in0=sl[:, tsl], scalar=128.0 ** -0.5, in1=rn[:], op0=ALU.mult, op1=ALU.mult),
                                    reads=[slk, rnk], writes=[slk])
                                P.op('act', lambda e, tsl=tsl, sl=sl: e.copy(out=qn[:, tsl], in_=sl[:, tsl]), reads=[slk], writes=['qn'])
                                P.op('dve', lambda e, tsl=tsl, sl=sl: e.tensor_tensor(out=qd[:, tsl], in0=sl[:, tsl], in1=egb[:, tsl], op=ALU.mult),
                                     reads=[slk, 'egb'], writes=['qd' + par])
                    if GSTOP == 1:
                        return
                    yield
                    for tg in range(4):
                        for (src, sk, outs) in ((kn, 'kn', ((kbg_t, 'kbg_t', 'bg'), (kdl_t, 'kdl_t' + par, 'edl'))),
                                                (vb, 'vb', ((vb_t, 'vb_t', 'nbeta'),))):
                            pb = 6 + (qrot['i'] % 2)
                            qrot['i'] += 1
                            for j in range(4):
                                blk_ = tg * 4 + j
                                P.op('pe', lambda e, j=j, blk_=blk_, pb=pb, src=src: e.transpose(
                                    out=ps[pb][:].bitcast(BF16)[:, j * 128:(j + 1) * 128], in_=src[:, blk_ * 128:(blk_ + 1) * 128],
                                    identity=identb[:]), reads=[sk], writes=[('ps', pb)])
                            for (dst, dk_, cn) in outs:
                                sgn = -1.0 if cn == 'nbeta' else 1.0
                                P.op('dve', lambda e, dst=dst, cn=cn, tg=tg, pb=pb, sgn=sgn: e.scalar_tensor_tensor(
                                    out=dst[:, tg * 4:(tg + 1) * 4, :],
                                    in0=ps[pb][:].bitcast(BF16)[:, 0:512].rearrange("p (a b) -> p a b", a=4), scalar=sgn,
                                    in1=colt[cn][:, tg * 4:(tg + 1) * 4].unsqueeze(2).to_broadcast([128, 4, 128]),
                                    op0=ALU.mult, op1=ALU.mult), reads=[('ps', pb), 'c_' + cn], writes=[dk_])
                    if GSTOP == 2:
                        return
                    yield
                    for g0 in range(0, 16, NG):
                        blks = list(range(g0, g0 + NG))
                        for j, b_ in enumerate(blks):
                            bs = slice(b_ * 128, (b_ + 1) * 128)
                            qa = next_q()
                            qb = next_q()
                            P.op('pe', lambda e, bs=bs, qa=qa: e.matmul(qap(*qa), lhsT=kn[:, bs], rhs=kn[:, bs], start=True, stop=True),
                                 reads=['kn'], writes=[('ps', qa[0])])
                            P.op('pe', lambda e, bs=bs, qb=qb: e.matmul(qap(*qb), lhsT=kn[:, bs], rhs=qn[:, bs], start=True, stop=True),
                                 reads=['kn', 'qn'], writes=[('ps', qb[0])])
                            if GSUB < 2:
                                continue
                            if GSUB < 3:
                                continue
                            P.op('act', lambda e, j=j, b_=b_, bs=bs: e.activation(out=cht['dL'][j][:], in_=Gb[:, bs], func=AF.Exp,
                                                                           bias=colt['gc'][:, b_:b_ + 1], scale=-1.0),
                                 reads=['Gb', 'c_gc'], writes=[('dL', j)])
                            P.op('act', lambda e, j=j, b_=b_, bs=bs: e.activation(out=cht['dU'][j][:], in_=GbU[:, bs], func=AF.Exp,
                                                                           bias=colt['ngc'][:, b_:b_ + 1], scale=1.0),
                                 reads=['GbU', 'c_ngc'], writes=[('dU', j)])
                            if GSUB < 4:
                                continue
                            P.op('dve', lambda e, j=j, b_=b_, qa=qa: e.scalar_tensor_tensor(
                                out=cht['A0'][j][:], in0=qap(*qa), scalar=colt['nbeta'][:, b_:b_ + 1], in1=cht['dL'][j][:],
                                op0=ALU.mult, op1=ALU.mult), reads=[('ps', qa[0]), 'c_nbeta', ('dL', j)], writes=[('A0', j)])
                            P.op('dve', lambda e, j=j, b_=b_, qb=qb: e.tensor_tensor(
                                out=intraT[:, b_, :], in0=qap(*qb), in1=cht['dU'][j][:], op=ALU.mult),
                                reads=[('ps', qb[0]), ('dU', j)], writes=['intraT' + par])
                            if GSUB < 5:
                                continue
                            qc = next_q()
                            P.op('pe', lambda e, j=j, qc=qc: e.transpose(out=qapb(*qc), in_=cht['A0'][j][:], identity=identb[:]),
                                 reads=[('A0', j)], writes=[('ps', qc[0])])
                            cp(cht['B0'][j][:], qapb(*qc), [('ps', qc[0])], [('B0', j)])
                            yield
                            P.op('pool', lambda e, j=j: e.tensor_tensor(out=cht['T'][j][:], in0=cht['A0'][j][:], in1=lvl[:, 0, 0, :], op=ALU.mult),
                                 reads=[('A0', j), 'lvl'], writes=[('T', j)])
                            P.op('pool', lambda e, j=j: e.tensor_tensor(out=cht['T'][j][:], in0=cht['T'][j][:], in1=identf[:], op=ALU.add),
                                 reads=[('T', j)], writes=[('T', j)])
                            P.op('pool', lambda e, j=j: e.tensor_tensor(out=cht['Tt'][j][:], in0=cht['B0'][j][:], in1=lvl[:, 0, 1, :], op=ALU.mult),
                                 reads=[('B0', j), 'lvl'], writes=[('Tt', j)])
                            P.op('pool', lambda e, j=j: e.tensor_tensor(out=cht['Tt'][j][:], in0=cht['Tt'][j][:], in1=identf[:], op=ALU.add),
                                 reads=[('Tt', j)], writes=[('Tt', j)])
                        for lv in range(1, 0 if GSTOP == 31 else 7):
                            yield
                            for j in range(NG):
                                qa = next_q()
                                P.op('pe', lambda e, j=j, qa=qa: e.matmul(qap(*qa), lhsT=cht['B0'][j][:], rhs=cht['T'][j][:], start=True, stop=True),
                                     reads=[('B0', j), ('T', j)], writes=[('ps', qa[0])])
                                P.op('act', lambda e, j=j, qa=qa: e.copy(out=cht['P1s'][j][:], in_=qap(*qa)), reads=[('ps', qa[0])], writes=[('P1s', j)])
                            yield
                            for j in range(NG):
                                qb = next_q()
                                P.op('pe', lambda e, j=j, qb=qb: e.matmul(qap(*qb), lhsT=cht['Tt'][j][:], rhs=cht['P1s'][j][:], start=True, stop=True),
                                     reads=[('Tt', j), ('P1s', j)], writes=[('ps', qb[0])])
                                P.op('dve', lambda e, j=j, qb=qb, lv=lv: e.tensor_tensor(out=cht['P2s'][j][:], in0=qap(*qb), in1=lvl[:, lv, 0, :], op=ALU.mult),
                                     reads=[('ps', qb[0]), 'lvl'], writes=[('P2s', j)])
                            yield
                            for j in range(NG):
                                P.op('pool', lambda e, j=j: e.tensor_tensor(out=cht['T'][j][:], in0=cht['T'][j][:], in1=cht['P2s'][j][:], op=ALU.add),
                                     reads=[('T', j), ('P2s', j)], writes=[('T', j)])
                                qc = next_q()
                                P.op('pe', lambda e, j=j, qc=qc: e.transpose(out=qapb(*qc), in_=cht['P2s'][j][:], identity=identb[:]),
                                     reads=[('P2s', j)], writes=[('ps', qc[0])])
                                P.op('dve', lambda e, j=j, qc=qc: e.tensor_tensor(out=cht['Tt'][j][:], in0=qapb(*qc), in1=cht['Tt'][j][:], op=ALU.add),
                                     reads=[('ps', qc[0]), ('Tt', j)], writes=[('Tt', j)])
                        for j, b_ in enumerate(blks if GSTOP not in (31, 32) else []):
                            bs = slice(b_ * 128, (b_ + 1) * 128)
                            qa = next_q()
                            qb = next_q()
                            P.op('pe', lambda e, j=j, b_=b_, qa=qa: e.matmul(qap(*qa), lhsT=cht['Tt'][j][:], rhs=vb_t[:, b_, :], start=True, stop=True),
                                 reads=[('Tt', j), 'vb_t'], writes=[('ps', qa[0])])
                            P.op('pe', lambda e, j=j, b_=b_, qb=qb: e.matmul(qap(*qb), lhsT=kbg_t[:, b_, :], rhs=cht['Tt'][j][:], start=True, stop=True),
                                 reads=[('Tt', j), 'kbg_t'], writes=[('ps', qb[0])])
                            cp(u_all[:, b_, :], qap(*qa), [('ps', qa[0])], ['u_all' + par])
                            cp(wT[:, bs], qap(*qb), [('ps', qb[0])], ['wT' + par])
                    yield


                def genB(h):
                    par = str(h % 2)
                    u_all = u_all2[h % 2]
                    wT = wT2[h % 2]
                    intraT = intraT2[h % 2]
                    qd = qd2[h % 2]
                    kdl_t = kdl_t2[h % 2]
                    cdb = cdb2[h % 2]
                    if GSTOP in (3, 31, 32):
                        return
                    for b_ in range(16):
                        bs = slice(b_ * 128, (b_ + 1) * 128)
                        vn, vnk = vnp.next()
                        if b_ == 0:
                            P.op('dve', lambda e, vn=vn: e.tensor_copy(out=vn[:], in_=u_all[:, 0, :]), reads=['u_all' + par], writes=[vnk])
                        else:
                            qa = next_q()
                            P.op('pe', lambda e, bs=bs, qa=qa: e.matmul(qap(*qa), lhsT=wT[:, bs], rhs=Sb[:], start=True, stop=True),
                                 reads=['wT' + par, 'Sb'], writes=[('ps', qa[0])])
                            P.op('dve', lambda e, vn=vn, b_=b_, qa=qa: e.tensor_tensor(out=vn[:], in0=u_all[:, b_, :], in1=qap(*qa), op=ALU.subtract),
                                 reads=['u_all' + par, ('ps', qa[0])], writes=[vnk])
                        qo = next_q()
                        if b_ > 0:
                            P.op('pe', lambda e, bs=bs, qo=qo: e.matmul(qap(*qo), lhsT=Sb[:], rhs=qd[:, bs], start=True, stop=False),
                                 reads=['Sb', 'qd' + par], writes=[('ps', qo[0])])
                        P.op('pe', lambda e, vn=vn, b_=b_, qo=qo: e.matmul(qap(*qo), lhsT=vn[:], rhs=intraT[:, b_, :], start=(b_ == 0), stop=True),
                             reads=[vnk, 'intraT' + par], writes=[('ps', qo[0])])
                        P.op('act', lambda e, bs=bs, qo=qo: e.copy(out=oTh[:, bs], in_=qap(*qo)), reads=[('ps', qo[0])], writes=['oTh'])
                        if b_ < 15:
                            qs_ = next_q()
                            P.op('pe', lambda e, vn=vn, b_=b_, qs_=qs_: e.matmul(qap(*qs_), lhsT=kdl_t[:, b_, :], rhs=vn[:], start=True, stop=True),
                                 reads=[vnk, 'kdl_t' + par], writes=[('ps', qs_[0])])
                            if b_ == 0:
                                P.op('dve', lambda e, qs_=qs_: e.tensor_copy(out=S32[:], in_=qap(*qs_)), reads=[('ps', qs_[0])], writes=['S32'])
                            else:
                                P.op('dve', lambda e, qs_=qs_, b_=b_: e.scalar_tensor_tensor(
                                    out=S32[:], in0=S32[:], scalar=cdb[:, b_:b_ + 1], in1=qap(*qs_), op0=ALU.mult, op1=ALU.add),
                                    reads=['S32', 'cdb' + par, ('ps', qs_[0])], writes=['S32'])
                            P.op('act', lambda e: e.copy(out=Sb[:], in_=S32[:]), reads=['S32'], writes=['Sb'])
                        yield
                    if GSTOP == 4:
                        return
                    for tt in range(4):
                        tsl = slice(tt * 512, (tt + 1) * 512)
                        sq, sqk = sqp.next()
                        rn, rnk = rnp.next()
                        pb = 6 + tt % 2
                        P.op('act', lambda e, sq=sq, tsl=tsl: e.activation(out=sq[:], in_=oTh[:, tsl], func=AF.Square), reads=['oTh'], writes=[sqk])
                        P.op('pe', lambda e, sq=sq, pb=pb: e.matmul(ps[pb][:], lhsT=onesf[:], rhs=sq[:], start=True, stop=True),
                             reads=[sqk], writes=[('ps', pb)])
                        P.op('act', lambda e, rn=rn, pb=pb: e.activation(out=rn[:], in_=ps[pb][:], func=AF.Sqrt, bias=epst[:, 0:1], scale=1.0 / 128),
                             reads=[('ps', pb)], writes=[rnk])
                        P.op('dve', lambda e, rn=rn: e.reciprocal(out=rn[:], in_=rn[:]), reads=[rnk], writes=[rnk])
                        zs, zsk = zsp.next()
                        P.dma('sp', lambda e, zs=zs, h=h, tsl=tsl: e.dma_start(out=zs[:], in_=gzsT[h * 128:(h + 1) * 128, tsl]), writes=[zsk], key=zsk)
                        o2, o2k = o2p.next()
                        P.op('dve', lambda e, o2=o2, rn=rn, tsl=tsl: e.scalar_tensor_tensor(
                            out=o2[:], in0=oTh[:, tsl], scalar=gon[:, 0:1], in1=rn[:], op0=ALU.mult, op1=ALU.mult),
                            reads=['oTh', 'gon', rnk], writes=[o2k])
                        ob, obk = obp.next()
                        P.op('pool', lambda e, o2=o2, zs=zs, ob=ob: e.tensor_tensor(out=ob[:], in0=o2[:], in1=zs[:], op=ALU.mult),
                             reads=[o2k, zsk], writes=[obk])
                        P.dma('pool', lambda e, ob=ob, h=h, tsl=tsl: e.dma_start(out=oT[2, h * 128:(h + 1) * 128, tsl], in_=ob[:]), reads=[obk], key=obk)
                        yield
                    yield

                def drain(g_):
                    for _ in g_:
                        pass
                drain(genA(0))
                for h in range(GHEADS):
                    gb_ = genB(h)
                    ga_ = genA(h + 1) if h + 1 < GHEADS else None
                    while gb_ is not None or ga_ is not None:
                        if ga_ is not None:
                            for _ in range(2):
                                try:
                                    next(ga_)
                                except StopIteration:
                                    ga_ = None
                                    break
                        if gb_ is not None:
                            try:
                                next(gb_)
                            except StopIteration:
                                gb_ = None


        def phase_merge(l, actT):
            with Phase("merge") as ph:
                obh = [ph.sb(f"obh{b}", [128, 8, S], BF16) for b in range(3)]
                wts = ph.rot("wm", [128, 8, 256], BF16, 6)
                gtp = ph.rot("gt", [128, 512], BF16, 4)
                mp = ph.rot("mm", [128, 512], F32, 4)
                sp_ = ph.rot("ms", [128, 512], F32, 2)
                for half in range(1):
                    for b in range(3):
                        for kh in range(4):
                            P.dma('sp', lambda e, b=b, kh=kh: e.dma_start(
                                out=obh[b][:, kh * 2:(kh + 1) * 2, :],
                                in_=oT[b, kh * 256:(kh + 1) * 256, :].rearrange("(kc p) t -> p kc t", p=128)),
                                writes=[('obh', b)], key=('ld', b))
                    for dg in range(8):
                        wl = []
                        for b in range(3):
                            wt, wk = wts.next()
                            load_w(wt, wk, w_branch[l, b], 8, dg * 256, 256)
                            wl.append((wt, wk))
                        for ct in range(2):
                            dchunk = dg * 2 + ct
                            for t2 in range(4):
                                tt = t2
                                ms = []
                                for b in range(3):
                                    pb = next_ps(0, 6)
                                    wt, wk = wl[b]
                                    for kc in range(8):
                                        P.op('pe', lambda e, kc=kc, ct=ct, t2=t2, pb=pb, wt=wt, b=b: e.matmul(
                                            ps[pb][:], lhsT=wt[:, kc, ct * 128:(ct + 1) * 128], rhs=obh[b][:, kc, t2 * 512:(t2 + 1) * 512],
                                            start=(kc == 0), stop=(kc == 7)), reads=[wk, ('obh', b)], writes=[('ps', pb)])
                                    gt, gk = gtp.next()
                                    r0 = b * 2048 + dchunk * 128
                                    P.dma('sp', lambda e, gt=gt, r0=r0, tt=tt: e.dma_start(
                                        out=gt[:], in_=gateT[r0:r0 + 128, tt * 512:(tt + 1) * 512]), writes=[gk], key=gk)
                                    m_, mk = mp.next()
                                    P.op('dve', lambda e, m_=m_, gt=gt, pb=pb: e.tensor_tensor(out=m_[:], in0=ps[pb][:], in1=gt[:], op=ALU.mult),
                                         reads=[('ps', pb), gk], writes=[mk])
                                    ms.append((m_, mk))
                                s_, sk = sp_.next()
                                P.op('dve', lambda e, s_=s_, ms=ms: e.tensor_tensor(out=s_[:], in0=ms[0][0][:], in1=ms[1][0][:], op=ALU.add),
                                     reads=[ms[0][1], ms[1][1]], writes=[sk])
                                P.op('dve', lambda e, s_=s_, ms=ms, dchunk=dchunk, tt=tt: e.tensor_tensor(
                                    out=actT[:, dchunk, tt * 512:(tt + 1) * 512], in0=s_[:], in1=ms[2][0][:], op=ALU.add),
                                    reads=[sk, ms[2][1]], writes=[('actT', tt)])

        def resid_epi(ph, r0):
            xl = ph.rot("xl", [128, 512], F32, 4)
            xs = ph.rot("xs", [128, 512], F32, 4)

            def epi(pb, m, ct, tt):
                rr = r0 + ct * 128
                t, k = xl.next()
                P.dma('sp', lambda e: e.dma_start(out=t[:], in_=xT[rr:rr + 128, tt * 512:(tt + 1) * 512]), writes=[k], key=k)
                o, ok = xs.next()
                P.op('dve', lambda e: e.tensor_tensor(out=o[:], in0=ps[pb][:], in1=t[:], op=ALU.add), reads=[('ps', pb), k], writes=[ok])
                P.dma('sp', lambda e: e.dma_start(out=xT[rr:rr + 128, tt * 512:(tt + 1) * 512], in_=o[:]), reads=[ok], key=ok)
            return epi

        def phase_wout(l, actT):
            with Phase("wout") as ph:
                wts = ph.rot("w", [128, 16, 512], BF16, 3)
                slots = {}

                def issue(j):
                    wt, wk = wts.next()
                    slots[j] = (wt, wk)
                    load_w(wt, wk, w_out[l], 16, j * 512, 512)
                epis = {}
                xl = ph.rot("xl", [128, 512], F32, 4)
                xs = ph.rot("xs", [128, 512], F32, 4)

                def mk_epi(r0):
                    def epi(pb, m, ct, tt):
                        rr = r0 + ct * 128
                        t, k = xl.next()
                        P.dma('sp', lambda e: e.dma_start(out=t[:], in_=xT[rr:rr + 128, tt * 512:(tt + 1) * 512]), writes=[k], key=k)
                        o, ok = xs.next()
                        P.op('dve', lambda e: e.tensor_tensor(out=o[:], in0=ps[pb][:], in1=t[:], op=ALU.add), reads=[('ps', pb), k], writes=[ok])
                        P.dma('act', lambda e: e.dma_start(out=xT[rr:rr + 128, tt * 512:(tt + 1) * 512], in_=o[:]), reads=[ok], key=ok)
                    return epi
                issue(0)
                issue(1)
                for j in range(4):
                    wt, wk = slots[j]
                    gemm_fm(wt, wk, 16, 512, actT, 'actT', mk_epi(j * 512))
                    if j + 2 < 4:
                        issue(j + 2)

        def phase_ffn_up(l, actT):
            with Phase("ffnup") as ph:
                norm_fm(ph, xT, 16, mlp_norm[l], actT, 'actT', D)
                wts = ph.rot("w", [128, 16, 512], BF16, 3)
                rp = ph.rot("r", [128, 512], F32, 3)
                hp = ph.rot("hb", [128, 512], BF16, 3)
                slots = {}

                def issue(j):
                    wt, wk = wts.next()
                    slots[j] = (wt, wk)
                    load_w(wt, wk, w_up[l], 16, j * 512, 512)

                def mk_epi(r0):
                    def epi(pb, m, ct, tt):
                        r, rk = rp.next()
                        P.op('act', lambda e: e.activation(out=r[:], in_=ps[pb][:], func=AF.Relu), reads=[('ps', pb)], writes=[rk])
                        hb, hk = hp.next()
                        P.op('pool', lambda e: e.tensor_tensor(out=hb[:], in0=r[:], in1=r[:], op=ALU.mult), reads=[rk], writes=[hk])
                        kc_ = (r0 + ct * 128) // 128
                        P.dma('sp', lambda e: e.dma_start(out=hidT[tt, :, kc_, :], in_=hb[:]), reads=[hk], key=hk)
                    return epi
                issue(0)
                issue(1)
                for j in range(16):
                    wt, wk = slots[j]
                    gemm_fm(wt, wk, 16, 512, actT, 'actT', mk_epi(j * 512))
                    if j + 2 < 16:
                        issue(j + 2)

        def phase_ffn_down(l):
            with Phase("ffndn") as ph:
                wd = [ph.sb(f"wd{i}", [128, 64, 512], BF16) for i in range(2)]
                hp = ph.rot("hid", [128, 8, 512], BF16, 3)
                xl = ph.rot("xl", [128, 512], F32, 4)
                xs = ph.rot("xs", [128, 512], F32, 4)

                def issue(eg):
                    load_w(wd[eg % 2], ('wd', eg % 2), w_down[l], 64, eg * 512, 512)
                issue(0)
                it = 0
                for eg in range(4):
                    if eg + 1 < 4:
                        issue(eg + 1)
                    wt, wk = wd[eg % 2], ('wd', eg % 2)
                    for tt in range(4):
                        base = 4 * (it % 2)
                        it += 1
                        for kcg in range(8):
                            ht, hk = hp.next()
                            P.dma('sp', lambda e, ht=ht, kcg=kcg, tt=tt: e.dma_start(
                                out=ht[:], in_=hidT[tt, :, kcg * 8:(kcg + 1) * 8, :]),
                                writes=[hk], key=hk)
                            for j in range(8):
                                kc = kcg * 8 + j
                                for ct in range(4):
                                    P.op('pe', lambda e, kc=kc, j=j, ct=ct, ht=ht, wt=wt, base=base: e.matmul(
                                        ps[base + ct][:], lhsT=wt[:, kc, ct * 128:(ct + 1) * 128], rhs=ht[:, j, :],
                                        start=(kc == 0), stop=(kc == 63)), reads=[wk, hk], writes=[('ps', base + ct)])
                        for ct in range(4):
                            pb = base + ct
                            rr = eg * 512 + ct * 128
                            t, k = xl.next()
                            P.dma('sp', lambda e, t=t, rr=rr, tt=tt: e.dma_start(out=t[:], in_=xT[rr:rr + 128, tt * 512:(tt + 1) * 512]), writes=[k], key=k)
                            o, ok = xs.next()
                            P.op('dve', lambda e, o=o, t=t, pb=pb: e.tensor_tensor(out=o[:], in0=ps[pb][:], in1=t[:], op=ALU.add),
                                 reads=[('ps', pb), k], writes=[ok])
                            P.dma('act', lambda e, o=o, rr=rr, tt=tt: e.dma_start(out=xT[rr:rr + 128, tt * 512:(tt + 1) * 512], in_=o[:]), reads=[ok], key=ok)

        def phase_final():
            with Phase("final") as ph:
                norm_fm(ph, xT, 16, final_norm, None, None, D, tag="nf", final_out=out_d)

        def with_act(fn):
            with ExitStack() as aes:
                actT = aes.enter_context(nc.sbuf_tensor(f"actT_{state['phase']}", [128, 16, S], BF16))
                fn(actT)

        if only == 'gdn':
            phase_gdn(0)
            nl = 0
            stop = 'x'
        else:
            phase_transpose_in()
        for l in range(nl):
            with_act(lambda actT: phase_inproj(l, actT))
            if stop == 'inproj':
                break
            phase_fox_prep(l)
            phase_attn(l)
            if stop == 'mla':
                break
            phase_gdn_prep(l)
            if stop == 'gprep':
                break
            phase_gdn(l)
            if stop == 'gdn':
                break

            def mo(actT):
                phase_merge(l, actT)
                phase_wout(l, actT)
            with_act(mo)
            if stop == 'wout':
                break
            with_act(lambda actT: phase_ffn_up(l, actT))
            phase_ffn_down(l)
        if stop is None:
            phase_final()
        print("total ops", P.nops, "sems", P.nsem, "cnt", P.cnt)
    return nc


def host_inputs(inputs):
    f = lambda a: np.ascontiguousarray(np.asarray(a, dtype=np.float32))
    m = {}
    m["attn_norm"] = f(np.asarray(inputs["attn_norm"]).reshape(NL, 16, 128).transpose(0, 2, 1))
    m["w_in"] = f(inputs["w_in"])
    m["fox_fgate_bias"] = f(np.asarray(inputs["fox_fgate_bias"]).reshape(NL, 8, 1))
    m["mla_q_norm"] = f(np.asarray(inputs["mla_q_norm"]).reshape(NL, 4, 128).transpose(0, 2, 1))
    m["mla_kv_norm"] = f(np.asarray(inputs["mla_kv_norm"]).reshape(NL, 4, 128).transpose(0, 2, 1))
    m["w_mla_uq"] = f(np.asarray(inputs["w_mla_uq"]).reshape(NL, 512, 1536))
    m["w_mla_ukv"] = f(np.asarray(inputs["w_mla_ukv"]).reshape(NL, 512, 2048))
    m["gdn_conv"] = f(np.asarray(inputs["gdn_conv"]).transpose(0, 2, 1))
    m["gdn_a_log"] = f(np.asarray(inputs["gdn_a_log"]).reshape(NL, 8, 1))
    m["gdn_dt_bias"] = f(np.asarray(inputs["gdn_dt_bias"]).reshape(NL, 8, 1))
    m["gdn_out_norm"] = f(np.asarray(inputs["gdn_out_norm"]).reshape(NL, 128, 1))
    m["w_branch"] = f(inputs["w_branch"])
    m["w_out"] = f(inputs["w_out"])
    m["mlp_norm"] = f(np.asarray(inputs["mlp_norm"]).reshape(NL, 16, 128).transpose(0, 2, 1))
    m["w_up"] = f(inputs["w_up"])
    m["w_down"] = f(inputs["w_down"])
    m["final_norm"] = f(np.asarray(inputs["final_norm"]).reshape(16, 128).T)
    m["c_ident"] = np.eye(128, dtype=np.float32)
    s_idx = np.arange(128)[None, :, None] + 128 * np.arange(4)[:, None, None]
    t_idx = np.arange(512)[None, None, :]
    m["c_mask"] = np.where(s_idx > t_idx, -30000.0, 0.0).astype(np.float32)
    inv_freq = (np.float32(10000.0) ** (-np.arange(0, 64, 2, dtype=np.float32) / np.float32(64))).astype(np.float32)
    ang = (np.arange(S, dtype=np.float32)[None, :] * inv_freq[:, None]).astype(np.float32)
    cos, sin = np.cos(ang).astype(np.float32), np.sin(ang).astype(np.float32)
    m["c_rope"] = np.stack([np.concatenate([cos, cos], 0), np.concatenate([sin, sin], 0)]).astype(np.float32)
    rot = np.zeros((64, 64), np.float32)
    for i in range(32):
        rot[i + 32, i] = -1.0
        rot[i, i + 32] = 1.0
    m["c_rot"] = rot
    blk = np.ones((8, S), np.float32)
    blk[:, ::128] = 0.0
    m["c_blk"] = blk
    a = np.arange(128)[:, None]
    b = np.arange(128)[None, :]
    lv = []
    for sz in (1, 2, 4, 8, 16, 32, 64):
        ms = ((a // (2 * sz) == b // (2 * sz)) & (a % (2 * sz) >= sz) & (b % (2 * sz) < sz)).astype(np.float32)
        lv.append(np.stack([ms, ms.T]))
    m["c_lvl"] = np.ascontiguousarray(np.stack(lv))
    m["c_tri"] = np.stack([np.where(b >= a, 0.0, -30000.0), np.where(b < a, 0.0, 30000.0)]).astype(np.float32)
    return m


_CACHE = {}


def kernel(**inputs):
    shared = host_inputs(inputs)
    x = np.asarray(inputs["x"], dtype=np.float32)
    if 'nc' not in _CACHE:
        _CACHE['nc'] = build()
    nc = _CACHE['nc']
    in_maps = []
    for b in range(8):
        mm = dict(shared)
        mm["x"] = np.ascontiguousarray(x[b])
        in_maps.append(mm)
    res = run_bass_kernel_spmd(nc, in_maps, core_ids=list(range(8)))
    return np.stack([np.asarray(r["out"], dtype=np.float32) for r in res.results], axis=0)
```
